# Optimizing a Trainium2 kernel written in Bass

```python
import math
import jax
import jax.numpy as jnp
from jax import lax
import numpy as np

D_MODEL = 1024
BATCH = 4
SEQ = 4096
DEPTH = 1
DEC_BATCH = 32
DEC_SEQ = 4
PAST_LEN = 8192
PAGE_SIZE = 128

D_MIX = D_MODEL
N_HEADS_ATT = 8
HEAD_DIM = 64
D_ATT = N_HEADS_ATT * HEAD_DIM
N_GROUPS_SGU = 8
SGU_GROUP_DIM = 64
D_SGU = N_GROUPS_SGU * SGU_GROUP_DIM
CHUNK = 128
BRANCHES = ((128, 1), (512, 4), (2048, 16))
WINDOW = max(w for w, _ in BRANCHES)
Q_BLOCK = 128
N_BUCKETS = 32
MAX_DISTANCE = WINDOW
D_FF = -(-8 * D_MODEL // (3 * 256)) * 256
D_IN = 3 * D_ATT + 2 * D_SGU
EPS = 1e-6
NEG_INF = -1e30

kernel_name = "hymba_dilated_sgu_decoder_step"


def _rmsnorm(x, g):
    xf = x.astype(jnp.float32)
    y = xf * lax.rsqrt(jnp.mean(xf * xf, axis=-1, keepdims=True) + EPS)
    return (y * g.astype(jnp.float32)).astype(x.dtype)


def _layernorm(x, g, b):
    xf = x.astype(jnp.float32)
    mu = jnp.mean(xf, axis=-1, keepdims=True)
    var = jnp.mean(jnp.square(xf - mu), axis=-1, keepdims=True)
    y = (xf - mu) * lax.rsqrt(var + EPS)
    return (y * g.astype(jnp.float32) + b.astype(jnp.float32)).astype(x.dtype)


def _rel_bucket(dist):
    max_exact = N_BUCKETS // 2
    df = jnp.maximum(dist, 1).astype(jnp.float32)
    large = max_exact + (jnp.log(df / max_exact) / math.log(MAX_DISTANCE / max_exact)
                         * (N_BUCKETS - max_exact)).astype(jnp.int32)
    large = jnp.minimum(large, N_BUCKETS - 1)
    return jnp.where(dist < max_exact, dist, large)


def _branch_biases(rel_bias):
    out = []
    for w, d in BRANCHES:
        nj = w // d + 1
        dist = jnp.arange(nj, dtype=jnp.int32) * d
        out.append(rel_bias[_rel_bucket(dist)].T.astype(jnp.float32))
    return out


def _heads(t):
    return t.reshape(t.shape[:-1] + (N_HEADS_ATT, HEAD_DIM))


def _project(x, g, w_in):
    z = _rmsnorm(x, g) @ w_in
    return jnp.split(z, [D_ATT, 2 * D_ATT, 3 * D_ATT, 3 * D_ATT + D_SGU], axis=-1)


def _dilated_prompt(q, k, v, bias_hj, d, nj):
    B, S, H, C = q.shape
    L = S // d
    nb = -(-L // Q_BLOCK)
    Lp = nb * Q_BLOCK

    def split(t):
        t = t.reshape(B, L, d, H, C)
        t = jnp.pad(t, ((0, 0), (0, Lp - L), (0, 0), (0, 0), (0, 0)))
        return t.reshape(B, nb, Q_BLOCK, d, H, C)

    def with_prev(t):
        prev = jnp.pad(t, ((0, 0), (1, 0), (0, 0), (0, 0), (0, 0), (0, 0)))[:, :-1]
        return jnp.concatenate([prev, t], axis=2)

    qb = split(q)
    kc = with_prev(split(k))
    vc = with_prev(split(v))
    qi = jnp.arange(Q_BLOCK)[:, None]
    ki = jnp.arange(2 * Q_BLOCK)[None, :]
    j = Q_BLOCK + qi - ki
    valid = (j >= 0) & (j < nj)
    valid = valid[None] & ((jnp.arange(nb)[:, None, None] > 0) | (ki[None] >= Q_BLOCK))
    bias = bias_hj[:, jnp.clip(j, 0, nj - 1)]
    s = jnp.einsum('bnqrhc,bnkrhc->bnrhqk', qb, kc).astype(jnp.float32)
    s = s * (HEAD_DIM ** -0.5) + bias[None, None, None]
    s = jnp.where(valid[None, :, None, None], s, NEG_INF)
    lse = jax.nn.logsumexp(s, axis=-1)
    p = jnp.exp(s - lse[..., None])
    o = jnp.einsum('bnrhqk,bnkrhc->bnqrhc', p.astype(vc.dtype), vc)
    o = o.reshape(B, Lp, d, H, C)[:, :L].reshape(B, S, H, C)
    lse = lse.transpose(0, 1, 4, 2, 3).reshape(B, Lp, d, H)[:, :L].reshape(B, S, H)
    return o, lse


def _dilated_sample(q, k_all, v_all, bias_hj, d, nj, wb):
    T = q.shape[1]
    idx = wb + jnp.arange(T)[:, None] - jnp.arange(nj)[None, :] * d
    valid = idx >= 0
    idxc = jnp.maximum(idx, 0)
    kg = k_all[:, idxc]
    vg = v_all[:, idxc]
    s = jnp.einsum('bthc,btjhc->bthj', q, kg).astype(jnp.float32)
    s = s * (HEAD_DIM ** -0.5) + bias_hj[None, None]
    s = jnp.where(valid[None, :, None, :], s, NEG_INF)
    lse = jax.nn.logsumexp(s, axis=-1)
    p = jnp.exp(s - lse[..., None])
    o = jnp.einsum('bthj,btjhc->bthc', p.astype(vg.dtype), vg)
    return o, lse


def _merge_branches(outs, lses):
    w = jax.nn.softmax(jnp.stack(lses, axis=0), axis=0)
    return jnp.sum(w[..., None] * jnp.stack(outs, axis=0).astype(jnp.float32), axis=0)


def _spatial_gate(vc, w_s, b_s):
    T = vc.shape[-3]
    causal = jnp.tril(jnp.ones((T, T), dtype=w_s.dtype))
    wm = w_s[:, :T, :T] * causal
    return jnp.einsum('gts,...sgc->...tgc', wm, vc) + b_s[:, :T].T[:, :, None]


def _swiglu(x, g, w_gate, w_up, w_down):
    h = _rmsnorm(x, g)
    return (jax.nn.silu(h @ w_gate) * (h @ w_up)) @ w_down


def _layer_prompt(x, biases, norm1_g, w_in, sgu_ln_g, sgu_ln_b, sgu_w, sgu_b, w_out,
                  norm2_g, w_gate, w_up, w_down):
    B, S, _ = x.shape
    q, k, v, u, vg = _project(x, norm1_g, w_in)
    q, k, v = _heads(q), _heads(k), _heads(v)
    outs, lses = [], []
    for (w, d), bias_hj in zip(BRANCHES, biases):
        o, l = _dilated_prompt(q, k, v, bias_hj, d, w // d + 1)
        outs.append(o)
        lses.append(l)
    att = _merge_branches(outs, lses).reshape(B, S, D_ATT).astype(x.dtype)
    vn = _layernorm(vg, sgu_ln_g, sgu_ln_b)
    vn = vn.reshape(B, S // CHUNK, CHUNK, N_GROUPS_SGU, SGU_GROUP_DIM)
    gate = _spatial_gate(vn, sgu_w, sgu_b).reshape(B, S, D_SGU)
    x = x + jnp.concatenate([att, u * gate], axis=-1) @ w_out
    x = x + _swiglu(x, norm2_g, w_gate, w_up, w_down)
    nw = min(WINDOW, S)
    return x, k[:, S - nw:], v[:, S - nw:]


def _layer_sample(x, cache_k, cache_v, biases, norm1_g, w_in, sgu_ln_g, sgu_ln_b, sgu_w,
                  sgu_b, w_out, norm2_g, w_gate, w_up, w_down):
    Bd, T, _ = x.shape
    wb = cache_k.shape[1]
    q, k, v, u, vg = _project(x, norm1_g, w_in)
    q, k, v = _heads(q), _heads(k), _heads(v)
    k_all = jnp.concatenate([cache_k.astype(k.dtype), k], axis=1)
    v_all = jnp.concatenate([cache_v.astype(v.dtype), v], axis=1)
    outs, lses = [], []
    for (w, d), bias_hj in zip(BRANCHES, biases):
        o, l = _dilated_sample(q, k_all, v_all, bias_hj, d, w // d + 1, wb)
        outs.append(o)
        lses.append(l)
    att = _merge_branches(outs, lses).reshape(Bd, T, D_ATT).astype(x.dtype)
    vn = _layernorm(vg, sgu_ln_g, sgu_ln_b)
    gate = _spatial_gate(vn.reshape(Bd, T, N_GROUPS_SGU, SGU_GROUP_DIM), sgu_w, sgu_b)
    gate = gate.reshape(Bd, T, D_SGU)
    x = x + jnp.concatenate([att, u * gate], axis=-1) @ w_out
    x = x + _swiglu(x, norm2_g, w_gate, w_up, w_down)
    keep = wb + T - min(WINDOW, wb + T)
    return x, k_all[:, keep:], v_all[:, keep:], vn


def setup_inputs(seed: int = 0) -> dict:
    key = jax.random.key(seed)
    ks = jax.random.split(key, 20)
    wb = min(WINDOW, PAST_LEN)
    f32 = jnp.float32
    nrm = lambda k, shape: jax.random.normal(k, shape, dtype=f32)
    return {
        "x_prompt": nrm(ks[0], (BATCH, SEQ, D_MODEL)),
        "x_sample": nrm(ks[1], (DEC_BATCH, DEC_SEQ, D_MODEL)),
        "cache_k_win": nrm(ks[2], (DEPTH, DEC_BATCH, wb, N_HEADS_ATT, HEAD_DIM)),
        "cache_v_win": nrm(ks[3], (DEPTH, DEC_BATCH, wb, N_HEADS_ATT, HEAD_DIM)),
        "norm1_g": 1.0 + 0.02 * nrm(ks[4], (DEPTH, D_MODEL)),
        "w_in": nrm(ks[5], (DEPTH, D_MODEL, D_IN)) * D_MODEL ** -0.5,
        "sgu_ln_g": 1.0 + 0.02 * nrm(ks[6], (DEPTH, D_SGU)),
        "sgu_ln_b": 0.02 * nrm(ks[7], (DEPTH, D_SGU)),
        "sgu_w": nrm(ks[8], (DEPTH, N_GROUPS_SGU, CHUNK, CHUNK)) * CHUNK ** -0.5,
        "sgu_b": 1.0 + 0.02 * nrm(ks[9], (DEPTH, N_GROUPS_SGU, CHUNK)),
        "w_out": nrm(ks[10], (DEPTH, D_MIX, D_MODEL)) * D_MIX ** -0.5,
        "norm2_g": 1.0 + 0.02 * nrm(ks[11], (DEPTH, D_MODEL)),
        "w_gate": nrm(ks[12], (DEPTH, D_MODEL, D_FF)) * D_MODEL ** -0.5,
        "w_up": nrm(ks[13], (DEPTH, D_MODEL, D_FF)) * D_MODEL ** -0.5,
        "w_down": nrm(ks[14], (DEPTH, D_FF, D_MODEL)) * D_FF ** -0.5,
        "rel_bias": 0.5 * nrm(ks[15], (N_BUCKETS, N_HEADS_ATT)),
        "final_g": 1.0 + 0.02 * nrm(ks[16], (D_MODEL,)),
    }


def reference(x_prompt, x_sample, cache_k_win, cache_v_win, norm1_g, w_in, sgu_ln_g, sgu_ln_b,
              sgu_w, sgu_b, w_out, norm2_g, w_gate, w_up, w_down, rel_bias, final_g):
    biases = _branch_biases(rel_bias)
    xp, xs = x_prompt, x_sample
    kp, vp, ksm, vsm, svs = [], [], [], [], []
    for l in range(DEPTH):
        lw = (norm1_g[l], w_in[l], sgu_ln_g[l], sgu_ln_b[l], sgu_w[l], sgu_b[l], w_out[l],
              norm2_g[l], w_gate[l], w_up[l], w_down[l])
        xp, k_new, v_new = _layer_prompt(xp, biases, *lw)
        kp.append(k_new)
        vp.append(v_new)
        xs, k_buf, v_buf, sv = _layer_sample(xs, cache_k_win[l], cache_v_win[l], biases, *lw)
        ksm.append(k_buf)
        vsm.append(v_buf)
        svs.append(sv)
    y_prompt = _rmsnorm(xp, final_g)
    y_sample = _rmsnorm(xs, final_g)
    return (y_prompt, y_sample, jnp.stack(kp), jnp.stack(vp), jnp.stack(ksm), jnp.stack(vsm), jnp.stack(svs))
```

```python
import numpy as np
from contextlib import ExitStack
import concourse.bass as bass
import concourse.mybir as mybir
from concourse.bass_utils import run_bass_kernel_spmd

F32 = mybir.dt.float32
BF = mybir.dt.bfloat16
AF = mybir.ActivationFunctionType
ALU = mybir.AluOpType

NCORES = 8
D = 1024
NT = 2048
NS = 16
NTS = NT + NS
DFF = 2816
NFC = 22
EPS = 1e-6
NEG = -30000.0
BRANCH_D = (1, 4, 16)
import os
STAGE = int(os.environ.get('KSTAGE', '99'))


class Prog:
    def __init__(self, nc, es):
        self.nc = nc
        self.es = es
        self.eng = {"pe": nc.tensor, "act": nc.scalar, "dve": nc.vector, "pool": nc.gpsimd, "sp": nc.sync}
        self.sem = {k: es.enter_context(nc.semaphore("s_" + k)) for k in self.eng}
        self.cnt = {k: 0 for k in self.eng}
        self.waited = {k: {} for k in self.eng}
        self.lastw = {}
        self.readers = {}
        self.dsem = {}
        self.dcnt = {}

    def _deps(self, e, reads, writes):
        toks = []
        for r in reads:
            t = self.lastw.get(r)
            if t is not None:
                toks.append(t)
            if isinstance(r, tuple) and r[0] in ("ps", "pt"):
                toks.extend(tk for tk in self.readers.get(r, ()) if tk[3] != e)
        for w in writes:
            t = self.lastw.get(w)
            if t is not None:
                toks.append(t)
            toks.extend(self.readers.get(w, ()))
        for (key, handle, val, prod) in toks:
            if prod == "pe" and e == "pe":
                continue
            if prod is None:
                val = max(val, 16 * self.dcnt[key[2:]])
            if self.waited[e].get(key, 0) >= val:
                continue
            self.eng[e].wait_ge(handle, val)
            self.waited[e][key] = val

    def _commit(self, tok, reads, writes):
        for w in writes:
            self.lastw[w] = tok
            self.readers[w] = []
        for r in reads:
            if r in writes:
                continue
            self.readers.setdefault(r, []).append(tok)

    def op(self, e, fn, reads=(), writes=()):
        self._deps(e, reads, writes)
        ins = fn(self.eng[e])
        self.cnt[e] += 1
        ins.then_inc(self.sem[e], 1)
        self._commit((e, self.sem[e], self.cnt[e], e), reads, writes)

    def dma(self, q, out, in_, reads=(), writes=(), stream=None, **kw):
        if stream not in self.dsem:
            self.dsem[stream] = self.es.enter_context(self.nc.semaphore("d_" + str(stream)))
            self.dcnt[stream] = 0
        self._deps(q, reads, writes)
        ins = self.eng[q].dma_start(out=out, in_=in_, **kw)
        self.dcnt[stream] += 1
        ins.then_inc(self.dsem[stream], 16)
        self._commit(("d_" + str(stream), self.dsem[stream], 16 * self.dcnt[stream], None), reads, writes)

    def dma_group(self, q, pairs, reads=(), writes=(), stream=None, **kw):
        if stream not in self.dsem:
            self.dsem[stream] = self.es.enter_context(self.nc.semaphore("d_" + str(stream)))
            self.dcnt[stream] = 0
        self._deps(q, reads, writes)
        for (out, in_) in pairs:
            ins = self.eng[q].dma_start(out=out, in_=in_, **kw)
            self.dcnt[stream] += 1
            ins.then_inc(self.dsem[stream], 16)
        self._commit(("d_" + str(stream), self.dsem[stream], 16 * self.dcnt[stream], None), reads, writes)

    def barrier(self):
        for e in self.eng:
            for o in self.eng:
                if o != e and self.cnt[o] > self.waited[e].get(o, 0):
                    self.eng[e].wait_ge(self.sem[o], self.cnt[o])
                    self.waited[e][o] = self.cnt[o]
            for s, h in self.dsem.items():
                if s in ("cpk", "cpv"):
                    continue
                v = 16 * self.dcnt[s]
                if v > self.waited[e].get("d_" + str(s), 0):
                    self.eng[e].wait_ge(h, v)
                    self.waited[e]["d_" + str(s)] = v
        self.lastw.clear()
        self.readers.clear()

    def finish(self):
        for s, h in self.dsem.items():
            v = 16 * self.dcnt[s]
            if v > self.waited["sp"].get("d_" + str(s), 0):
                self.nc.sync.wait_ge(h, v)


def _rel_bucket_np(dist):
    dist = np.asarray(dist)
    df = np.maximum(dist, 1).astype(np.float32)
    large = 16 + (np.log(df / np.float32(16)) / np.float32(np.log(2048 / 16)) * np.float32(16)).astype(np.int32)
    large = np.minimum(large, 31)
    return np.where(dist < 16, dist, large)


def _onehot_consts():
    oh = np.zeros((3, 33, 384), np.float32)
    for bi, d in enumerate(BRANCH_D):
        for i in range(384):
            j = 255 - i
            if 0 <= j <= 128:
                oh[bi, int(_rel_bucket_np(j * d)), i] = 1.0
            else:
                oh[bi, 32, i] = 1.0
    return oh


def _nmult(delta):
    return int(delta <= 128) + int(delta % 4 == 0 and delta <= 512) + int(delta % 16 == 0 and delta <= 2048)


def _sample_row(u, p):
    return 16 * (32 * u + p // 4) + p % 4 if u < 3 else 1536 + 128 * (u - 3) + p


def _sample_consts():
    ohs = np.zeros((34, 28, 128), np.float32)
    for u in range(7):
        for t in range(4):
            for p in range(128):
                delta = 2048 + t - _sample_row(u, p)
                n = _nmult(delta)
                if n == 0:
                    ohs[32, u * 4 + t, p] = 1.0
                else:
                    ohs[int(_rel_bucket_np(delta)), u * 4 + t, p] = 1.0
                    ohs[33, u * 4 + t, p] = np.float32(np.log(n))
    ohn = np.zeros((34, 4, 4), np.float32)
    for t in range(4):
        for tp in range(4):
            delta = t - tp
            if delta < 0:
                ohn[32, t, tp] = 1.0
            else:
                ohn[int(_rel_bucket_np(delta)), t, tp] = 1.0
                ohn[33, t, tp] = np.float32(np.log(_nmult(delta)))
    return ohs, ohn


def build_program():
    nc = bass.Bass("TRN2", target_bir_lowering=False)
    dt = lambda n, s, kind="ExternalInput": nc.dram_tensor(n, s, F32, kind=kind).ap()
    xo = dt("xo", [NT, D]); xh = dt("xh", [NT, D]); xsm = dt("xsm", [NS, D])
    hm_d = dt("hm", [128, 1])
    w_in = dt("w_in", [D, 2560]); w_out = dt("w_out", [D, D])
    w_gate = dt("w_gate", [D, DFF]); w_up = dt("w_up", [D, DFF]); w_down = dt("w_down", [DFF, D])
    g1t_d = dt("g1t", [128, 8]); g2t_d = dt("g2t", [128, 8]); gf_d = dt("gfb", [128, D])
    lng_d = dt("lng", [128, 512]); lnb_d = dt("lnb", [128, 512])
    sgw_d = dt("sgwT", [8, 128, 128]); tri_d = dt("tri", [128, 128]); bsp_d = dt("bsp", [4, 128, 128])
    bdm_d = dt("bdm", [NS, NS])
    rb34_d = dt("rb34", [34, 8]); ohs_d = dt("ohs", [34, 28, 128]); ohn_d = dt("ohn", [34, 4, 4])
    rb_d = dt("rb33", [33, 8]); oh_d = dt("oh", [3, 33, 384]); id_d = dt("ident", [128, 128])
    ck_d = dt("ck", [4, 2048, 512]); cv_d = dt("cv", [4, 2048, 512])
    y_d = dt("y", [NT, D], "ExternalOutput"); ys_d = dt("ys", [NS, D], "ExternalOutput")
    ko_d = dt("ko", [NT, 512], "ExternalOutput"); vo_d = dt("vo", [NT, 512], "ExternalOutput")
    kso_d = dt("kso", [4, 2048, 512], "ExternalOutput"); vso_d = dt("vso", [4, 2048, 512], "ExternalOutput")
    svo_d = dt("svo", [NS, 512], "ExternalOutput")
    x2s = dt("x2s", [NTS, D], "Internal")
    esc = dt("esc", [3, 8, 384], "Internal")

    with ExitStack() as es:
        P = Prog(nc, es)

        uid = [0]

        def mk_sb(st):
            def f(n, s, d=F32):
                uid[0] += 1
                return st.enter_context(nc.sbuf_tensor("sb%d_%s" % (uid[0], n), s, d))
            return f
        sb = mk_sb(es)
        psb, ptb, psA = [], [], [None]
        esp = [ExitStack()]
        pcount = [0]

        def set_psum(mode):
            esp[0].close()
            esp[0] = ExitStack()
            pcount[0] += 1
            psb[:] = []
            ptb[:] = []
            if mode == "std":
                psb.extend(esp[0].enter_context(nc.psum_tensor("ps%d_%d" % (pcount[0], i), [128, 512], F32)) for i in range(6))
                ptb.extend(esp[0].enter_context(nc.psum_tensor("pt%d_%d" % (pcount[0], i), [128, 1024], BF)) for i in range(2))
            else:
                psA[0] = esp[0].enter_context(nc.psum_tensor("psA%d" % pcount[0], [128, 8, 512], F32))
        set_psum("std")

        ident_f = sb("ident_f", [128, 128]); ident = sb("ident", [128, 128], BF)
        g1t = sb("g1t", [128, 8]); g2t = sb("g2t", [128, 8]); hm = sb("hm", [128, 1])
        epst = sb("epst", [128, 1]); ss = sb("ss", [128, 8]); ones = sb("ones", [128, 64], BF)
        P.dma("sp", ident_f[:], id_d, writes=["ident_f"], stream="c0")
        P.dma("sp", g1t[:], g1t_d, writes=["g1t"], stream="c0")
        P.dma("sp", g2t[:], g2t_d, writes=["g2t"], stream="c0")
        P.dma("sp", hm[:], hm_d, writes=["hm"], stream="c0")
        P.op("dve", lambda e: e.tensor_copy(out=ident[:], in_=ident_f[:]), reads=["ident_f"], writes=["ident"])
        P.op("dve", lambda e: e.memset(epst[:], EPS), writes=["eps"])
        P.op("dve", lambda e: e.memset(ones[:], 1.0), writes=["ones"])

        cp_pending = []
        for b in range(4):
            for q in range(4):
                cp_pending.append((kso_d[b, 511 * q:511 * (q + 1), :], ck_d[b, 4 + 511 * q:4 + 511 * (q + 1), :], "cpk"))
                cp_pending.append((vso_d[b, 511 * q:511 * (q + 1), :], cv_d[b, 4 + 511 * q:4 + 511 * (q + 1), :], "cpv"))

        def issue_copies(n, q="act"):
            for _ in range(min(n, len(cp_pending))):
                o, i_, st = cp_pending.pop(0)
                P.dma(q, o, i_, stream=st)

        BS = sb("BS", [128, 7, 4, 8]); BN = sb("BN", [4, 4, 8])
        rb34 = sb("rb34", [34, 8]); ohn = sb("ohn", [34, 4, 4])
        with ExitStack() as es0:
            sb0 = mk_sb(es0)
            rb33 = sb0("rb33", [33, 8]); ohs = sb0("ohs", [33, 3, 384]); e_sb = sb0("e_sb", [8, 3, 384])
            P.dma("sp", rb33[:], rb_d, writes=["rb33"], stream="c0")
            P.dma("sp", ohs[:], oh_d.rearrange("d k i -> k d i"), writes=["ohs"], stream="c0")
            for bi in range(3):
                P.op("pe", lambda e, bi=bi: e.matmul(psb[bi][:8, 0:384], lhsT=rb33[:, :], rhs=ohs[:, bi, :], start=True, stop=True),
                     reads=["rb33", "ohs"], writes=[("ps", bi)])
                P.op("dve", lambda e, bi=bi: e.tensor_copy(out=e_sb[:, bi, :], in_=psb[bi][:8, 0:384]), reads=[("ps", bi)], writes=["e_sb"])
            P.dma("sp", esc.rearrange("d h i -> h d i"), e_sb[:], reads=["e_sb"], writes=["esc"], stream="c0")
            P.barrier()

        hT = sb("hT", [128, 8, NTS], BF)
        QT = sb("QT", [128, 4, NTS], BF)
        KTs = sb("KTs", [128, 4, NS], BF)
        wpre = sb("wpre", [128, 8, 512], BF)

        EB0 = sb("EB0", [128, 3, 2, 2, 128], BF)

        def eb_setup(j, EB, ebname, hq, hqname, tb, tbname):
            for bi in range(3):
                hv = hq[:, :].rearrange("p (h t q) -> p h t q", h=2, t=2)
                tv = tb[:, :].rearrange("p (a h q) -> p a h q", a=2, h=2)
                pairs = []
                for hh in range(2):
                    src = bass.AP(esc.tensor, (bi * 8 + 2 * j + hh) * 384, [[1, 128], [128, 2], [1, 128]])
                    pairs.append((hv[:, hh, :, :], src))
                P.dma_group("sp", pairs, reads=["esc"], writes=[hqname], stream="hq")
                for dp in range(2):
                    for hh in range(2):
                        x = hv[:, hh, 1 - dp, 127:128]
                        rv = bass.AP(x.tensor, x.offset, [x.ap[0], [-1, 128]])
                        P.op("pool", lambda e, dp=dp, hh=hh, rv=rv, tv=tv: e.tensor_copy(out=tv[:, dp, hh, :], in_=rv),
                             reads=[hqname], writes=[tbname])
                P.op("act", lambda e, bi=bi, tb=tb: e.activation(out=EB[:, bi, :, :, :].rearrange("p a h q -> p (a h q)"), in_=tb[:, :], func=AF.Exp),
                     reads=[tbname], writes=[ebname])

        wcount = [0]
        tcount = [0]
        kvc = [0]
        TG_OWN = [(0, 512), (512, 512), (1024, 512), (1536, 512), (2048, NS)]
        TG_HIST = [(0, 512), (512, 512), (1024, 512), (1536, 512)]

        class WS:
            pass
        W = WS()

        def alloc_work(sbw):
            W.xs = [sbw("xs%d" % i, [128, D]) for i in range(2)]
            W.xn = [sbw("xn%d" % i, [128, D], BF) for i in range(2)]
            W.wsl = [sbw("wsl%d" % i, [128, 8, 512], BF) for i in range(2)]

        def load_wslab(parts):
            s = wcount[0] % 2
            wcount[0] += 1
            pairs = []
            for (wd, c0, ncols, o0) in parts:
                pairs += [(W.wsl[s][:, kc, o0:o0 + ncols], wd[kc * 128:(kc + 1) * 128, c0:c0 + ncols]) for kc in range(8)]
            P.dma_group("pool", pairs, writes=[("wsl", s)], stream="w%d" % s)
            return s

        def norm_T(src_d, rows, gt, gname, dstT, dstname, col0):
            s = tcount[0] % 2
            tcount[0] += 1
            X = W.xs[s]
            P.dma("sp", X[:rows, :], src_d, writes=[("xs", s)], stream="x%d" % s)
            norm_T_sb(X, ("xs", s), rows, gt, gname, dstT, dstname, col0, s)
            return s

        def rstd_of(X, xname, rows, s):
            P.op("act", lambda e: e.activation(out=W.xn[s][:rows, :], in_=X[:rows, :], func=AF.Square,
                                               accum_out=ss[:rows, s:s + 1]),
                 reads=[xname], writes=[("ss", s), ("xn", s)])
            P.op("act", lambda e: e.activation(out=ss[:rows, 2 + s:3 + s], in_=ss[:rows, s:s + 1], func=AF.Ln,
                                               scale=1.0 / D, bias=epst[:rows, :]),
                 reads=[("ss", s), "eps"], writes=[("sd", s)])
            P.op("act", lambda e: e.activation(out=ss[:rows, 4 + s:5 + s], in_=ss[:rows, 2 + s:3 + s], func=AF.Exp, scale=-0.5),
                 reads=[("sd", s)], writes=[("rstd", s)])

        def norm_B(X, xname, rows, s):
            rstd_of(X, xname, rows, s)
            P.op("dve", lambda e: e.tensor_scalar(out=W.xn[s][:rows, :], in0=X[:rows, :], scalar1=ss[:rows, 4 + s:5 + s],
                                                  scalar2=None, op0=ALU.mult),
                 reads=[xname, ("rstd", s)], writes=[("xn", s)])

        def norm_C(rows, gt, gname, dstT, dstname, col0, s, gran=512):
            pt = ptb[s]

            def tr(e):
                ins = None
                for kc in range(8):
                    ins = e.transpose(out=pt[:, kc * 128:kc * 128 + rows], in_=W.xn[s][:rows, kc * 128:(kc + 1) * 128],
                                      identity=ident[:rows, :rows])
                return ins
            P.op("pe", tr, reads=[("xn", s), "ident"], writes=[("pt", s)])
            src3 = pt[:].rearrange("p (k t) -> p k t", k=8)[:, :, 0:rows]
            gb = bass.AP(gt[:].tensor, gt[:].offset, [gt[:].ap[0], [1, 8], [0, rows]])
            P.op("dve", lambda e: e.tensor_tensor(out=dstT[:, :, col0:col0 + rows], in0=src3, in1=gb, op=ALU.mult),
                 reads=[("pt", s), gname], writes=[(dstname, col0 // gran)])

        def norm_T_sb(X, xname, rows, gt, gname, dstT, dstname, col0, s, gran=512):
            norm_B(X, xname, rows, s)
            norm_C(rows, gt, gname, dstT, dstname, col0, s, gran)

        def norm_pipeline(tiles, after_cb):
            n = len(tiles)
            base = tcount[0]
            tcount[0] += n
            for i in range(n + 2):
                if i < n:
                    src_d, rows, col0 = tiles[i]
                    s = (base + i) % 2
                    P.dma("sp", W.xs[s][:rows, :], src_d, writes=[("xs", s)], stream="x%d" % s)
                if 0 <= i - 1 < n:
                    src_d, rows, col0 = tiles[i - 1]
                    s = (base + i - 1) % 2
                    norm_B(W.xs[s], ("xs", s), rows, s)
                if 0 <= i - 2 < n:
                    src_d, rows, col0 = tiles[i - 2]
                    s = (base + i - 2) % 2
                    norm_C(rows, g1t, "g1t", hT, "hT", col0, s, gran=128)
                    after_cb(i - 2)

        def fm_proj(slab, c_lo, nchunks, srcT, srcname, tgs, evac):
            k = 0
            for c in range(nchunks):
                for (t0, n) in tgs:
                    bank = k % 4
                    k += 1
                    pb = psb[bank]

                    def mm(e, c=c, t0=t0, n=n, pb=pb):
                        ins = None
                        for kc in range(8):
                            ins = e.matmul(pb[:, 0:n], lhsT=W.wsl[slab][:, kc, (c_lo + c) * 128:(c_lo + c + 1) * 128],
                                           rhs=srcT[:, kc, t0:t0 + n], start=(kc == 0), stop=(kc == 7))
                        return ins
                    P.op("pe", mm, reads=[("wsl", slab), (srcname, t0 // 512)], writes=[("ps", bank)])
                    evac(c, t0, n, pb, ("ps", bank))

        def tm_proj(slab, srcname, col_ap_fn, rows, evac, src_reads=None):
            bank = 4 + (kvc[0] % 2)
            pb = psb[bank]

            def mm(e):
                ins = None
                for kc in range(8):
                    ins = e.matmul(pb[:rows, :], lhsT=col_ap_fn(kc), rhs=W.wsl[slab][:, kc, :], start=(kc == 0), stop=(kc == 7))
                return ins
            P.op("pe", mm, reads=[("wsl", slab)] + (src_reads if src_reads is not None else [(srcname, i) for i in range(5)]),
                 writes=[("ps", bank)])
            evac(pb, ("ps", bank))
            kvc[0] += 1

        vt_index = {}
        for bi, d in enumerate(BRANCH_D):
            for r in range(d):
                for b in range(-1, 16 // d):
                    vt_index[(bi, r, b)] = len(vt_index)
        NVT = len(vt_index)

        with ExitStack() as es1:
            sb1 = mk_sb(es1)
            KT = sb1("KT", [128, 4, 2 * NT], BF)
            V = sb1("V", [128, NVT, 512], BF)
            with ExitStack() as es2:
                sb2 = mk_sb(es2)
                alloc_work(sb2)
                kvst = [sb2("kvst%d" % i, [128, 512]) for i in range(2)]
                wv = sb2("wv", [128, 8, 512], BF)
                P.dma_group("pool", [(wv[:, kc, :], w_in[kc * 128:(kc + 1) * 128, 1024:1536]) for kc in range(8)], writes=["wv"], stream="wv")
                from collections import deque
                pend = deque()
                fmk = [0]

                def fm_unit(slab, c, t0, n, evac):
                    def run():
                        bank = fmk[0] % 4
                        fmk[0] += 1
                        pb = psb[bank]

                        def mm(e):
                            ins = None
                            for kc in range(8):
                                ins = e.matmul(pb[:, 0:n], lhsT=W.wsl[slab][:, kc, c * 128:(c + 1) * 128], rhs=hT[:, kc, t0:t0 + n],
                                               start=(kc == 0), stop=(kc == 7))
                            return ins
                        P.op("pe", mm, reads=[("wsl", slab)] + [("hT", x) for x in range(t0 // 128, (t0 + n + 127) // 128)], writes=[("ps", bank)])
                        evac(c, t0, n, pb, ("ps", bank))
                    return run

                def tm_unit(wsrc, wname, col_ap_fn, rows, tiles, evac):
                    def run():
                        bank = 4 + (kvc[0] % 2)
                        pb = psb[bank]

                        def mm(e):
                            ins = None
                            for kc in range(8):
                                ins = e.matmul(pb[:rows, :], lhsT=col_ap_fn(kc), rhs=wsrc[:, kc, :], start=(kc == 0), stop=(kc == 7))
                            return ins
                        P.op("pe", mm, reads=[wname] + [("hT", x) for x in tiles], writes=[("ps", bank)])
                        evac(pb, ("ps", bank))
                        kvc[0] += 1
                    return run

                def drain(k):
                    for _ in range(min(k, len(pend))):
                        pend.popleft()()

                def v_unit(bi, d, r, b, hist):
                    c0 = (NT - 128 * d + r) if hist else (128 * d * b + r)
                    vi = vt_index[(bi, r, b)]
                    if d == 1:
                        tiles = [15] if hist else [b]
                    elif d == 4:
                        tiles = list(range(12, 16)) if hist else list(range(4 * b, 4 * b + 4))
                    else:
                        tiles = list(range(16))

                    def ev(pb, bn):
                        if d == 1 and not hist:
                            st = kvc[0] % 2
                            P.op("dve", lambda e: e.tensor_copy(out=kvst[st][:, :], in_=pb[:, :]), reads=[bn], writes=[("kvst", st)])
                            P.op("pool", lambda e: e.tensor_copy(out=V[:, vi, :], in_=kvst[st][:, :]), reads=[("kvst", st)], writes=[("V", vi)])
                            P.dma("sp", vo_d[b * 128:(b + 1) * 128, :], kvst[st][:, :], reads=[("kvst", st)], stream="kv%d" % st)
                        else:
                            P.op("act", lambda e: e.copy(out=V[:, vi, :], in_=pb[:, :]), reads=[bn], writes=[("V", vi)])
                    return tm_unit(wv, "wv", lambda kc: hT[:, kc, c0:c0 + 127 * d + 1:d], 128, tiles, ev)

                sk = load_wslab([(w_in, 512, 512, 0)])
                P.dma("sp", rb34[:], rb34_d, writes=["rb34"], stream="c1")
                P.dma("sp", ohn[:], ohn_d, writes=["ohn"], stream="c1")
                for q in range(7):
                    st = q % 2
                    ohv = kvst[st][0:34, :].rearrange("p (a b) -> p a b", a=4)
                    P.dma("sp", ohv, ohs_d[:, 4 * q:4 * q + 4, :], writes=[("kvst", st)], stream="kv%d" % st)

                    def bsm(e, q=q, ohv=ohv):
                        ins = None
                        for k in range(4):
                            ut = 4 * q + k
                            ins = e.matmul(psb[5][:, ut * 8:ut * 8 + 8], lhsT=ohv[:, k, :], rhs=rb34[:, :], start=True, stop=True)
                        return ins
                    P.op("pe", bsm, reads=["rb34", ("kvst", st)], writes=[("ps", 5)])

                def bnm(e):
                    ins = None
                    for t in range(4):
                        ins = e.matmul(psb[5][0:4, 256 + t * 8:256 + t * 8 + 8], lhsT=ohn[:, t, :], rhs=rb34[:, :], start=True, stop=True)
                    return ins
                P.op("pe", bnm, reads=["rb34", "ohn"], writes=[("ps", 5)])
                P.op("dve", lambda e: e.tensor_copy(out=BS[:].rearrange("p u t h -> p (u t h)"), in_=psb[5][:, 0:224]), reads=[("ps", 5)], writes=["BS"])
                P.op("dve", lambda e: e.tensor_copy(out=BN[:].rearrange("p t h -> p (t h)"), in_=psb[5][0:4, 256:288]), reads=[("ps", 5)], writes=["BN"])

                def evac_kh(c, t0, n, pb, bn):
                    P.op("act", lambda e: e.copy(out=KT[:, c, t0:t0 + n], in_=pb[:, 0:n]), reads=[bn], writes=[("KT", c, t0 // 512)])

                def after_a(t):
                    if t % 4 == 3:
                        for c in range(4):
                            pend.append(fm_unit(sk, c, 512 * (t // 4), 512, evac_kh))
                    if t == 15:
                        pend.append(v_unit(0, 1, 0, -1, True))
                        for r in range(4):
                            pend.append(v_unit(1, 4, r, -1, True))
                        for r in range(16):
                            pend.append(v_unit(2, 16, r, -1, True))
                    drain(2)
                norm_pipeline([(xh[t * 128:(t + 1) * 128, :], 128, t * 128) for t in range(16)], after_a)
                sq = load_wslab([(w_in, 0, 512, 0)])
                drain(12)

                def evac_q(c, t0, n, pb, bn):
                    P.op("act", lambda e: e.mul(out=QT[:, c, t0:t0 + n], in_=pb[:, 0:n], mul=0.125), reads=[bn], writes=[("QT", c, t0 // 512)])

                def evac_k(c, t0, n, pb, bn):
                    if t0 < NT:
                        P.op("act", lambda e: e.copy(out=KT[:, c, NT + t0:NT + t0 + n], in_=pb[:, 0:n]), reads=[bn], writes=[("KT", c, 4 + t0 // 512)])
                    else:
                        P.op("act", lambda e: e.copy(out=KTs[:, c, :], in_=pb[:, 0:n]), reads=[bn], writes=[("KTs", c)])

                def ktm_unit(t):
                    rows = 128 if t < 16 else NS

                    def ev(pb, bn):
                        st = kvc[0] % 2
                        P.op("dve", lambda e: e.tensor_copy(out=kvst[st][:rows, :], in_=pb[:rows, :]), reads=[bn], writes=[("kvst", st)])
                        if t < 16:
                            P.dma("sp", ko_d[t * 128:(t + 1) * 128, :], kvst[st][:, :], reads=[("kvst", st)], stream="kv%d" % st)
                        else:
                            P.dma_group("sp", [(kso_d[b, 2044:2048, :], kvst[st][4 * b:4 * b + 4, :]) for b in range(4)],
                                        reads=[("kvst", st)], stream="kv%d" % st)
                    return tm_unit(W.wsl[sk], ("wsl", sk), lambda kc: hT[:, kc, t * 128:t * 128 + rows], rows, [t], ev)

                def vs_unit():
                    def ev(pb, bn):
                        st = kvc[0] % 2
                        P.op("dve", lambda e: e.tensor_copy(out=kvst[st][:NS, :], in_=pb[:NS, :]), reads=[bn], writes=[("kvst", st)])
                        P.dma_group("sp", [(vso_d[b, 2044:2048, :], kvst[st][4 * b:4 * b + 4, :]) for b in range(4)],
                                    reads=[("kvst", st)], stream="kv%d" % st)
                    return tm_unit(wv, "wv", lambda kc: hT[:, kc, NT:NTS], NS, [16], ev)

                def after_b(t):
                    pend.append(ktm_unit(t))
                    if t < 16:
                        pend.append(v_unit(0, 1, 0, t, False))
                    else:
                        pend.append(vs_unit())
                    if t % 4 == 3 or t == 16:
                        t0, n = TG_OWN[t // 4]
                        for c in range(4):
                            pend.append(fm_unit(sq, c, t0, n, evac_q))
                            pend.append(fm_unit(sk, c, t0, n, evac_k))
                        if t < 16:
                            for r in range(4):
                                pend.append(v_unit(1, 4, r, t // 4, False))
                    if t == 15:
                        for r in range(16):
                            pend.append(v_unit(2, 16, r, 0, False))
                    drain(4)
                drain(10 ** 6)
                norm_pipeline([(xo[t * 128:(t + 1) * 128, :], 128, t * 128) for t in range(16)] + [(xsm, NS, NT)], after_b)
                eb_setup(0, EB0, ("EB", 0), kvst[0], ("kvst", 0), kvst[1], ("kvst", 1))
                drain(10 ** 6)
                P.barrier()

            if STAGE < 6:
                P.finish()
                return nc
            with ExitStack() as es3:
                sb3 = mk_sb(es3)
                set_psum("attn")
                PA = psA[0]
                acc = sb3("acc", [128, 2, NT])
                NBUF = 3
                Eb = [sb3("Eb%d" % i, [128, 512], BF) for i in range(NBUF)]
                PT = [sb3("PT%d" % i, [128, 512], BF) for i in range(NBUF)]
                Hq = sb3("Hq", [128, 3, 2, 2, 128])
                TB = sb3("TB", [128, 3, 2, 2, 128])
                EBs = [EB0, sb3("EB1", [128, 3, 2, 2, 128], BF)]
                groups = []
                for j in range(4):
                    for bi, d in enumerate(BRANCH_D):
                        for r in range(d):
                            for b in range(16 // d):
                                groups.append((j, bi, d, r, b))

                def chunk_setup1(j):
                    pairs = []
                    for bi in range(3):
                        for hh in range(2):
                            src = bass.AP(esc.tensor, (bi * 8 + 2 * j + hh) * 384, [[1, 128], [128, 2], [1, 128]])
                            pairs.append((Hq[:, bi, hh, :, :], src))
                    P.dma_group("sp", pairs, reads=["esc"], writes=["Hq"], stream="hq")
                    for bi in range(3):
                        for dp in range(2):
                            for hh in range(2):
                                x = Hq[:, bi, hh, 1 - dp, 127:128]
                                rv = bass.AP(x.tensor, x.offset, [x.ap[0], [-1, 128]])
                                P.op("pool", lambda e, bi=bi, dp=dp, hh=hh, rv=rv: e.tensor_copy(out=TB[:, bi, dp, hh, :], in_=rv),
                                     reads=["Hq"], writes=["TB"])

                def chunk_setup2(j):
                    P.op("act", lambda e: e.activation(out=EBs[j % 2][:].rearrange("p a b c q -> p (a b c q)"),
                                                       in_=TB[:].rearrange("p a b c q -> p (a b c q)"), func=AF.Exp),
                         reads=["TB"], writes=[("EB", j % 2)])

                def geom(gi):
                    j, bi, d, r, b = groups[gi]
                    q0 = 128 * d * b + r
                    kD0 = NT + q0
                    kP0 = NT + 128 * d * (b - 1) + r
                    return j, bi, d, r, b, q0, kD0, kP0

                def stage1(gi):
                    j, bi, d, r, b, q0, kD0, kP0 = geom(gi)
                    if gi % 48 == 10 and j < 3:
                        chunk_setup1(j + 1)
                    if gi % 48 == 30 and j < 3:
                        chunk_setup2(j + 1)
                    if gi % 6 == 3:
                        issue_copies(1, q="sp")
                    if gi == 4:
                        P.dma_group("pool", [(wpre[:, kc, :], w_in[kc * 128:(kc + 1) * 128, 1536:2048]) for kc in range(8)],
                                    writes=["wpre"], stream="wpre")
                    sl = gi % NBUF
                    qgs = sorted(set([q0 // 512, (q0 + 127 * d) // 512]))
                    kgs = sorted(set([kD0 // 512, (kD0 + 127 * d) // 512, kP0 // 512, (kP0 + 127 * d) // 512]))

                    def smm(e):
                        ins = None
                        for hh in range(2):
                            rows = slice(64 * hh, 64 * hh + 64)
                            for dp, k0 in ((0, kD0), (1, kP0)):
                                ins = e.matmul(PA[:, 2 * sl + hh, dp * 128:dp * 128 + 128], lhsT=KT[rows, j, k0:k0 + 127 * d + 1:d],
                                               rhs=QT[rows, j, q0:q0 + 127 * d + 1:d], start=True, stop=True)
                        return ins
                    P.op("pe", smm, reads=[("QT", j, x) for x in qgs] + [("KT", j, x) for x in kgs], writes=[("ps", 2 * sl), ("ps", 2 * sl + 1)])
                    src = PA[:, 2 * sl:2 * sl + 2, 0:256]
                    e3 = Eb[sl][:, :].rearrange("p (h c) -> p h c", h=2)
                    if b == 0:
                        P.op("act", lambda e: e.activation(out=e3[:, :, 0:128], in_=src[:, :, 0:128], func=AF.Exp),
                             reads=[("ps", 2 * sl), ("ps", 2 * sl + 1)], writes=[("Eba", sl)])
                        P.op("act", lambda e: e.activation(out=e3[:, :, 128:256], in_=src[:, :, 128:256], func=AF.Exp, bias=hm[:, :]),
                             reads=[("ps", 2 * sl), ("ps", 2 * sl + 1), "hm"], writes=[("Ebb", sl)])
                    else:
                        P.op("act", lambda e: e.activation(out=e3, in_=src, func=AF.Exp),
                             reads=[("ps", 2 * sl), ("ps", 2 * sl + 1)], writes=[("Eba", sl), ("Ebb", sl)])
                    ebv = EBs[j % 2][:, bi, :, :, :].rearrange("p a h q -> p h a q")
                    e4 = Eb[sl][:, :].rearrange("p (h a q) -> p h a q", h=2, a=2)
                    p4 = PT[sl][:, :].rearrange("p (h a q) -> p h a q", h=2, a=2)
                    P.op("dve", lambda e: e.tensor_tensor(out=p4, in0=e4, in1=ebv, op=ALU.mult),
                         reads=[("Eba", sl), ("Ebb", sl), ("EB", j % 2)], writes=[("PTa", sl), ("PTb", sl)])

                def stage2(gi):
                    j, bi, d, r, b, q0, kD0, kP0 = geom(gi)
                    sl = gi % NBUF
                    ob = 6 + gi % 2
                    obank = PA[:, ob, :]
                    viD = vt_index[(bi, r, b)]
                    viP = vt_index[(bi, r, b - 1)]
                    qgs = sorted(set([q0 // 512, (q0 + 127 * d) // 512]))

                    def pvm(e):
                        ins = None
                        for hh in range(2):
                            h = 2 * j + hh
                            rows = slice(64 * hh, 64 * hh + 64)
                            pD = PT[sl][:, hh * 256:hh * 256 + 128]
                            pP = PT[sl][:, hh * 256 + 128:hh * 256 + 256]
                            e.matmul(PA[rows, ob, 0:128], lhsT=V[:, viD, 64 * h:64 * h + 64], rhs=pD, start=True, stop=False)
                            e.matmul(PA[rows, ob, 0:128], lhsT=V[:, viP, 64 * h:64 * h + 64], rhs=pP, start=False, stop=True)
                            e.matmul(PA[rows, ob, 128:256], lhsT=ones[:, :], rhs=pD, start=True, stop=False)
                            ins = e.matmul(PA[rows, ob, 128:256], lhsT=ones[:, :], rhs=pP, start=False, stop=True)
                        return ins
                    P.op("pe", pvm, reads=[("PTa", sl), ("PTb", sl), ("V", viD), ("V", viP), "ones"], writes=[("ps", ob)])
                    accv = acc[:, :, q0:q0 + 127 * d + 1:d]
                    ov = PA[:, ob, 0:256].rearrange("p (a q) -> p a q", a=2)
                    if bi == 0:
                        P.op("act", lambda e: e.copy(out=accv, in_=ov), reads=[("ps", ob)], writes=[("acc", x) for x in qgs])
                    else:
                        P.op("dve", lambda e: e.tensor_tensor(out=accv, in0=ov, in1=accv, op=ALU.add),
                             reads=[("ps", ob)] + [("acc", x) for x in qgs], writes=[("acc", x) for x in qgs])
                    if (bi, r, b) == (2, 15, 0):
                        allacc = [("acc", x) for x in range(4)]
                        P.op("act", lambda e: e.activation(out=acc[:, 1, :], in_=acc[:, 1, :], func=AF.Ln), reads=allacc, writes=allacc)
                        P.op("act", lambda e: e.activation(out=acc[:, 1, :], in_=acc[:, 1, :], func=AF.Exp, scale=-1.0), reads=allacc, writes=allacc)
                        P.op("dve", lambda e: e.tensor_tensor(out=QT[:, j, 0:NT], in0=acc[:, 0, :], in1=acc[:, 1, :], op=ALU.mult),
                             reads=allacc, writes=[("QT", j, x) for x in range(4)])

                LA = 2
                NG = len(groups)
                for gi in range(NG + LA):
                    if gi < NG:
                        stage1(gi)
                    if gi - LA >= 0:
                        stage2(gi - LA)
                P.barrier()
                set_psum("std")

        if STAGE < 7:
            P.finish()
            return nc
        with ExitStack() as es4:
            sb4 = mk_sb(es4)
            alloc_work(sb4)
            GU = sb4("GU", [128, 4, NTS], BF)
            WD = sb4("WD", [128, NFC, D], BF)
            with ExitStack() as es7:
                sb7 = mk_sb(es7)
                SbS = sb7("SbS", [128, 2, 128]); PS = sb7("PS", [128, 2, 128], BF)
                osb = sb7("osb", [128, 2, 64])
                Vn = sb7("Vn", [4, 4, 512], BF)
                Kst = sb7("Kst", [128, 3, 512]); Vst = sb7("Vst", [128, 3, 512])
                Kc = sb7("Kc", [128, 7, 512], BF)
                Vc = [sb7("Vc%d" % i, [128, 7, 512], BF) for i in range(2)]
                KsT = sb7("KsT", [128, 4, 7, 128], BF)

                def issue_loads(b, vfull=True):
                    pk, pv = [], []
                    for u in range(3):
                        for r4 in range(4):
                            r0 = 512 * u + r4
                            pk.append((Kst[r4:128:4, u, :], ck_d[b, r0:r0 + 16 * 31 + 1:16, :]))
                            pv.append((Vst[r4:128:4, u, :], cv_d[b, r0:r0 + 16 * 31 + 1:16, :]))
                    P.dma_group("sp", pk, writes=["Kst"], stream="kst")
                    P.dma_group("sp", pv, writes=["Vst"], stream="vst")
                    P.dma_group("pool", [(Kc[:, 3 + k, :], ck_d[b, 1536 + 128 * k:1536 + 128 * (k + 1), :]) for k in range(4)],
                                writes=["KcF"], stream="kcf")
                    if vfull:
                        issue_vfull(b)

                def issue_vfull(b):
                    P.dma_group("pool", [(Vc[b % 2][:, 3 + k, :], cv_d[b, 1536 + 128 * k:1536 + 128 * (k + 1), :]) for k in range(4)],
                                writes=[("VcF", b % 2)], stream="vcf%d" % (b % 2))
                issue_loads(0)
                sv = load_wslab([(w_in, 1024, 512, 0)])
                P.lastw["wpre"] = ("d_wpre", P.dsem["wpre"], 16 * P.dcnt["wpre"], None)
                k = 0
                for c in range(4):
                    for (t0, n) in TG_OWN:
                        bank = k % 4
                        k += 1
                        pb = psb[bank]

                        def mm(e, c=c, t0=t0, n=n, pb=pb):
                            ins = None
                            for kc in range(8):
                                ins = e.matmul(pb[:, 0:n], lhsT=wpre[:, kc, c * 128:(c + 1) * 128], rhs=hT[:, kc, t0:t0 + n],
                                               start=(kc == 0), stop=(kc == 7))
                            return ins
                        P.op("pe", mm, reads=["wpre"], writes=[("ps", bank)])
                        P.op("act", lambda e, c=c, t0=t0, n=n, pb=pb: e.copy(out=GU[:, c, t0:t0 + n], in_=pb[:, 0:n]),
                             reads=[("ps", bank)], writes=[("GU", c, t0 // 512)])

                P.op("dve", lambda e: e.memset(SbS[:], 0.0), writes=["SbS"])
                for b in range(4):
                    def evn(pb, bn, b=b):
                        P.op("act", lambda e: e.copy(out=Vn[0:4, b, :], in_=pb[0:4, :]), reads=[bn], writes=[("Vn", b)])
                    tm_proj(sv, "hT", lambda kc, b=b: hT[:, kc, NT + 4 * b:NT + 4 * b + 4], 4, evn)
                obank = psb[5]

                def st_T(b):
                    i = b % 2
                    P.op("act", lambda e: e.copy(out=Kc[:, 0:3, :], in_=Kst[:, :, :]), reads=["Kst"], writes=["KcP"])
                    P.op("dve", lambda e, i=i: e.tensor_copy(out=Vc[i][:, 0:3, :], in_=Vst[:, :, :]), reads=["Vst"], writes=[("VcP", i)])
                    blocks = [(u, jj) for u in range(7) for jj in range(4)]
                    for c0 in range(0, 28, 8):
                        chunk = blocks[c0:c0 + 8]
                        ti = (c0 // 8) % 2
                        pt = ptb[ti]

                        def trs(e, chunk=chunk, pt=pt, i=i):
                            ins = None
                            for k, (u, jj) in enumerate(chunk):
                                ins = e.transpose(out=pt[:, k * 128:(k + 1) * 128], in_=Kc[:, u, jj * 128:(jj + 1) * 128], identity=ident[:, :])
                            return ins
                        P.op("pe", trs, reads=["KcP", "KcF", "ident"], writes=[("pt", ti)])
                        for k, (u, jj) in enumerate(chunk):
                            eng = "act" if k % 2 == 0 else "dve"
                            P.op(eng, (lambda e, k=k, u=u, jj=jj, pt=pt, i=i: (e.copy if False else e.tensor_copy)(out=KsT[:, jj, u, :], in_=pt[:, k * 128:(k + 1) * 128]))
                                 if eng == "dve" else (lambda e, k=k, u=u, jj=jj, pt=pt, i=i: e.copy(out=KsT[:, jj, u, :], in_=pt[:, k * 128:(k + 1) * 128])),
                                 reads=[("pt", ti)], writes=[("KsT", u, jj)])

                def st_S(b):
                    i = b % 2
                    qc = NT + 4 * b

                    def ssm(e, i=i, b=b, qc=qc):
                        ins = None
                        for hh in range(2):
                            rows = slice(64 * hh, 64 * hh + 64)
                            for jj in range(4):
                                for u in range(7):
                                    ins = e.matmul(psb[1 + hh][:, (u * 4 + jj) * 4:(u * 4 + jj) * 4 + 4], lhsT=KsT[rows, jj, u, :],
                                                   rhs=QT[rows, jj, qc:qc + 4], start=True, stop=True)
                                ins = e.matmul(psb[1 + hh][0:4, 112 + jj * 4:112 + jj * 4 + 4], lhsT=KTs[rows, jj, 4 * b:4 * b + 4],
                                               rhs=QT[rows, jj, qc:qc + 4], start=True, stop=True)
                        return ins
                    P.op("pe", ssm, reads=[("KsT", u, jj) for u in range(7) for jj in range(4)] + [("QT", jj, 4) for jj in range(4)] + [("KTs", jj) for jj in range(4)],
                         writes=[("ps", 1), ("ps", 2)])
                    for hh in range(2):
                        sbk = psb[1 + hh]
                        in0 = sbk[:, 0:112].rearrange("p (u j t) -> p u j t", u=7, j=4)
                        x = BS[:, :, :, hh:hh + 1]
                        in1 = bass.AP(x.tensor, x.offset, [x.ap[0], [32, 7], [2, 4], [8, 4]])
                        out = SbS[:, hh, 0:112].rearrange("p (u j t) -> p u j t", u=7, j=4)
                        P.op("dve", lambda e, in0=in0, in1=in1, out=out: e.tensor_tensor(out=out, in0=in0, in1=in1, op=ALU.add),
                             reads=[("ps", 1 + hh), "BS"], writes=[("SbS", hh)])
                        in0n = sbk[0:4, 112:128].rearrange("p (j t) -> p j t", j=4)
                        xn_ = BN[:, :, hh:hh + 1]
                        in1n = bass.AP(xn_.tensor, xn_.offset, [xn_.ap[0], [2, 4], [8, 4]])
                        outn = SbS[0:4, hh, 112:128].rearrange("p (j t) -> p j t", j=4)
                        P.op("dve", lambda e, in0n=in0n, in1n=in1n, outn=outn: e.tensor_tensor(out=outn, in0=in0n, in1=in1n, op=ALU.add),
                             reads=[("ps", 1 + hh), "BN"], writes=[("SbSn", hh)])
                    P.op("act", lambda e: e.activation(out=PS[:].rearrange("p a c -> p (a c)"), in_=SbS[:].rearrange("p a c -> p (a c)"), func=AF.Exp),
                         reads=[("SbS", 0), ("SbS", 1), ("SbSn", 0), ("SbSn", 1), "SbS"], writes=["PS"])

                def st_V(b):
                    i = b % 2

                    def spv(e, i=i, b=b):
                        ins = None
                        for hh in range(2):
                            rows = slice(64 * hh, 64 * hh + 64)
                            for jj in range(4):
                                h = 2 * jj + hh
                                for part in range(2):
                                    oc = (part * 4 + jj) * 16 + 4 * b
                                    for u in range(7):
                                        lhsT = Vc[i][:, u, 64 * h:64 * h + 64] if part == 0 else ones[:, :]
                                        e.matmul(obank[rows, oc:oc + 4], lhsT=lhsT, rhs=PS[:, hh, (u * 4 + jj) * 4:(u * 4 + jj) * 4 + 4],
                                                 start=(u == 0), stop=False)
                                    lhsT = Vn[0:4, b, 64 * h:64 * h + 64] if part == 0 else ones[0:4, :]
                                    ins = e.matmul(obank[rows, oc:oc + 4], lhsT=lhsT, rhs=PS[0:4, hh, 112 + jj * 4:112 + jj * 4 + 4],
                                                   start=False, stop=True)
                        return ins
                    P.op("pe", spv, reads=["PS", ("VcP", i), ("VcF", i), ("Vn", b), "ones"], writes=[("ps", 5)])
                st_T(0)
                issue_loads(1)
                st_S(0)
                for b in range(1, 4):
                    st_T(b)
                    if b + 1 < 4:
                        issue_loads(b + 1, vfull=False)
                    st_V(b - 1)
                    if b + 1 < 4:
                        issue_vfull(b + 1)
                    if b == 3:
                        svg = load_wslab([(w_in, 2048, 512, 0)])
                        wo0 = load_wslab([(w_out, 0, 512, 0)])
                        P.dma_group("pool", [(wpre[:, kc, :], w_out[kc * 128:(kc + 1) * 128, 512:1024]) for kc in range(8)],
                                    writes=["wpre"], stream="wpre")
                    st_S(b)
                st_V(3)
                P.op("dve", lambda e: e.tensor_copy(out=osb[:].rearrange("p a c -> p (a c)"), in_=obank[:, 0:128]), reads=[("ps", 5)], writes=["osb"])
                P.op("dve", lambda e: e.reciprocal(out=osb[:, 1, :], in_=osb[:, 1, :]), reads=["osb"], writes=["osb"])
                P.op("dve", lambda e: e.tensor_tensor(out=QT[:, :, NT:NTS], in0=osb[:, 0, :].rearrange("p (j c) -> p j c", j=4),
                                                      in1=osb[:, 1, :].rearrange("p (j c) -> p j c", j=4), op=ALU.mult),
                     reads=["osb"], writes=[("QT", jj, 4) for jj in range(4)])
                P.barrier()

            with ExitStack() as es5:
                sb5 = mk_sb(es5)
                lng = sb5("lng", [128, 512]); lnb = sb5("lnb", [128, 512])
                lnw = [sb5("lnw%d" % i, [128, 512]) for i in range(3)]
                vnb = [sb5("vnb%d" % i, [128, 512], BF) for i in range(3)]
                wtmp = sb5("wtmp", [128, 8, 128]); tri = sb5("tri", [128, 128]); WmT = sb5("WmT", [128, 8, 128], BF)
                bsp = sb5("bsp", [128, 4, 128]); gtmp = [sb5("gtmp%d" % i, [128, 512]) for i in range(2)]
                bdf = sb5("bdf", [NS, 8, NS]); bdm = sb5("bdm", [NS, NS]); BD = sb5("BD", [NS, 8, NS], BF)
                x2t = [sb5("x2t%d" % i, [128, D]) for i in range(2)]
                st2 = sb5("st2", [128, 24])
                P.dma("sp", lng[:], lng_d, writes=["lng"], stream="c0")
                P.dma("sp", lnb[:], lnb_d, writes=["lnb"], stream="c0")
                P.dma("sp", tri[:], tri_d, writes=["tri"], stream="c0")
                P.dma("sp", bdm[:], bdm_d, writes=["bdm"], stream="c0")
                P.dma("sp", bsp[:], bsp_d.rearrange("j p t -> p j t"), writes=["bsp"], stream="c0")
                P.dma("sp", wtmp[:], sgw_d.rearrange("g s t -> s g t"), writes=["wtmp"], stream="c0")
                trb = bass.AP(tri[:].tensor, tri[:].offset, [tri[:].ap[0], [0, 8], [1, 128]])
                P.op("dve", lambda e: e.tensor_tensor(out=WmT[:], in0=wtmp[:], in1=trb, op=ALU.mult), reads=["wtmp", "tri"], writes=["WmT"])
                P.op("dve", lambda e: e.memset(bdf[:], 0.0), writes=["bdf"])
                with nc.allow_non_contiguous_dma(reason="tiny 4x4 sgu blocks"):
                    P.dma_group("sp", [(bdf[4 * b:4 * b + 4, :, 4 * b:4 * b + 4], sgw_d[:, 0:4, 0:4].rearrange("g s t -> s g t")) for b in range(4)],
                                reads=[], writes=["bdf"], stream="c0")
                bdb = bass.AP(bdm[:].tensor, bdm[:].offset, [bdm[:].ap[0], [0, 8], [1, NS]])
                P.op("dve", lambda e: e.tensor_tensor(out=BD[:], in0=bdf[:], in1=bdb, op=ALU.mult), reads=["bdf", "bdm"], writes=["BD"])

                lnc = [0]

                def ln_rows(pb, bn, rows, out_ap, outname):
                    i = lnc[0] % 2
                    lnc[0] += 1
                    o = 8 * i
                    L = lnw[i]
                    P.op("act", lambda e: e.activation(out=L[:rows, :], in_=pb[:rows, :], func=AF.Identity, accum_out=st2[:rows, o:o + 1]),
                         reads=[bn], writes=[("lnw", i), ("st_sum", i)])
                    P.op("act", lambda e: e.activation(out=W.xn[i][:rows, 0:512], in_=pb[:rows, :], func=AF.Square, accum_out=st2[:rows, o + 1:o + 2]),
                         reads=[bn], writes=[("xn", i), ("st_sq", i)])
                    P.op("dve", lambda e: e.tensor_scalar(out=st2[:rows, o + 2:o + 3], in0=st2[:rows, o:o + 1], scalar1=1.0 / 512, scalar2=None, op0=ALU.mult),
                         reads=[("st_sum", i)], writes=[("st_mean", i)])
                    P.op("dve", lambda e: e.tensor_tensor(out=st2[:rows, o + 3:o + 4], in0=st2[:rows, o + 2:o + 3], in1=st2[:rows, o + 2:o + 3], op=ALU.mult),
                         reads=[("st_mean", i)], writes=[("st_m2", i)])
                    P.op("dve", lambda e: e.scalar_tensor_tensor(out=st2[:rows, o + 4:o + 5], in0=st2[:rows, o + 1:o + 2], scalar=1.0 / 512,
                                                                 in1=st2[:rows, o + 3:o + 4], op0=ALU.mult, op1=ALU.subtract),
                         reads=[("st_sq", i), ("st_m2", i)], writes=[("st_var", i)])
                    P.op("act", lambda e: e.activation(out=st2[:rows, o + 5:o + 6], in_=st2[:rows, o + 4:o + 5], func=AF.Sqrt, scale=1.0, bias=epst[:rows, :]),
                         reads=[("st_var", i), "eps"], writes=[("st_sd", i)])
                    P.op("dve", lambda e: e.reciprocal(out=st2[:rows, o + 6:o + 7], in_=st2[:rows, o + 5:o + 6]), reads=[("st_sd", i)], writes=[("st_rstd", i)])
                    P.op("dve", lambda e: e.tensor_scalar(out=L[:rows, :], in0=L[:rows, :], scalar1=st2[:rows, o + 2:o + 3], scalar2=st2[:rows, o + 6:o + 7],
                                                          op0=ALU.subtract, op1=ALU.mult),
                         reads=[("lnw", i), ("st_mean", i), ("st_rstd", i)], writes=[("lnw", i)])
                    P.op("dve", lambda e: e.tensor_tensor(out=L[:rows, :], in0=L[:rows, :], in1=lng[:rows, :], op=ALU.mult),
                         reads=[("lnw", i), "lng"], writes=[("lnw", i)])
                    P.op("dve", lambda e: e.tensor_tensor(out=out_ap, in0=L[:rows, :], in1=lnb[:rows, :], op=ALU.add),
                         reads=[("lnw", i), "lnb"], writes=[outname])
                    return i

                wx = wpre
                vslot = {}

                ljunk = [sb5("ljunk%d" % i, [128, 512], BF) for i in range(3)]
                vslot = {}
                pbank = {}

                def cd_vg(t):
                    rows = 128 if t < 16 else NS
                    bank = (4, 5, 1)[t % 3]
                    pb = psb[bank]
                    pbank[t] = (pb, ("ps", bank))

                    def mm(e):
                        ins = None
                        for kc in range(8):
                            ins = e.matmul(pb[:rows, :], lhsT=hT[:, kc, t * 128:t * 128 + rows], rhs=W.wsl[svg][:, kc, :], start=(kc == 0), stop=(kc == 7))
                        return ins
                    P.op("pe", mm, reads=[("wsl", svg), ("hTt", t)], writes=[("ps", bank)])

                def cd_ln(t):
                    rows = 128 if t < 16 else NS
                    pb, bn = pbank[t]
                    i = t % 3
                    vslot[t] = i
                    o = 8 * i
                    L = lnw[i]
                    P.op("act", lambda e: e.activation(out=L[:rows, :], in_=pb[:rows, :], func=AF.Identity, accum_out=st2[:rows, o:o + 1]),
                         reads=[bn], writes=[("lnw", i), ("st_sum", i)])
                    P.op("act", lambda e: e.activation(out=ljunk[i][:rows, :], in_=pb[:rows, :], func=AF.Square, accum_out=st2[:rows, o + 1:o + 2]),
                         reads=[bn], writes=[("ljunk", i), ("st_sq", i)])
                    P.op("dve", lambda e: e.tensor_scalar(out=st2[:rows, o + 2:o + 3], in0=st2[:rows, o:o + 1], scalar1=1.0 / 512, scalar2=None, op0=ALU.mult),
                         reads=[("st_sum", i)], writes=[("st_mean", i)])
                    P.op("dve", lambda e: e.tensor_tensor(out=st2[:rows, o + 3:o + 4], in0=st2[:rows, o + 2:o + 3], in1=st2[:rows, o + 2:o + 3], op=ALU.mult),
                         reads=[("st_mean", i)], writes=[("st_m2", i)])
                    P.op("dve", lambda e: e.scalar_tensor_tensor(out=st2[:rows, o + 4:o + 5], in0=st2[:rows, o + 1:o + 2], scalar=1.0 / 512,
                                                                 in1=st2[:rows, o + 3:o + 4], op0=ALU.mult, op1=ALU.subtract),
                         reads=[("st_sq", i), ("st_m2", i)], writes=[("st_var", i)])
                    P.op("act", lambda e: e.activation(out=st2[:rows, o + 5:o + 6], in_=st2[:rows, o + 4:o + 5], func=AF.Ln, scale=1.0, bias=epst[:rows, :]),
                         reads=[("st_var", i), "eps"], writes=[("st_sd", i)])
                    P.op("act", lambda e: e.activation(out=st2[:rows, o + 6:o + 7], in_=st2[:rows, o + 5:o + 6], func=AF.Exp, scale=-0.5),
                         reads=[("st_sd", i)], writes=[("st_rstd", i)])
                    P.op("dve", lambda e: e.tensor_scalar(out=L[:rows, :], in0=L[:rows, :], scalar1=st2[:rows, o + 2:o + 3], scalar2=st2[:rows, o + 6:o + 7],
                                                          op0=ALU.subtract, op1=ALU.mult),
                         reads=[("lnw", i), ("st_mean", i), ("st_rstd", i)], writes=[("lnw", i)])
                    P.op("pool", lambda e: e.tensor_tensor(out=L[:rows, :], in0=L[:rows, :], in1=lng[:rows, :], op=ALU.mult),
                         reads=[("lnw", i), "lng"], writes=[("lnw", i)])
                    if t < 16:
                        P.op("pool", lambda e: e.tensor_tensor(out=vnb[i][:rows, :], in0=L[:rows, :], in1=lnb[:rows, :], op=ALU.add),
                             reads=[("lnw", i), "lnb"], writes=[("vnb", i)])
                    else:
                        P.op("pool", lambda e: e.tensor_tensor(out=L[:rows, :], in0=L[:rows, :], in1=lnb[:rows, :], op=ALU.add),
                             reads=[("lnw", i), "lnb"], writes=[("lnw", i)])
                        P.dma("sp", svo_d, L[:rows, :], reads=[("lnw", i)], stream="svo")
                        P.op("pool", lambda e: e.tensor_copy(out=vnb[i][:rows, :], in_=L[:rows, :]), reads=[("lnw", i)], writes=[("vnb", i)])

                def cd_sgu(t):
                    rows = 128 if t < 16 else NS
                    ncol = rows
                    vi = vslot[t]
                    gi = t % 2
                    gb_ = psb[0]

                    def gm(e):
                        ins = None
                        for g8 in range(8):
                            jj, hh = g8 // 2, g8 % 2
                            rhs = WmT[:, g8, :] if t < 16 else BD[:, g8, :]
                            ins = e.matmul(gb_[64 * hh:64 * hh + 64, jj * 128:jj * 128 + ncol], lhsT=vnb[vi][:rows, 64 * g8:64 * g8 + 64],
                                           rhs=rhs, start=True, stop=True)
                        return ins
                    P.op("pe", gm, reads=[("vnb", vi), "WmT", "BD"], writes=[("ps", 0)])
                    gv = gb_[:, :].rearrange("p (j q) -> p j q", j=4)[:, :, 0:ncol]
                    gt_ = gtmp[gi][:, :].rearrange("p (j q) -> p j q", j=4)[:, :, 0:ncol]
                    if t < 16:
                        bv = bsp[:, :, :]
                    else:
                        x = bsp[:, :, 0:4]
                        bv = bass.AP(x.tensor, x.offset, [x.ap[0], x.ap[1], [0, 4], [1, 4]])
                        gv = gv.rearrange("p j (b q) -> p j b q", b=4)
                        gt_ = gt_.rearrange("p j (b q) -> p j b q", b=4)
                    P.op("dve", lambda e: e.tensor_tensor(out=gt_, in0=gv, in1=bv, op=ALU.add), reads=[("ps", 0), "bsp"], writes=[("gtmp", gi)])
                    guv = GU[:, :, t * 128:t * 128 + ncol]
                    g2 = gtmp[gi][:, :].rearrange("p (j q) -> p j q", j=4)[:, :, 0:ncol]
                    P.op("dve", lambda e: e.tensor_tensor(out=guv, in0=g2, in1=guv, op=ALU.mult),
                         reads=[("gtmp", gi)] + [("GU", c, t // 4) for c in range(4)], writes=[("GUt", t)])

                dslot = {}

                def cd_xload(t):
                    rows = 128 if t < 16 else NS
                    s = tcount[0] % 2
                    tcount[0] += 1
                    dslot[t] = s
                    P.dma("sp", W.xs[s][:rows, :], (xo[t * 128:(t + 1) * 128, :] if t < 16 else xsm), writes=[("xs", s)], stream="x%d" % s)

                def cd_oproj(t):
                    rows = 128 if t < 16 else NS
                    for half in range(2):
                        bank = 2 + half
                        pb = psb[bank]
                        wsrc = W.wsl[wo0] if half == 0 else wx

                        def om(e, pb=pb, wsrc=wsrc):
                            ins = None
                            for kc in range(8):
                                src = QT if kc < 4 else GU
                                ins = e.matmul(pb[:rows, :], lhsT=src[:, kc % 4, t * 128:t * 128 + rows], rhs=wsrc[:, kc, :],
                                               start=(kc == 0), stop=(kc == 7))
                            return ins
                        P.op("pe", om, reads=[("wsl", wo0), "wx", ("GUt", t)], writes=[("ps", bank)])

                def cd_resid(t):
                    rows = 128 if t < 16 else NS
                    s = dslot[t]
                    for half in range(2):
                        bank = 2 + half
                        pb = psb[bank]
                        P.op("dve", lambda e, pb=pb, half=half: e.tensor_tensor(
                            out=x2t[s][:rows, half * 512:(half + 1) * 512], in0=pb[:rows, :], in1=W.xs[s][:rows, half * 512:(half + 1) * 512], op=ALU.add),
                            reads=[("ps", bank), ("xs", s)], writes=[("x2t", s, half)])
                    P.dma("sp", x2s[t * 128:t * 128 + rows, :], x2t[s][:rows, :], reads=[("x2t", s, 0), ("x2t", s, 1)], writes=[("x2s", t)], stream="x2o%d" % s)
                    norm_B(x2t[s], ("x2t", s, 1), rows, s)

                def cd_tr(t):
                    rows = 128 if t < 16 else NS
                    norm_C(rows, g2t, "g2t", hT, "hTt", t * 128, dslot[t], gran=128)

                cd_vg(0)
                cd_ln(0)
                cd_vg(1)
                cd_ln(1)
                cd_sgu(0)
                cd_xload(0)
                for i in range(18):
                    if i + 1 < 17:
                        cd_xload(i + 1)
                    if i < 11:
                        P.dma_group("pool", [(WD[:, fc, :], w_down[fc * 128:(fc + 1) * 128, :]) for fc in (2 * i, 2 * i + 1)],
                                    writes=[("WD", i)], stream="wd")
                    if i + 2 < 17:
                        cd_vg(i + 2)
                    if i + 1 < 17:
                        cd_sgu(i + 1)
                    if i < 17:
                        cd_oproj(i)
                    if i + 2 < 17:
                        cd_ln(i + 2)
                    if i < 17:
                        cd_resid(i)
                    if 0 <= i - 1 < 17:
                        cd_tr(i - 1)
                s_ffn0 = load_wslab([(w_gate, 0, 256, 0), (w_up, 0, 256, 256)])
                P.barrier()

            if STAGE < 9:
                P.finish()
                return nc
            with ExitStack() as es6:
                sb6 = mk_sb(es6)
                HT = 1024 + NS
                AT = sb6("AT", [128, NFC, HT], BF)
                tmpf = [sb6("tmpf%d" % i, [128, 512]) for i in range(2)]
                yst = W.xs
                gfb = sb6("gfb", [128, D])
                P.dma("sp", gfb[:], gf_d, writes=["gfb"], stream="c0")
                issue_copies(99)
                ec = [0]
                for hf in range(2):
                    base = 1024 * hf
                    tgs = [(0, 512), (512, 512)] + ([(1024, NS)] if hf == 1 else [])
                    for fs in range(11):
                        if hf == 0 and fs == 0:
                            s = s_ffn0
                            P.lastw[("wsl", s)] = None
                        else:
                            s = load_wslab([(w_gate, fs * 256, 256, 0), (w_up, fs * 256, 256, 256)])
                        for fcl in range(2):
                            fc = 2 * fs + fcl
                            for (l0, n) in tgs:
                                i = ec[0] % 2
                                ec[0] += 1
                                ba, bb = psb[i], psb[2 + i]
                                t0 = base + l0

                                def gm(e, ba=ba, t0=t0, n=n, fcl=fcl, s=s):
                                    ins = None
                                    for kc in range(8):
                                        ins = e.matmul(ba[:, 0:n], lhsT=W.wsl[s][:, kc, fcl * 128:(fcl + 1) * 128], rhs=hT[:, kc, t0:t0 + n],
                                                       start=(kc == 0), stop=(kc == 7))
                                    return ins

                                def um(e, bb=bb, t0=t0, n=n, fcl=fcl, s=s):
                                    ins = None
                                    for kc in range(8):
                                        ins = e.matmul(bb[:, 0:n], lhsT=W.wsl[s][:, kc, 256 + fcl * 128:256 + (fcl + 1) * 128], rhs=hT[:, kc, t0:t0 + n],
                                                       start=(kc == 0), stop=(kc == 7))
                                    return ins
                                P.op("pe", gm, reads=[("wsl", s), ("hT", t0 // 512)], writes=[("ps", i)])
                                P.op("pe", um, reads=[("wsl", s), ("hT", t0 // 512)], writes=[("ps", 2 + i)])
                                P.op("act", lambda e, i=i, ba=ba, n=n: e.activation(out=tmpf[i][:, 0:n], in_=ba[:, 0:n], func=AF.Silu),
                                     reads=[("ps", i)], writes=[("tmpf", i)])
                                P.op("dve", lambda e, i=i, bb=bb, n=n, fc=fc, l0=l0: e.tensor_tensor(out=AT[:, fc, l0:l0 + n], in0=bb[:, 0:n], in1=tmpf[i][:, 0:n], op=ALU.mult),
                                     reads=[("ps", 2 + i), ("tmpf", i)], writes=[("AT", l0 // 512)])
                    tiles = list(range(8 * hf, 8 * hf + 8)) + ([16] if hf == 1 else [])
                    for t in tiles:
                        rows = 128 if t < 16 else NS
                        l0 = t * 128 - base
                        s = tcount[0] % 2
                        tcount[0] += 1
                        P.dma("sp", W.xs[s][:rows, :], x2s[t * 128:t * 128 + rows, :], reads=[("x2s", t)],
                              writes=[("xs", s), ("yst", s, 0), ("yst", s, 1)], stream="x%d" % s)
                        for half in range(2):
                            bank = 4 + half
                            pb = psb[bank]

                            def dm(e, pb=pb, half=half, l0=l0, rows=rows):
                                ins = None
                                for fc in range(NFC):
                                    ins = e.matmul(pb[:rows, :], lhsT=AT[:, fc, l0:l0 + rows], rhs=WD[:, fc, half * 512:(half + 1) * 512],
                                                   start=(fc == 0), stop=(fc == NFC - 1))
                                return ins
                            P.op("pe", dm, reads=["WD", ("AT", l0 // 512)], writes=[("ps", bank)])
                            P.op("dve", lambda e, pb=pb, half=half, s=s, rows=rows: e.tensor_tensor(
                                out=yst[s][:rows, half * 512:(half + 1) * 512], in0=pb[:rows, :], in1=W.xs[s][:rows, half * 512:(half + 1) * 512], op=ALU.add),
                                reads=[("ps", bank), ("xs", s)], writes=[("yst", s, half)])
                        rstd_of(yst[s], ("yst", s, 1), rows, s)
                        P.op("dve", lambda e, s=s, rows=rows: e.scalar_tensor_tensor(out=yst[s][:rows, :], in0=yst[s][:rows, :], scalar=ss[:rows, 4 + s:5 + s],
                                                                                   in1=gfb[:rows, :], op0=ALU.mult, op1=ALU.mult),
                             reads=[("yst", s, 0), ("yst", s, 1), ("rstd", s), "gfb"], writes=[("yst", s, 0), ("yst", s, 1)])
                        dst = y_d[t * 128:(t + 1) * 128, :] if t < 16 else ys_d
                        P.dma("sp", dst, yst[s][:rows, :], reads=[("yst", s, 0), ("yst", s, 1)], stream="yo%d" % s)

        P.finish()
        esp[0].close()
    return nc


_CACHE = {}


def _get_program():
    if "nc" not in _CACHE:
        _CACHE["nc"] = build_program()
    return _CACHE["nc"]


def kernel(x_prompt, x_sample, cache_k_win, cache_v_win, norm1_g, w_in, sgu_ln_g, sgu_ln_b, sgu_w, sgu_b,
           w_out, norm2_g, w_gate, w_up, w_down, rel_bias, final_g):
    f = lambda a: np.ascontiguousarray(np.asarray(a, dtype=np.float32))
    xp = f(x_prompt); xsa = f(x_sample)
    ck = f(cache_k_win)[0].reshape(32, 2048, 512); cv = f(cache_v_win)[0].reshape(32, 2048, 512)
    common = {
        "w_in": f(w_in)[0], "w_out": f(w_out)[0], "w_gate": f(w_gate)[0], "w_up": f(w_up)[0], "w_down": f(w_down)[0],
        "g1t": f(f(norm1_g)[0].reshape(8, 128).T), "g2t": f(f(norm2_g)[0].reshape(8, 128).T),
        "gfb": f(np.broadcast_to(f(final_g)[None, :], (128, D))),
        "lng": f(np.broadcast_to(f(sgu_ln_g)[0][None, :], (128, 512))),
        "lnb": f(np.broadcast_to(f(sgu_ln_b)[0][None, :], (128, 512))),
        "sgwT": f(f(sgu_w)[0].transpose(0, 2, 1)),
        "tri": f(np.triu(np.ones((128, 128), np.float32))),
        "bsp": f(np.repeat(f(sgu_b)[0].reshape(4, 2, 128), 64, axis=1)),
        "rb33": f(np.concatenate([f(rel_bias), np.full((1, 8), NEG, np.float32)], axis=0)),
        "oh": _onehot_consts(), "ident": np.eye(128, dtype=np.float32),
        "bdm": f(np.kron(np.eye(4, dtype=np.float32), np.triu(np.ones((4, 4), np.float32)))),
        "rb34": f(np.concatenate([f(rel_bias), np.full((1, 8), NEG, np.float32), np.ones((1, 8), np.float32)], axis=0)),
        "ohs": _sample_consts()[0], "ohn": _sample_consts()[1],
    }
    in_maps = []
    for c in range(NCORES):
        b, half = c // 2, c % 2
        own = xp[b, half * NT:(half + 1) * NT]
        hist = xp[b, 0:NT]
        m = dict(common)
        m.update({
            "xo": f(own), "xh": f(hist), "xsm": f(xsa[4 * c:4 * c + 4].reshape(NS, D)),
            "hm": np.full((128, 1), 0.0 if half == 1 else NEG, np.float32),
            "ck": f(ck[4 * c:4 * c + 4]), "cv": f(cv[4 * c:4 * c + 4]),
        })
        in_maps.append(m)
    if _CACHE.get("debug_core") is not None:
        c = _CACHE["debug_core"]
        return run_bass_kernel_spmd(_get_program(), [in_maps[c]], core_ids=[0]).results[0]
    nc = _get_program()
    res = run_bass_kernel_spmd(nc, in_maps, core_ids=list(range(NCORES)))
    R = res.results
    y = np.zeros((4, 4096, D), np.float32)
    kp = np.zeros((1, 4, 2048, 8, 64), np.float32); vp = np.zeros_like(kp)
    for c in range(NCORES):
        b, half = c // 2, c % 2
        y[b, half * NT:(half + 1) * NT] = R[c]["y"]
        if half == 1:
            kp[0, b] = R[c]["ko"].reshape(2048, 8, 64)
            vp[0, b] = R[c]["vo"].reshape(2048, 8, 64)
    ysm = np.concatenate([R[c]["ys"].reshape(4, 4, D) for c in range(NCORES)], axis=0)
    ks = np.concatenate([R[c]["kso"] for c in range(NCORES)], axis=0).reshape(1, 32, 2048, 8, 64)
    vs = np.concatenate([R[c]["vso"] for c in range(NCORES)], axis=0).reshape(1, 32, 2048, 8, 64)
    sv = np.concatenate([R[c]["svo"].reshape(4, 4, 512) for c in range(NCORES)], axis=0).reshape(1, 32, 4, 512)
    return (y, ysm, kp, vp, ks, vs, sv)
```

```python
import numpy as np
from contextlib import ExitStack
import concourse.bass as bass
import concourse.mybir as mybir
from concourse.bass_utils import run_bass_kernel_spmd

F32 = mybir.dt.float32
BF = mybir.dt.bfloat16
AF = mybir.ActivationFunctionType
ALU = mybir.AluOpType

NCORES = 8
D = 1024
NT = 2048
NS = 16
NTS = NT + NS
DFF = 2816
NFC = 22
EPS = 1e-6
NEG = -30000.0
BRANCH_D = (1, 4, 16)
import os
STAGE = int(os.environ.get('KSTAGE', '99'))


class Prog:
    def __init__(self, nc, es):
        self.nc = nc
        self.es = es
        self.eng = {"pe": nc.tensor, "act": nc.scalar, "dve": nc.vector, "pool": nc.gpsimd, "sp": nc.sync}
        self.sem = {k: es.enter_context(nc.semaphore("s_" + k)) for k in self.eng}
        self.cnt = {k: 0 for k in self.eng}
        self.waited = {k: {} for k in self.eng}
        self.lastw = {}
        self.readers = {}
        self.dsem = {}
        self.dcnt = {}

    def _deps(self, e, reads, writes):
        toks = []
        for r in reads:
            t = self.lastw.get(r)
            if t is not None:
                toks.append(t)
            if isinstance(r, tuple) and r[0] in ("ps", "pt"):
                toks.extend(tk for tk in self.readers.get(r, ()) if tk[3] != e)
        for w in writes:
            t = self.lastw.get(w)
            if t is not None:
                toks.append(t)
            toks.extend(self.readers.get(w, ()))
        for (key, handle, val, prod) in toks:
            if prod == "pe" and e == "pe":
                continue
            if prod is None:
                val = max(val, 16 * self.dcnt[key[2:]])
            if self.waited[e].get(key, 0) >= val:
                continue
            self.eng[e].wait_ge(handle, val)
            self.waited[e][key] = val

    def _commit(self, tok, reads, writes):
        for w in writes:
            self.lastw[w] = tok
            self.readers[w] = []
        for r in reads:
            if r in writes:
                continue
            self.readers.setdefault(r, []).append(tok)

    def op(self, e, fn, reads=(), writes=()):
        self._deps(e, reads, writes)
        ins = fn(self.eng[e])
        self.cnt[e] += 1
        ins.then_inc(self.sem[e], 1)
        self._commit((e, self.sem[e], self.cnt[e], e), reads, writes)

    def dma(self, q, out, in_, reads=(), writes=(), stream=None, **kw):
        if stream not in self.dsem:
            self.dsem[stream] = self.es.enter_context(self.nc.semaphore("d_" + str(stream)))
            self.dcnt[stream] = 0
        self._deps(q, reads, writes)
        ins = self.eng[q].dma_start(out=out, in_=in_, **kw)
        self.dcnt[stream] += 1
        ins.then_inc(self.dsem[stream], 16)
        self._commit(("d_" + str(stream), self.dsem[stream], 16 * self.dcnt[stream], None), reads, writes)

    def dma_group(self, q, pairs, reads=(), writes=(), stream=None, **kw):
        if stream not in self.dsem:
            self.dsem[stream] = self.es.enter_context(self.nc.semaphore("d_" + str(stream)))
            self.dcnt[stream] = 0
        self._deps(q, reads, writes)
        for (out, in_) in pairs:
            ins = self.eng[q].dma_start(out=out, in_=in_, **kw)
            self.dcnt[stream] += 1
            ins.then_inc(self.dsem[stream], 16)
        self._commit(("d_" + str(stream), self.dsem[stream], 16 * self.dcnt[stream], None), reads, writes)

    def barrier(self):
        for e in self.eng:
            for o in self.eng:
                if o != e and self.cnt[o] > self.waited[e].get(o, 0):
                    self.eng[e].wait_ge(self.sem[o], self.cnt[o])
                    self.waited[e][o] = self.cnt[o]
            for s, h in self.dsem.items():
                if s in ("cpk", "cpv"):
                    continue
                v = 16 * self.dcnt[s]
                if v > self.waited[e].get("d_" + str(s), 0):
                    self.eng[e].wait_ge(h, v)
                    self.waited[e]["d_" + str(s)] = v
        self.lastw.clear()
        self.readers.clear()

    def finish(self):
        for s, h in self.dsem.items():
            v = 16 * self.dcnt[s]
            if v > self.waited["sp"].get("d_" + str(s), 0):
                self.nc.sync.wait_ge(h, v)


def _rel_bucket_np(dist):
    dist = np.asarray(dist)
    df = np.maximum(dist, 1).astype(np.float32)
    large = 16 + (np.log(df / np.float32(16)) / np.float32(np.log(2048 / 16)) * np.float32(16)).astype(np.int32)
    large = np.minimum(large, 31)
    return np.where(dist < 16, dist, large)


def _onehot_consts():
    oh = np.zeros((3, 33, 384), np.float32)
    for bi, d in enumerate(BRANCH_D):
        for i in range(384):
            j = 255 - i
            if 0 <= j <= 128:
                oh[bi, int(_rel_bucket_np(j * d)), i] = 1.0
            else:
                oh[bi, 32, i] = 1.0
    return oh


def _nmult(delta):
    return int(delta <= 128) + int(delta % 4 == 0 and delta <= 512) + int(delta % 16 == 0 and delta <= 2048)


def _sample_row(u, p):
    return 16 * (32 * u + p // 4) + p % 4 if u < 3 else 1536 + 128 * (u - 3) + p


def _sample_consts():
    ohs = np.zeros((34, 28, 128), np.float32)
    for u in range(7):
        for t in range(4):
            for p in range(128):
                delta = 2048 + t - _sample_row(u, p)
                n = _nmult(delta)
                if n == 0:
                    ohs[32, u * 4 + t, p] = 1.0
                else:
                    ohs[int(_rel_bucket_np(delta)), u * 4 + t, p] = 1.0
                    ohs[33, u * 4 + t, p] = np.float32(np.log(n))
    ohn = np.zeros((34, 4, 4), np.float32)
    for t in range(4):
        for tp in range(4):
            delta = t - tp
            if delta < 0:
                ohn[32, t, tp] = 1.0
            else:
                ohn[int(_rel_bucket_np(delta)), t, tp] = 1.0
                ohn[33, t, tp] = np.float32(np.log(_nmult(delta)))
    return ohs, ohn


def build_program():
    nc = bass.Bass("TRN2", target_bir_lowering=False)
    dt = lambda n, s, kind="ExternalInput": nc.dram_tensor(n, s, F32, kind=kind).ap()
    xo = dt("xo", [NT, D]); xh = dt("xh", [NT, D]); xsm = dt("xsm", [NS, D])
    hm_d = dt("hm", [128, 1])
    w_in = dt("w_in", [D, 2560]); w_out = dt("w_out", [D, D])
    w_gate = dt("w_gate", [D, DFF]); w_up = dt("w_up", [D, DFF]); w_down = dt("w_down", [DFF, D])
    g1t_d = dt("g1t", [128, 8]); g2t_d = dt("g2t", [128, 8]); gf_d = dt("gfb", [128, D])
    lng_d = dt("lng", [128, 512]); lnb_d = dt("lnb", [128, 512])
    sgw_d = dt("sgwT", [8, 128, 128]); tri_d = dt("tri", [128, 128]); bsp_d = dt("bsp", [4, 128, 128])
    bdm_d = dt("bdm", [NS, NS])
    rb34_d = dt("rb34", [34, 8]); ohs_d = dt("ohs", [34, 28, 128]); ohn_d = dt("ohn", [34, 4, 4])
    rb_d = dt("rb33", [33, 8]); oh_d = dt("oh", [3, 33, 384]); id_d = dt("ident", [128, 128])
    ck_d = dt("ck", [4, 2048, 512]); cv_d = dt("cv", [4, 2048, 512])
    y_d = dt("y", [NT, D], "ExternalOutput"); ys_d = dt("ys", [NS, D], "ExternalOutput")
    ko_d = dt("ko", [NT, 512], "ExternalOutput"); vo_d = dt("vo", [NT, 512], "ExternalOutput")
    kso_d = dt("kso", [4, 2048, 512], "ExternalOutput"); vso_d = dt("vso", [4, 2048, 512], "ExternalOutput")
    svo_d = dt("svo", [NS, 512], "ExternalOutput")
    x2s = dt("x2s", [NTS, D], "Internal")
    esc = dt("esc", [3, 8, 384], "Internal")

    with ExitStack() as es:
        P = Prog(nc, es)

        uid = [0]

        def mk_sb(st):
            def f(n, s, d=F32):
                uid[0] += 1
                return st.enter_context(nc.sbuf_tensor("sb%d_%s" % (uid[0], n), s, d))
            return f
        sb = mk_sb(es)
        psb, ptb, psA = [], [], [None]
        esp = [ExitStack()]
        pcount = [0]

        def set_psum(mode):
            esp[0].close()
            esp[0] = ExitStack()
            pcount[0] += 1
            psb[:] = []
            ptb[:] = []
            if mode == "std":
                psb.extend(esp[0].enter_context(nc.psum_tensor("ps%d_%d" % (pcount[0], i), [128, 512], F32)) for i in range(6))
                ptb.extend(esp[0].enter_context(nc.psum_tensor("pt%d_%d" % (pcount[0], i), [128, 1024], BF)) for i in range(2))
            else:
                psA[0] = esp[0].enter_context(nc.psum_tensor("psA%d" % pcount[0], [128, 8, 512], F32))
        set_psum("std")

        ident_f = sb("ident_f", [128, 128]); ident = sb("ident", [128, 128], BF)
        g1t = sb("g1t", [128, 8]); g2t = sb("g2t", [128, 8]); hm = sb("hm", [128, 1])
        epst = sb("epst", [128, 1]); ss = sb("ss", [128, 8]); ones = sb("ones", [128, 64], BF)
        P.dma("sp", ident_f[:], id_d, writes=["ident_f"], stream="c0")
        P.dma("sp", g1t[:], g1t_d, writes=["g1t"], stream="c0")
        P.dma("sp", g2t[:], g2t_d, writes=["g2t"], stream="c0")
        P.dma("sp", hm[:], hm_d, writes=["hm"], stream="c0")
        P.op("dve", lambda e: e.tensor_copy(out=ident[:], in_=ident_f[:]), reads=["ident_f"], writes=["ident"])
        P.op("dve", lambda e: e.memset(epst[:], EPS), writes=["eps"])
        P.op("dve", lambda e: e.memset(ones[:], 1.0), writes=["ones"])

        cp_pending = []
        for b in range(4):
            for q in range(4):
                cp_pending.append((kso_d[b, 511 * q:511 * (q + 1), :], ck_d[b, 4 + 511 * q:4 + 511 * (q + 1), :], "cpk"))
                cp_pending.append((vso_d[b, 511 * q:511 * (q + 1), :], cv_d[b, 4 + 511 * q:4 + 511 * (q + 1), :], "cpv"))

        def issue_copies(n, q="act"):
            for _ in range(min(n, len(cp_pending))):
                o, i_, st = cp_pending.pop(0)
                P.dma(q, o, i_, stream=st)

        BS = sb("BS", [128, 7, 4, 8]); BN = sb("BN", [4, 4, 8])
        rb34 = sb("rb34", [34, 8]); ohn = sb("ohn", [34, 4, 4])
        with ExitStack() as es0:
            sb0 = mk_sb(es0)
            rb33 = sb0("rb33", [33, 8]); ohs = sb0("ohs", [33, 3, 384]); e_sb = sb0("e_sb", [8, 3, 384])
            P.dma("sp", rb33[:], rb_d, writes=["rb33"], stream="c0")
            P.dma("sp", ohs[:], oh_d.rearrange("d k i -> k d i"), writes=["ohs"], stream="c0")
            for bi in range(3):
                P.op("pe", lambda e, bi=bi: e.matmul(psb[bi][:8, 0:384], lhsT=rb33[:, :], rhs=ohs[:, bi, :], start=True, stop=True),
                     reads=["rb33", "ohs"], writes=[("ps", bi)])
                P.op("dve", lambda e, bi=bi: e.tensor_copy(out=e_sb[:, bi, :], in_=psb[bi][:8, 0:384]), reads=[("ps", bi)], writes=["e_sb"])
            P.dma("sp", esc.rearrange("d h i -> h d i"), e_sb[:], reads=["e_sb"], writes=["esc"], stream="c0")
            P.barrier()

        hT = sb("hT", [128, 8, NTS], BF)
        QT = sb("QT", [128, 4, NTS], BF)
        KTs = sb("KTs", [128, 4, NS], BF)
        wpre = sb("wpre", [128, 8, 512], BF)

        EB0 = sb("EB0", [128, 3, 2, 2, 128], BF)

        def eb_setup(j, EB, ebname, hq, hqname, tb, tbname):
            for bi in range(3):
                hv = hq[:, :].rearrange("p (h t q) -> p h t q", h=2, t=2)
                tv = tb[:, :].rearrange("p (a h q) -> p a h q", a=2, h=2)
                pairs = []
                for hh in range(2):
                    src = bass.AP(esc.tensor, (bi * 8 + 2 * j + hh) * 384, [[1, 128], [128, 2], [1, 128]])
                    pairs.append((hv[:, hh, :, :], src))
                P.dma_group("sp", pairs, reads=["esc"], writes=[hqname], stream="hq")
                for dp in range(2):
                    for hh in range(2):
                        x = hv[:, hh, 1 - dp, 127:128]
                        rv = bass.AP(x.tensor, x.offset, [x.ap[0], [-1, 128]])
                        P.op("pool", lambda e, dp=dp, hh=hh, rv=rv, tv=tv: e.tensor_copy(out=tv[:, dp, hh, :], in_=rv),
                             reads=[hqname], writes=[tbname])
                P.op("act", lambda e, bi=bi, tb=tb: e.activation(out=EB[:, bi, :, :, :].rearrange("p a h q -> p (a h q)"), in_=tb[:, :], func=AF.Exp),
                     reads=[tbname], writes=[ebname])

        wcount = [0]
        tcount = [0]
        kvc = [0]
        TG_OWN = [(0, 512), (512, 512), (1024, 512), (1536, 512), (2048, NS)]
        TG_HIST = [(0, 512), (512, 512), (1024, 512), (1536, 512)]

        class WS:
            pass
        W = WS()

        def alloc_work(sbw):
            W.xs = [sbw("xs%d" % i, [128, D]) for i in range(2)]
            W.xn = [sbw("xn%d" % i, [128, D], BF) for i in range(2)]
            W.wsl = [sbw("wsl%d" % i, [128, 8, 512], BF) for i in range(2)]

        def load_wslab(parts):
            s = wcount[0] % 2
            wcount[0] += 1
            pairs = []
            for (wd, c0, ncols, o0) in parts:
                pairs += [(W.wsl[s][:, kc, o0:o0 + ncols], wd[kc * 128:(kc + 1) * 128, c0:c0 + ncols]) for kc in range(8)]
            P.dma_group("pool", pairs, writes=[("wsl", s)], stream="w%d" % s)
            return s

        def norm_T(src_d, rows, gt, gname, dstT, dstname, col0):
            s = tcount[0] % 2
            tcount[0] += 1
            X = W.xs[s]
            P.dma("sp", X[:rows, :], src_d, writes=[("xs", s)], stream="x%d" % s)
            norm_T_sb(X, ("xs", s), rows, gt, gname, dstT, dstname, col0, s)
            return s

        def rstd_of(X, xname, rows, s):
            P.op("act", lambda e: e.activation(out=W.xn[s][:rows, :], in_=X[:rows, :], func=AF.Square,
                                               accum_out=ss[:rows, s:s + 1]),
                 reads=[xname], writes=[("ss", s), ("xn", s)])
            P.op("act", lambda e: e.activation(out=ss[:rows, 2 + s:3 + s], in_=ss[:rows, s:s + 1], func=AF.Ln,
                                               scale=1.0 / D, bias=epst[:rows, :]),
                 reads=[("ss", s), "eps"], writes=[("sd", s)])
            P.op("act", lambda e: e.activation(out=ss[:rows, 4 + s:5 + s], in_=ss[:rows, 2 + s:3 + s], func=AF.Exp, scale=-0.5),
                 reads=[("sd", s)], writes=[("rstd", s)])

        def norm_B(X, xname, rows, s):
            rstd_of(X, xname, rows, s)
            P.op("dve", lambda e: e.tensor_scalar(out=W.xn[s][:rows, :], in0=X[:rows, :], scalar1=ss[:rows, 4 + s:5 + s],
                                                  scalar2=None, op0=ALU.mult),
                 reads=[xname, ("rstd", s)], writes=[("xn", s)])

        def norm_C(rows, gt, gname, dstT, dstname, col0, s, gran=512):
            pt = ptb[s]

            def tr(e):
                ins = None
                for kc in range(8):
                    ins = e.transpose(out=pt[:, kc * 128:kc * 128 + rows], in_=W.xn[s][:rows, kc * 128:(kc + 1) * 128],
                                      identity=ident[:rows, :rows])
                return ins
            P.op("pe", tr, reads=[("xn", s), "ident"], writes=[("pt", s)])
            src3 = pt[:].rearrange("p (k t) -> p k t", k=8)[:, :, 0:rows]
            gb = bass.AP(gt[:].tensor, gt[:].offset, [gt[:].ap[0], [1, 8], [0, rows]])
            P.op("dve", lambda e: e.tensor_tensor(out=dstT[:, :, col0:col0 + rows], in0=src3, in1=gb, op=ALU.mult),
                 reads=[("pt", s), gname], writes=[(dstname, col0 // gran)])

        def norm_T_sb(X, xname, rows, gt, gname, dstT, dstname, col0, s, gran=512):
            norm_B(X, xname, rows, s)
            norm_C(rows, gt, gname, dstT, dstname, col0, s, gran)

        def norm_pipeline(tiles, after_cb):
            n = len(tiles)
            base = tcount[0]
            tcount[0] += n
            for i in range(n + 2):
                if i < n:
                    src_d, rows, col0 = tiles[i]
                    s = (base + i) % 2
                    P.dma("sp", W.xs[s][:rows, :], src_d, writes=[("xs", s)], stream="x%d" % s)
                if 0 <= i - 1 < n:
                    src_d, rows, col0 = tiles[i - 1]
                    s = (base + i - 1) % 2
                    norm_B(W.xs[s], ("xs", s), rows, s)
                if 0 <= i - 2 < n:
                    src_d, rows, col0 = tiles[i - 2]
                    s = (base + i - 2) % 2
                    norm_C(rows, g1t, "g1t", hT, "hT", col0, s, gran=128)
                    after_cb(i - 2)

        def fm_proj(slab, c_lo, nchunks, srcT, srcname, tgs, evac):
            k = 0
            for c in range(nchunks):
                for (t0, n) in tgs:
                    bank = k % 4
                    k += 1
                    pb = psb[bank]

                    def mm(e, c=c, t0=t0, n=n, pb=pb):
                        ins = None
                        for kc in range(8):
                            ins = e.matmul(pb[:, 0:n], lhsT=W.wsl[slab][:, kc, (c_lo + c) * 128:(c_lo + c + 1) * 128],
                                           rhs=srcT[:, kc, t0:t0 + n], start=(kc == 0), stop=(kc == 7))
                        return ins
                    P.op("pe", mm, reads=[("wsl", slab), (srcname, t0 // 512)], writes=[("ps", bank)])
                    evac(c, t0, n, pb, ("ps", bank))

        def tm_proj(slab, srcname, col_ap_fn, rows, evac, src_reads=None):
            bank = 4 + (kvc[0] % 2)
            pb = psb[bank]

            def mm(e):
                ins = None
                for kc in range(8):
                    ins = e.matmul(pb[:rows, :], lhsT=col_ap_fn(kc), rhs=W.wsl[slab][:, kc, :], start=(kc == 0), stop=(kc == 7))
                return ins
            P.op("pe", mm, reads=[("wsl", slab)] + (src_reads if src_reads is not None else [(srcname, i) for i in range(5)]),
                 writes=[("ps", bank)])
            evac(pb, ("ps", bank))
            kvc[0] += 1

        vt_index = {}
        for bi, d in enumerate(BRANCH_D):
            for r in range(d):
                for b in range(-1, 16 // d):
                    vt_index[(bi, r, b)] = len(vt_index)
        NVT = len(vt_index)

        with ExitStack() as es1:
            sb1 = mk_sb(es1)
            KT = sb1("KT", [128, 4, 2 * NT], BF)
            V = sb1("V", [128, NVT, 512], BF)
            with ExitStack() as es2:
                sb2 = mk_sb(es2)
                alloc_work(sb2)
                kvst = [sb2("kvst%d" % i, [128, 512]) for i in range(2)]
                wv = sb2("wv", [128, 8, 512], BF)
                from collections import deque
                pend = deque()
                fmk = [0]

                def fm_unit(slab, c, t0, n, evac):
                    def run():
                        bank = fmk[0] % 4
                        fmk[0] += 1
                        pb = psb[bank]

                        def mm(e):
                            ins = None
                            for kc in range(8):
                                ins = e.matmul(pb[:, 0:n], lhsT=W.wsl[slab][:, kc, c * 128:(c + 1) * 128], rhs=hT[:, kc, t0:t0 + n],
                                               start=(kc == 0), stop=(kc == 7))
                            return ins
                        P.op("pe", mm, reads=[("wsl", slab)] + [("hT", x) for x in range(t0 // 128, (t0 + n + 127) // 128)], writes=[("ps", bank)])
                        evac(c, t0, n, pb, ("ps", bank))
                    return run

                def tm_unit(wsrc, wname, col_ap_fn, rows, tiles, evac):
                    def run():
                        bank = 4 + (kvc[0] % 2)
                        pb = psb[bank]

                        def mm(e):
                            ins = None
                            for kc in range(8):
                                ins = e.matmul(pb[:rows, :], lhsT=col_ap_fn(kc), rhs=wsrc[:, kc, :], start=(kc == 0), stop=(kc == 7))
                            return ins
                        P.op("pe", mm, reads=[wname] + [("hT", x) for x in tiles], writes=[("ps", bank)])
                        evac(pb, ("ps", bank))
                        kvc[0] += 1
                    return run

                def drain(k):
                    for _ in range(min(k, len(pend))):
                        pend.popleft()()

                def v_unit(bi, d, r, b, hist):
                    c0 = (NT - 128 * d + r) if hist else (128 * d * b + r)
                    vi = vt_index[(bi, r, b)]
                    if d == 1:
                        tiles = [15] if hist else [b]
                    elif d == 4:
                        tiles = list(range(12, 16)) if hist else list(range(4 * b, 4 * b + 4))
                    else:
                        tiles = list(range(16))

                    def ev(pb, bn):
                        if d == 1 and not hist:
                            st = kvc[0] % 2
                            P.op("dve", lambda e: e.tensor_copy(out=kvst[st][:, :], in_=pb[:, :]), reads=[bn], writes=[("kvst", st)])
                            P.op("pool", lambda e: e.tensor_copy(out=V[:, vi, :], in_=kvst[st][:, :]), reads=[("kvst", st)], writes=[("V", vi)])
                            P.dma("sp", vo_d[b * 128:(b + 1) * 128, :], kvst[st][:, :], reads=[("kvst", st)], stream="kv%d" % st)
                        else:
                            P.op("act", lambda e: e.copy(out=V[:, vi, :], in_=pb[:, :]), reads=[bn], writes=[("V", vi)])
                    return tm_unit(wv, "wv", lambda kc: hT[:, kc, c0:c0 + 127 * d + 1:d], 128, tiles, ev)

                sk = load_wslab([(w_in, 512, 512, 0)])
                P.dma("sp", rb34[:], rb34_d, writes=["rb34"], stream="c1")
                P.dma("sp", ohn[:], ohn_d, writes=["ohn"], stream="c1")
                for q in range(7):
                    st = q % 2
                    ohv = kvst[st][0:34, :].rearrange("p (a b) -> p a b", a=4)
                    P.dma("sp", ohv, ohs_d[:, 4 * q:4 * q + 4, :], writes=[("kvst", st)], stream="kv%d" % st)

                    def bsm(e, q=q, ohv=ohv):
                        ins = None
                        for k in range(4):
                            ut = 4 * q + k
                            ins = e.matmul(psb[5][:, ut * 8:ut * 8 + 8], lhsT=ohv[:, k, :], rhs=rb34[:, :], start=True, stop=True)
                        return ins
                    P.op("pe", bsm, reads=["rb34", ("kvst", st)], writes=[("ps", 5)])

                def bnm(e):
                    ins = None
                    for t in range(4):
                        ins = e.matmul(psb[5][0:4, 256 + t * 8:256 + t * 8 + 8], lhsT=ohn[:, t, :], rhs=rb34[:, :], start=True, stop=True)
                    return ins
                P.op("pe", bnm, reads=["rb34", "ohn"], writes=[("ps", 5)])
                P.op("dve", lambda e: e.tensor_copy(out=BS[:].rearrange("p u t h -> p (u t h)"), in_=psb[5][:, 0:224]), reads=[("ps", 5)], writes=["BS"])
                P.op("dve", lambda e: e.tensor_copy(out=BN[:].rearrange("p t h -> p (t h)"), in_=psb[5][0:4, 256:288]), reads=[("ps", 5)], writes=["BN"])

                def evac_kh(c, t0, n, pb, bn):
                    P.op("act", lambda e: e.copy(out=KT[:, c, t0:t0 + n], in_=pb[:, 0:n]), reads=[bn], writes=[("KT", c, t0 // 512)])

                def after_a(t):
                    if t == 3:
                        P.dma_group("pool", [(wv[:, kc, :], w_in[kc * 128:(kc + 1) * 128, 1024:1536]) for kc in range(8)],
                                    writes=["wv"], stream="wv")
                    if t % 4 == 3:
                        for c in range(4):
                            pend.append(fm_unit(sk, c, 512 * (t // 4), 512, evac_kh))
                    if t == 15:
                        pend.append(v_unit(0, 1, 0, -1, True))
                        for r in range(4):
                            pend.append(v_unit(1, 4, r, -1, True))
                        for r in range(16):
                            pend.append(v_unit(2, 16, r, -1, True))
                    drain(2)
                norm_pipeline([(xh[t * 128:(t + 1) * 128, :], 128, t * 128) for t in range(16)], after_a)
                sq = load_wslab([(w_in, 0, 512, 0)])
                drain(12)

                def evac_q(c, t0, n, pb, bn):
                    P.op("act", lambda e: e.mul(out=QT[:, c, t0:t0 + n], in_=pb[:, 0:n], mul=0.125), reads=[bn], writes=[("QT", c, t0 // 512)])

                def evac_k(c, t0, n, pb, bn):
                    if t0 < NT:
                        P.op("act", lambda e: e.copy(out=KT[:, c, NT + t0:NT + t0 + n], in_=pb[:, 0:n]), reads=[bn], writes=[("KT", c, 4 + t0 // 512)])
                    else:
                        P.op("act", lambda e: e.copy(out=KTs[:, c, :], in_=pb[:, 0:n]), reads=[bn], writes=[("KTs", c)])

                def ktm_unit(t):
                    rows = 128 if t < 16 else NS

                    def ev(pb, bn):
                        st = kvc[0] % 2
                        P.op("dve", lambda e: e.tensor_copy(out=kvst[st][:rows, :], in_=pb[:rows, :]), reads=[bn], writes=[("kvst", st)])
                        if t < 16:
                            P.dma("sp", ko_d[t * 128:(t + 1) * 128, :], kvst[st][:, :], reads=[("kvst", st)], stream="kv%d" % st)
                        else:
                            P.dma_group("sp", [(kso_d[b, 2044:2048, :], kvst[st][4 * b:4 * b + 4, :]) for b in range(4)],
                                        reads=[("kvst", st)], stream="kv%d" % st)
                    return tm_unit(W.wsl[sk], ("wsl", sk), lambda kc: hT[:, kc, t * 128:t * 128 + rows], rows, [t], ev)

                def vs_unit():
                    def ev(pb, bn):
                        st = kvc[0] % 2
                        P.op("dve", lambda e: e.tensor_copy(out=kvst[st][:NS, :], in_=pb[:NS, :]), reads=[bn], writes=[("kvst", st)])
                        P.dma_group("sp", [(vso_d[b, 2044:2048, :], kvst[st][4 * b:4 * b + 4, :]) for b in range(4)],
                                    reads=[("kvst", st)], stream="kv%d" % st)
                    return tm_unit(wv, "wv", lambda kc: hT[:, kc, NT:NTS], NS, [16], ev)

                def after_b(t):
                    pend.append(ktm_unit(t))
                    if t < 16:
                        pend.append(v_unit(0, 1, 0, t, False))
                    else:
                        pend.append(vs_unit())
                    if t % 4 == 3 or t == 16:
                        t0, n = TG_OWN[t // 4]
                        for c in range(4):
                            pend.append(fm_unit(sq, c, t0, n, evac_q))
                            pend.append(fm_unit(sk, c, t0, n, evac_k))
                        if t < 16:
                            for r in range(4):
                                pend.append(v_unit(1, 4, r, t // 4, False))
                    if t == 15:
                        for r in range(16):
                            pend.append(v_unit(2, 16, r, 0, False))
                    drain(4)
                drain(10 ** 6)
                norm_pipeline([(xo[t * 128:(t + 1) * 128, :], 128, t * 128) for t in range(16)] + [(xsm, NS, NT)], after_b)
                eb_setup(0, EB0, ("EB", 0), kvst[0], ("kvst", 0), kvst[1], ("kvst", 1))
                drain(10 ** 6)
                P.barrier()

            if STAGE < 6:
                P.finish()
                return nc
            with ExitStack() as es3:
                sb3 = mk_sb(es3)
                set_psum("attn")
                PA = psA[0]
                acc = sb3("acc", [128, 2, NT])
                NBUF = 3
                Eb = [sb3("Eb%d" % i, [128, 512], BF) for i in range(NBUF)]
                PT = [sb3("PT%d" % i, [128, 512], BF) for i in range(NBUF)]
                Hq = sb3("Hq", [128, 3, 2, 2, 128])
                TB = sb3("TB", [128, 3, 2, 2, 128])
                EBs = [EB0, sb3("EB1", [128, 3, 2, 2, 128], BF)]
                groups = []
                for j in range(4):
                    for bi, d in enumerate(BRANCH_D):
                        for r in range(d):
                            for b in range(16 // d):
                                groups.append((j, bi, d, r, b))

                def chunk_setup1(j):
                    pairs = []
                    for bi in range(3):
                        for hh in range(2):
                            src = bass.AP(esc.tensor, (bi * 8 + 2 * j + hh) * 384, [[1, 128], [128, 2], [1, 128]])
                            pairs.append((Hq[:, bi, hh, :, :], src))
                    P.dma_group("sp", pairs, reads=["esc"], writes=["Hq"], stream="hq")
                    for bi in range(3):
                        for dp in range(2):
                            for hh in range(2):
                                x = Hq[:, bi, hh, 1 - dp, 127:128]
                                rv = bass.AP(x.tensor, x.offset, [x.ap[0], [-1, 128]])
                                P.op("pool", lambda e, bi=bi, dp=dp, hh=hh, rv=rv: e.tensor_copy(out=TB[:, bi, dp, hh, :], in_=rv),
                                     reads=["Hq"], writes=["TB"])

                def chunk_setup2(j):
                    P.op("act", lambda e: e.activation(out=EBs[j % 2][:].rearrange("p a b c q -> p (a b c q)"),
                                                       in_=TB[:].rearrange("p a b c q -> p (a b c q)"), func=AF.Exp),
                         reads=["TB"], writes=[("EB", j % 2)])

                def geom(gi):
                    j, bi, d, r, b = groups[gi]
                    q0 = 128 * d * b + r
                    kD0 = NT + q0
                    kP0 = NT + 128 * d * (b - 1) + r
                    return j, bi, d, r, b, q0, kD0, kP0

                def stage1(gi):
                    j, bi, d, r, b, q0, kD0, kP0 = geom(gi)
                    if gi % 48 == 10 and j < 3:
                        chunk_setup1(j + 1)
                    if gi % 48 == 30 and j < 3:
                        chunk_setup2(j + 1)
                    if gi % 6 == 3:
                        issue_copies(1, q="sp")
                    if gi == 4:
                        P.dma_group("pool", [(wpre[:, kc, :], w_in[kc * 128:(kc + 1) * 128, 1536:2048]) for kc in range(8)],
                                    writes=["wpre"], stream="wpre")
                    sl = gi % NBUF
                    qgs = sorted(set([q0 // 512, (q0 + 127 * d) // 512]))
                    kgs = sorted(set([kD0 // 512, (kD0 + 127 * d) // 512, kP0 // 512, (kP0 + 127 * d) // 512]))

                    def smm(e):
                        ins = None
                        for hh in range(2):
                            rows = slice(64 * hh, 64 * hh + 64)
                            for dp, k0 in ((0, kD0), (1, kP0)):
                                ins = e.matmul(PA[:, 2 * sl + hh, dp * 128:dp * 128 + 128], lhsT=KT[rows, j, k0:k0 + 127 * d + 1:d],
                                               rhs=QT[rows, j, q0:q0 + 127 * d + 1:d], start=True, stop=True)
                        return ins
                    P.op("pe", smm, reads=[("QT", j, x) for x in qgs] + [("KT", j, x) for x in kgs], writes=[("ps", 2 * sl), ("ps", 2 * sl + 1)])
                    src = PA[:, 2 * sl:2 * sl + 2, 0:256]
                    e3 = Eb[sl][:, :].rearrange("p (h c) -> p h c", h=2)
                    if b == 0:
                        P.op("act", lambda e: e.activation(out=e3[:, :, 0:128], in_=src[:, :, 0:128], func=AF.Exp),
                             reads=[("ps", 2 * sl), ("ps", 2 * sl + 1)], writes=[("Eba", sl)])
                        P.op("act", lambda e: e.activation(out=e3[:, :, 128:256], in_=src[:, :, 128:256], func=AF.Exp, bias=hm[:, :]),
                             reads=[("ps", 2 * sl), ("ps", 2 * sl + 1), "hm"], writes=[("Ebb", sl)])
                    else:
                        P.op("act", lambda e: e.activation(out=e3, in_=src, func=AF.Exp),
                             reads=[("ps", 2 * sl), ("ps", 2 * sl + 1)], writes=[("Eba", sl), ("Ebb", sl)])
                    ebv = EBs[j % 2][:, bi, :, :, :].rearrange("p a h q -> p h a q")
                    e4 = Eb[sl][:, :].rearrange("p (h a q) -> p h a q", h=2, a=2)
                    p4 = PT[sl][:, :].rearrange("p (h a q) -> p h a q", h=2, a=2)
                    P.op("dve", lambda e: e.tensor_tensor(out=p4, in0=e4, in1=ebv, op=ALU.mult),
                         reads=[("Eba", sl), ("Ebb", sl), ("EB", j % 2)], writes=[("PTa", sl), ("PTb", sl)])

                def stage2(gi):
                    j, bi, d, r, b, q0, kD0, kP0 = geom(gi)
                    sl = gi % NBUF
                    ob = 6 + gi % 2
                    obank = PA[:, ob, :]
                    viD = vt_index[(bi, r, b)]
                    viP = vt_index[(bi, r, b - 1)]
                    qgs = sorted(set([q0 // 512, (q0 + 127 * d) // 512]))

                    def pvm(e):
                        ins = None
                        for hh in range(2):
                            h = 2 * j + hh
                            rows = slice(64 * hh, 64 * hh + 64)
                            pD = PT[sl][:, hh * 256:hh * 256 + 128]
                            pP = PT[sl][:, hh * 256 + 128:hh * 256 + 256]
                            e.matmul(PA[rows, ob, 0:128], lhsT=V[:, viD, 64 * h:64 * h + 64], rhs=pD, start=True, stop=False)
                            e.matmul(PA[rows, ob, 0:128], lhsT=V[:, viP, 64 * h:64 * h + 64], rhs=pP, start=False, stop=True)
                            e.matmul(PA[rows, ob, 128:256], lhsT=ones[:, :], rhs=pD, start=True, stop=False)
                            ins = e.matmul(PA[rows, ob, 128:256], lhsT=ones[:, :], rhs=pP, start=False, stop=True)
                        return ins
                    P.op("pe", pvm, reads=[("PTa", sl), ("PTb", sl), ("V", viD), ("V", viP), "ones"], writes=[("ps", ob)])
                    accv = acc[:, :, q0:q0 + 127 * d + 1:d]
                    ov = PA[:, ob, 0:256].rearrange("p (a q) -> p a q", a=2)
                    if bi == 0:
                        P.op("act", lambda e: e.copy(out=accv, in_=ov), reads=[("ps", ob)], writes=[("acc", x) for x in qgs])
                    else:
                        P.op("dve", lambda e: e.tensor_tensor(out=accv, in0=ov, in1=accv, op=ALU.add),
                             reads=[("ps", ob)] + [("acc", x) for x in qgs], writes=[("acc", x) for x in qgs])
                    if (bi, r, b) == (2, 15, 0):
                        for qq in range(4):
                            cs = slice(512 * qq, 512 * qq + 512)
                            aq = [("acc", qq)]
                            P.op("act", lambda e, cs=cs: e.activation(out=acc[:, 1, cs], in_=acc[:, 1, cs], func=AF.Ln), reads=aq, writes=aq)
                            P.op("act", lambda e, cs=cs: e.activation(out=acc[:, 1, cs], in_=acc[:, 1, cs], func=AF.Exp, scale=-1.0), reads=aq, writes=aq)
                            P.op("dve", lambda e, cs=cs: e.tensor_tensor(out=QT[:, j, cs], in0=acc[:, 0, cs], in1=acc[:, 1, cs], op=ALU.mult),
                                 reads=aq, writes=[("QT", j, qq)])

                LA = 2
                NG = len(groups)
                for gi in range(NG + LA):
                    if gi < NG:
                        stage1(gi)
                    if gi - LA >= 0:
                        stage2(gi - LA)
                P.barrier()
                set_psum("std")

        if STAGE < 7:
            P.finish()
            return nc
        with ExitStack() as es4:
            sb4 = mk_sb(es4)
            alloc_work(sb4)
            GU = sb4("GU", [128, 4, NTS], BF)
            WD = sb4("WD", [128, NFC, D], BF)
            with ExitStack() as es7:
                sb7 = mk_sb(es7)
                SbS = sb7("SbS", [128, 2, 128]); PS = sb7("PS", [128, 2, 128], BF)
                osb = sb7("osb", [128, 2, 64])
                Vn = sb7("Vn", [4, 4, 512], BF)
                Kst = sb7("Kst", [128, 3, 512]); Vst = sb7("Vst", [128, 3, 512])
                Kc = sb7("Kc", [128, 7, 512], BF)
                Vc = [sb7("Vc%d" % i, [128, 7, 512], BF) for i in range(2)]
                KsT = sb7("KsT", [128, 4, 7, 128], BF)

                def issue_loads(b, vfull=True):
                    pk, pv = [], []
                    for u in range(3):
                        for r4 in range(4):
                            r0 = 512 * u + r4
                            pk.append((Kst[r4:128:4, u, :], ck_d[b, r0:r0 + 16 * 31 + 1:16, :]))
                            pv.append((Vst[r4:128:4, u, :], cv_d[b, r0:r0 + 16 * 31 + 1:16, :]))
                    P.dma_group("sp", pk, writes=["Kst"], stream="kst")
                    P.dma_group("sp", pv, writes=["Vst"], stream="vst")
                    P.dma_group("pool", [(Kc[:, 3 + k, :], ck_d[b, 1536 + 128 * k:1536 + 128 * (k + 1), :]) for k in range(4)],
                                writes=["KcF"], stream="kcf")
                    if vfull:
                        issue_vfull(b)

                def issue_vfull(b):
                    P.dma_group("pool", [(Vc[b % 2][:, 3 + k, :], cv_d[b, 1536 + 128 * k:1536 + 128 * (k + 1), :]) for k in range(4)],
                                writes=[("VcF", b % 2)], stream="vcf%d" % (b % 2))
                issue_loads(0)
                sv = load_wslab([(w_in, 1024, 512, 0)])
                P.lastw["wpre"] = ("d_wpre", P.dsem["wpre"], 16 * P.dcnt["wpre"], None)
                k = 0
                for c in range(4):
                    for (t0, n) in TG_OWN:
                        bank = k % 4
                        k += 1
                        pb = psb[bank]

                        def mm(e, c=c, t0=t0, n=n, pb=pb):
                            ins = None
                            for kc in range(8):
                                ins = e.matmul(pb[:, 0:n], lhsT=wpre[:, kc, c * 128:(c + 1) * 128], rhs=hT[:, kc, t0:t0 + n],
                                               start=(kc == 0), stop=(kc == 7))
                            return ins
                        P.op("pe", mm, reads=["wpre"], writes=[("ps", bank)])
                        P.op("act", lambda e, c=c, t0=t0, n=n, pb=pb: e.copy(out=GU[:, c, t0:t0 + n], in_=pb[:, 0:n]),
                             reads=[("ps", bank)], writes=[("GU", c, t0 // 512)])

                P.op("dve", lambda e: e.memset(SbS[:], 0.0), writes=["SbS"])
                for b in range(4):
                    def evn(pb, bn, b=b):
                        P.op("act", lambda e: e.copy(out=Vn[0:4, b, :], in_=pb[0:4, :]), reads=[bn], writes=[("Vn", b)])
                    tm_proj(sv, "hT", lambda kc, b=b: hT[:, kc, NT + 4 * b:NT + 4 * b + 4], 4, evn)
                obank = psb[5]

                def st_T(b):
                    i = b % 2
                    P.op("act", lambda e: e.copy(out=Kc[:, 0:3, :], in_=Kst[:, :, :]), reads=["Kst"], writes=["KcP"])
                    P.op("dve", lambda e, i=i: e.tensor_copy(out=Vc[i][:, 0:3, :], in_=Vst[:, :, :]), reads=["Vst"], writes=[("VcP", i)])
                    blocks = [(u, jj) for u in range(7) for jj in range(4)]
                    for c0 in range(0, 28, 8):
                        chunk = blocks[c0:c0 + 8]
                        ti = (c0 // 8) % 2
                        pt = ptb[ti]

                        def trs(e, chunk=chunk, pt=pt, i=i):
                            ins = None
                            for k, (u, jj) in enumerate(chunk):
                                ins = e.transpose(out=pt[:, k * 128:(k + 1) * 128], in_=Kc[:, u, jj * 128:(jj + 1) * 128], identity=ident[:, :])
                            return ins
                        P.op("pe", trs, reads=["KcP", "KcF", "ident"], writes=[("pt", ti)])
                        for k, (u, jj) in enumerate(chunk):
                            eng = "act" if k % 2 == 0 else "dve"
                            P.op(eng, (lambda e, k=k, u=u, jj=jj, pt=pt, i=i: (e.copy if False else e.tensor_copy)(out=KsT[:, jj, u, :], in_=pt[:, k * 128:(k + 1) * 128]))
                                 if eng == "dve" else (lambda e, k=k, u=u, jj=jj, pt=pt, i=i: e.copy(out=KsT[:, jj, u, :], in_=pt[:, k * 128:(k + 1) * 128])),
                                 reads=[("pt", ti)], writes=[("KsT", u, jj)])

                def st_S(b):
                    i = b % 2
                    qc = NT + 4 * b

                    def ssm(e, i=i, b=b, qc=qc):
                        ins = None
                        for hh in range(2):
                            rows = slice(64 * hh, 64 * hh + 64)
                            for jj in range(4):
                                for u in range(7):
                                    ins = e.matmul(psb[1 + hh][:, (u * 4 + jj) * 4:(u * 4 + jj) * 4 + 4], lhsT=KsT[rows, jj, u, :],
                                                   rhs=QT[rows, jj, qc:qc + 4], start=True, stop=True)
                                ins = e.matmul(psb[1 + hh][0:4, 112 + jj * 4:112 + jj * 4 + 4], lhsT=KTs[rows, jj, 4 * b:4 * b + 4],
                                               rhs=QT[rows, jj, qc:qc + 4], start=True, stop=True)
                        return ins
                    P.op("pe", ssm, reads=[("KsT", u, jj) for u in range(7) for jj in range(4)] + [("QT", jj, 4) for jj in range(4)] + [("KTs", jj) for jj in range(4)],
                         writes=[("ps", 1), ("ps", 2)])
                    for hh in range(2):
                        sbk = psb[1 + hh]
                        in0 = sbk[:, 0:112].rearrange("p (u j t) -> p u j t", u=7, j=4)
                        x = BS[:, :, :, hh:hh + 1]
                        in1 = bass.AP(x.tensor, x.offset, [x.ap[0], [32, 7], [2, 4], [8, 4]])
                        out = SbS[:, hh, 0:112].rearrange("p (u j t) -> p u j t", u=7, j=4)
                        P.op("dve", lambda e, in0=in0, in1=in1, out=out: e.tensor_tensor(out=out, in0=in0, in1=in1, op=ALU.add),
                             reads=[("ps", 1 + hh), "BS"], writes=[("SbS", hh)])
                        in0n = sbk[0:4, 112:128].rearrange("p (j t) -> p j t", j=4)
                        xn_ = BN[:, :, hh:hh + 1]
                        in1n = bass.AP(xn_.tensor, xn_.offset, [xn_.ap[0], [2, 4], [8, 4]])
                        outn = SbS[0:4, hh, 112:128].rearrange("p (j t) -> p j t", j=4)
                        P.op("dve", lambda e, in0n=in0n, in1n=in1n, outn=outn: e.tensor_tensor(out=outn, in0=in0n, in1=in1n, op=ALU.add),
                             reads=[("ps", 1 + hh), "BN"], writes=[("SbSn", hh)])
                    P.op("act", lambda e: e.activation(out=PS[:].rearrange("p a c -> p (a c)"), in_=SbS[:].rearrange("p a c -> p (a c)"), func=AF.Exp),
                         reads=[("SbS", 0), ("SbS", 1), ("SbSn", 0), ("SbSn", 1), "SbS"], writes=["PS"])

                def st_V(b):
                    i = b % 2

                    def spv(e, i=i, b=b):
                        ins = None
                        for hh in range(2):
                            rows = slice(64 * hh, 64 * hh + 64)
                            for jj in range(4):
                                h = 2 * jj + hh
                                for part in range(2):
                                    oc = (part * 4 + jj) * 16 + 4 * b
                                    for u in range(7):
                                        lhsT = Vc[i][:, u, 64 * h:64 * h + 64] if part == 0 else ones[:, :]
                                        e.matmul(obank[rows, oc:oc + 4], lhsT=lhsT, rhs=PS[:, hh, (u * 4 + jj) * 4:(u * 4 + jj) * 4 + 4],
                                                 start=(u == 0), stop=False)
                                    lhsT = Vn[0:4, b, 64 * h:64 * h + 64] if part == 0 else ones[0:4, :]
                                    ins = e.matmul(obank[rows, oc:oc + 4], lhsT=lhsT, rhs=PS[0:4, hh, 112 + jj * 4:112 + jj * 4 + 4],
                                                   start=False, stop=True)
                        return ins
                    P.op("pe", spv, reads=["PS", ("VcP", i), ("VcF", i), ("Vn", b), "ones"], writes=[("ps", 5)])
                st_T(0)
                issue_loads(1)
                st_S(0)
                for b in range(1, 4):
                    st_T(b)
                    if b + 1 < 4:
                        issue_loads(b + 1, vfull=False)
                    st_V(b - 1)
                    if b + 1 < 4:
                        issue_vfull(b + 1)
                    if b == 3:
                        svg = load_wslab([(w_in, 2048, 512, 0)])
                        wo0 = load_wslab([(w_out, 0, 512, 0)])
                        P.dma_group("pool", [(wpre[:, kc, :], w_out[kc * 128:(kc + 1) * 128, 512:1024]) for kc in range(8)],
                                    writes=["wpre"], stream="wpre")
                    st_S(b)
                st_V(3)
                P.op("dve", lambda e: e.tensor_copy(out=osb[:].rearrange("p a c -> p (a c)"), in_=obank[:, 0:128]), reads=[("ps", 5)], writes=["osb"])
                P.op("dve", lambda e: e.reciprocal(out=osb[:, 1, :], in_=osb[:, 1, :]), reads=["osb"], writes=["osb"])
                P.op("dve", lambda e: e.tensor_tensor(out=QT[:, :, NT:NTS], in0=osb[:, 0, :].rearrange("p (j c) -> p j c", j=4),
                                                      in1=osb[:, 1, :].rearrange("p (j c) -> p j c", j=4), op=ALU.mult),
                     reads=["osb"], writes=[("QT", jj, 4) for jj in range(4)])
                P.barrier()

            with ExitStack() as es5:
                sb5 = mk_sb(es5)
                lng = sb5("lng", [128, 512]); lnb = sb5("lnb", [128, 512])
                lnw = [sb5("lnw%d" % i, [128, 512]) for i in range(3)]
                vnb = [sb5("vnb%d" % i, [128, 512], BF) for i in range(3)]
                wtmp = sb5("wtmp", [128, 8, 128]); tri = sb5("tri", [128, 128]); WmT = sb5("WmT", [128, 8, 128], BF)
                bsp = sb5("bsp", [128, 4, 128]); gtmp = [sb5("gtmp%d" % i, [128, 512]) for i in range(2)]
                bdf = sb5("bdf", [NS, 8, NS]); bdm = sb5("bdm", [NS, NS]); BD = sb5("BD", [NS, 8, NS], BF)
                x2t = [sb5("x2t%d" % i, [128, D]) for i in range(2)]
                st2 = sb5("st2", [128, 24])
                P.dma("sp", lng[:], lng_d, writes=["lng"], stream="c0")
                P.dma("sp", lnb[:], lnb_d, writes=["lnb"], stream="c0")
                P.dma("sp", tri[:], tri_d, writes=["tri"], stream="c0")
                P.dma("sp", bdm[:], bdm_d, writes=["bdm"], stream="c0")
                P.dma("sp", bsp[:], bsp_d.rearrange("j p t -> p j t"), writes=["bsp"], stream="c0")
                P.dma("sp", wtmp[:], sgw_d.rearrange("g s t -> s g t"), writes=["wtmp"], stream="c0")
                trb = bass.AP(tri[:].tensor, tri[:].offset, [tri[:].ap[0], [0, 8], [1, 128]])
                P.op("dve", lambda e: e.tensor_tensor(out=WmT[:], in0=wtmp[:], in1=trb, op=ALU.mult), reads=["wtmp", "tri"], writes=["WmT"])
                P.op("dve", lambda e: e.memset(bdf[:], 0.0), writes=["bdf"])
                with nc.allow_non_contiguous_dma(reason="tiny 4x4 sgu blocks"):
                    P.dma_group("sp", [(bdf[4 * b:4 * b + 4, :, 4 * b:4 * b + 4], sgw_d[:, 0:4, 0:4].rearrange("g s t -> s g t")) for b in range(4)],
                                reads=[], writes=["bdf"], stream="c0")
                bdb = bass.AP(bdm[:].tensor, bdm[:].offset, [bdm[:].ap[0], [0, 8], [1, NS]])
                P.op("dve", lambda e: e.tensor_tensor(out=BD[:], in0=bdf[:], in1=bdb, op=ALU.mult), reads=["bdf", "bdm"], writes=["BD"])

                lnc = [0]

                def ln_rows(pb, bn, rows, out_ap, outname):
                    i = lnc[0] % 2
                    lnc[0] += 1
                    o = 8 * i
                    L = lnw[i]
                    P.op("act", lambda e: e.activation(out=L[:rows, :], in_=pb[:rows, :], func=AF.Identity, accum_out=st2[:rows, o:o + 1]),
                         reads=[bn], writes=[("lnw", i), ("st_sum", i)])
                    P.op("act", lambda e: e.activation(out=W.xn[i][:rows, 0:512], in_=pb[:rows, :], func=AF.Square, accum_out=st2[:rows, o + 1:o + 2]),
                         reads=[bn], writes=[("xn", i), ("st_sq", i)])
                    P.op("dve", lambda e: e.tensor_scalar(out=st2[:rows, o + 2:o + 3], in0=st2[:rows, o:o + 1], scalar1=1.0 / 512, scalar2=None, op0=ALU.mult),
                         reads=[("st_sum", i)], writes=[("st_mean", i)])
                    P.op("dve", lambda e: e.tensor_tensor(out=st2[:rows, o + 3:o + 4], in0=st2[:rows, o + 2:o + 3], in1=st2[:rows, o + 2:o + 3], op=ALU.mult),
                         reads=[("st_mean", i)], writes=[("st_m2", i)])
                    P.op("dve", lambda e: e.scalar_tensor_tensor(out=st2[:rows, o + 4:o + 5], in0=st2[:rows, o + 1:o + 2], scalar=1.0 / 512,
                                                                 in1=st2[:rows, o + 3:o + 4], op0=ALU.mult, op1=ALU.subtract),
                         reads=[("st_sq", i), ("st_m2", i)], writes=[("st_var", i)])
                    P.op("act", lambda e: e.activation(out=st2[:rows, o + 5:o + 6], in_=st2[:rows, o + 4:o + 5], func=AF.Sqrt, scale=1.0, bias=epst[:rows, :]),
                         reads=[("st_var", i), "eps"], writes=[("st_sd", i)])
                    P.op("dve", lambda e: e.reciprocal(out=st2[:rows, o + 6:o + 7], in_=st2[:rows, o + 5:o + 6]), reads=[("st_sd", i)], writes=[("st_rstd", i)])
                    P.op("dve", lambda e: e.tensor_scalar(out=L[:rows, :], in0=L[:rows, :], scalar1=st2[:rows, o + 2:o + 3], scalar2=st2[:rows, o + 6:o + 7],
                                                          op0=ALU.subtract, op1=ALU.mult),
                         reads=[("lnw", i), ("st_mean", i), ("st_rstd", i)], writes=[("lnw", i)])
                    P.op("dve", lambda e: e.tensor_tensor(out=L[:rows, :], in0=L[:rows, :], in1=lng[:rows, :], op=ALU.mult),
                         reads=[("lnw", i), "lng"], writes=[("lnw", i)])
                    P.op("dve", lambda e: e.tensor_tensor(out=out_ap, in0=L[:rows, :], in1=lnb[:rows, :], op=ALU.add),
                         reads=[("lnw", i), "lnb"], writes=[outname])
                    return i

                wx = wpre
                vslot = {}

                ljunk = [sb5("ljunk%d" % i, [128, 512], BF) for i in range(3)]
                vslot = {}
                pbank = {}

                def cd_vg(t):
                    rows = 128 if t < 16 else NS
                    bank = (4, 5, 1)[t % 3]
                    pb = psb[bank]
                    pbank[t] = (pb, ("ps", bank))

                    def mm(e):
                        ins = None
                        for kc in range(8):
                            ins = e.matmul(pb[:rows, :], lhsT=hT[:, kc, t * 128:t * 128 + rows], rhs=W.wsl[svg][:, kc, :], start=(kc == 0), stop=(kc == 7))
                        return ins
                    P.op("pe", mm, reads=[("wsl", svg), ("hTt", t)], writes=[("ps", bank)])

                def cd_ln(t):
                    rows = 128 if t < 16 else NS
                    pb, bn = pbank[t]
                    i = t % 3
                    vslot[t] = i
                    o = 8 * i
                    L = lnw[i]
                    P.op("act", lambda e: e.activation(out=L[:rows, :], in_=pb[:rows, :], func=AF.Identity, accum_out=st2[:rows, o:o + 1]),
                         reads=[bn], writes=[("lnw", i), ("st_sum", i)])
                    P.op("act", lambda e: e.activation(out=ljunk[i][:rows, :], in_=pb[:rows, :], func=AF.Square, accum_out=st2[:rows, o + 1:o + 2]),
                         reads=[bn], writes=[("ljunk", i), ("st_sq", i)])
                    P.op("dve", lambda e: e.tensor_scalar(out=st2[:rows, o + 2:o + 3], in0=st2[:rows, o:o + 1], scalar1=1.0 / 512, scalar2=None, op0=ALU.mult),
                         reads=[("st_sum", i)], writes=[("st_mean", i)])
                    P.op("dve", lambda e: e.tensor_tensor(out=st2[:rows, o + 3:o + 4], in0=st2[:rows, o + 2:o + 3], in1=st2[:rows, o + 2:o + 3], op=ALU.mult),
                         reads=[("st_mean", i)], writes=[("st_m2", i)])
                    P.op("dve", lambda e: e.scalar_tensor_tensor(out=st2[:rows, o + 4:o + 5], in0=st2[:rows, o + 1:o + 2], scalar=1.0 / 512,
                                                                 in1=st2[:rows, o + 3:o + 4], op0=ALU.mult, op1=ALU.subtract),
                         reads=[("st_sq", i), ("st_m2", i)], writes=[("st_var", i)])
                    P.op("act", lambda e: e.activation(out=st2[:rows, o + 5:o + 6], in_=st2[:rows, o + 4:o + 5], func=AF.Ln, scale=1.0, bias=epst[:rows, :]),
                         reads=[("st_var", i), "eps"], writes=[("st_sd", i)])
                    P.op("act", lambda e: e.activation(out=st2[:rows, o + 6:o + 7], in_=st2[:rows, o + 5:o + 6], func=AF.Exp, scale=-0.5),
                         reads=[("st_sd", i)], writes=[("st_rstd", i)])
                    P.op("dve", lambda e: e.tensor_scalar(out=L[:rows, :], in0=L[:rows, :], scalar1=st2[:rows, o + 2:o + 3], scalar2=st2[:rows, o + 6:o + 7],
                                                          op0=ALU.subtract, op1=ALU.mult),
                         reads=[("lnw", i), ("st_mean", i), ("st_rstd", i)], writes=[("lnw", i)])
                    P.op("pool", lambda e: e.tensor_tensor(out=L[:rows, :], in0=L[:rows, :], in1=lng[:rows, :], op=ALU.mult),
                         reads=[("lnw", i), "lng"], writes=[("lnw", i)])
                    if t < 16:
                        P.op("pool", lambda e: e.tensor_tensor(out=vnb[i][:rows, :], in0=L[:rows, :], in1=lnb[:rows, :], op=ALU.add),
                             reads=[("lnw", i), "lnb"], writes=[("vnb", i)])
                    else:
                        P.op("pool", lambda e: e.tensor_tensor(out=L[:rows, :], in0=L[:rows, :], in1=lnb[:rows, :], op=ALU.add),
                             reads=[("lnw", i), "lnb"], writes=[("lnw", i)])
                        P.dma("sp", svo_d, L[:rows, :], reads=[("lnw", i)], stream="svo")
                        P.op("pool", lambda e: e.tensor_copy(out=vnb[i][:rows, :], in_=L[:rows, :]), reads=[("lnw", i)], writes=[("vnb", i)])

                def cd_sgu(t):
                    rows = 128 if t < 16 else NS
                    ncol = rows
                    vi = vslot[t]
                    gi = t % 2
                    gb_ = psb[0]

                    def gm(e):
                        ins = None
                        for g8 in range(8):
                            jj, hh = g8 // 2, g8 % 2
                            rhs = WmT[:, g8, :] if t < 16 else BD[:, g8, :]
                            ins = e.matmul(gb_[64 * hh:64 * hh + 64, jj * 128:jj * 128 + ncol], lhsT=vnb[vi][:rows, 64 * g8:64 * g8 + 64],
                                           rhs=rhs, start=True, stop=True)
                        return ins
                    P.op("pe", gm, reads=[("vnb", vi), "WmT", "BD"], writes=[("ps", 0)])
                    gv = gb_[:, :].rearrange("p (j q) -> p j q", j=4)[:, :, 0:ncol]
                    gt_ = gtmp[gi][:, :].rearrange("p (j q) -> p j q", j=4)[:, :, 0:ncol]
                    if t < 16:
                        bv = bsp[:, :, :]
                    else:
                        x = bsp[:, :, 0:4]
                        bv = bass.AP(x.tensor, x.offset, [x.ap[0], x.ap[1], [0, 4], [1, 4]])
                        gv = gv.rearrange("p j (b q) -> p j b q", b=4)
                        gt_ = gt_.rearrange("p j (b q) -> p j b q", b=4)
                    P.op("dve", lambda e: e.tensor_tensor(out=gt_, in0=gv, in1=bv, op=ALU.add), reads=[("ps", 0), "bsp"], writes=[("gtmp", gi)])
                    guv = GU[:, :, t * 128:t * 128 + ncol]
                    g2 = gtmp[gi][:, :].rearrange("p (j q) -> p j q", j=4)[:, :, 0:ncol]
                    P.op("dve", lambda e: e.tensor_tensor(out=guv, in0=g2, in1=guv, op=ALU.mult),
                         reads=[("gtmp", gi)] + [("GU", c, t // 4) for c in range(4)], writes=[("GUt", t)])

                dslot = {}

                def cd_oproj(t):
                    rows = 128 if t < 16 else NS
                    s = tcount[0] % 2
                    tcount[0] += 1
                    dslot[t] = s
                    P.dma("sp", W.xs[s][:rows, :], (xo[t * 128:(t + 1) * 128, :] if t < 16 else xsm), writes=[("xs", s)], stream="x%d" % s)
                    for half in range(2):
                        bank = 2 + half
                        pb = psb[bank]
                        wsrc = W.wsl[wo0] if half == 0 else wx

                        def om(e, pb=pb, wsrc=wsrc):
                            ins = None
                            for kc in range(8):
                                src = QT if kc < 4 else GU
                                ins = e.matmul(pb[:rows, :], lhsT=src[:, kc % 4, t * 128:t * 128 + rows], rhs=wsrc[:, kc, :],
                                               start=(kc == 0), stop=(kc == 7))
                            return ins
                        P.op("pe", om, reads=[("wsl", wo0), "wx", ("GUt", t)], writes=[("ps", bank)])

                def cd_resid(t):
                    rows = 128 if t < 16 else NS
                    s = dslot[t]
                    for half in range(2):
                        bank = 2 + half
                        pb = psb[bank]
                        P.op("dve", lambda e, pb=pb, half=half: e.tensor_tensor(
                            out=x2t[s][:rows, half * 512:(half + 1) * 512], in0=pb[:rows, :], in1=W.xs[s][:rows, half * 512:(half + 1) * 512], op=ALU.add),
                            reads=[("ps", bank), ("xs", s)], writes=[("x2t", s, half)])
                    P.dma("sp", x2s[t * 128:t * 128 + rows, :], x2t[s][:rows, :], reads=[("x2t", s, 0), ("x2t", s, 1)], writes=[("x2s", t)], stream="x2o%d" % s)
                    norm_B(x2t[s], ("x2t", s, 1), rows, s)

                def cd_tr(t):
                    rows = 128 if t < 16 else NS
                    norm_C(rows, g2t, "g2t", hT, "hTt", t * 128, dslot[t], gran=128)

                cd_vg(0)
                cd_ln(0)
                cd_vg(1)
                cd_ln(1)
                cd_sgu(0)
                for i in range(18):
                    if i < 11:
                        P.dma_group("pool", [(WD[:, fc, :], w_down[fc * 128:(fc + 1) * 128, :]) for fc in (2 * i, 2 * i + 1)],
                                    writes=[("WD", i)], stream="wd")
                    if i + 2 < 17:
                        cd_vg(i + 2)
                    if i + 1 < 17:
                        cd_sgu(i + 1)
                    if i < 17:
                        cd_oproj(i)
                    if i + 2 < 17:
                        cd_ln(i + 2)
                    if i < 17:
                        cd_resid(i)
                    if 0 <= i - 1 < 17:
                        cd_tr(i - 1)
                s_ffn0 = load_wslab([(w_gate, 0, 256, 0), (w_up, 0, 256, 256)])
                P.barrier()

            if STAGE < 9:
                P.finish()
                return nc
            with ExitStack() as es6:
                sb6 = mk_sb(es6)
                HT = 1024 + NS
                AT = sb6("AT", [128, NFC, HT], BF)
                tmpf = [sb6("tmpf%d" % i, [128, 512]) for i in range(2)]
                yst = W.xs
                gfb = sb6("gfb", [128, D])
                P.dma("sp", gfb[:], gf_d, writes=["gfb"], stream="c0")
                issue_copies(99)
                ec = [0]
                for hf in range(2):
                    base = 1024 * hf
                    tgs = [(0, 512), (512, 512)] + ([(1024, NS)] if hf == 1 else [])
                    for fs in range(11):
                        if hf == 0 and fs == 0:
                            s = s_ffn0
                            P.lastw[("wsl", s)] = None
                        else:
                            s = load_wslab([(w_gate, fs * 256, 256, 0), (w_up, fs * 256, 256, 256)])
                        for fcl in range(2):
                            fc = 2 * fs + fcl
                            for (l0, n) in tgs:
                                i = ec[0] % 2
                                ec[0] += 1
                                ba, bb = psb[i], psb[2 + i]
                                t0 = base + l0

                                def gm(e, ba=ba, t0=t0, n=n, fcl=fcl, s=s):
                                    ins = None
                                    for kc in range(8):
                                        ins = e.matmul(ba[:, 0:n], lhsT=W.wsl[s][:, kc, fcl * 128:(fcl + 1) * 128], rhs=hT[:, kc, t0:t0 + n],
                                                       start=(kc == 0), stop=(kc == 7))
                                    return ins

                                def um(e, bb=bb, t0=t0, n=n, fcl=fcl, s=s):
                                    ins = None
                                    for kc in range(8):
                                        ins = e.matmul(bb[:, 0:n], lhsT=W.wsl[s][:, kc, 256 + fcl * 128:256 + (fcl + 1) * 128], rhs=hT[:, kc, t0:t0 + n],
                                                       start=(kc == 0), stop=(kc == 7))
                                    return ins
                                P.op("pe", gm, reads=[("wsl", s), ("hT", t0 // 512)], writes=[("ps", i)])
                                P.op("pe", um, reads=[("wsl", s), ("hT", t0 // 512)], writes=[("ps", 2 + i)])
                                P.op("act", lambda e, i=i, ba=ba, n=n: e.activation(out=tmpf[i][:, 0:n], in_=ba[:, 0:n], func=AF.Silu),
                                     reads=[("ps", i)], writes=[("tmpf", i)])
                                P.op("dve", lambda e, i=i, bb=bb, n=n, fc=fc, l0=l0: e.tensor_tensor(out=AT[:, fc, l0:l0 + n], in0=bb[:, 0:n], in1=tmpf[i][:, 0:n], op=ALU.mult),
                                     reads=[("ps", 2 + i), ("tmpf", i)], writes=[("AT", l0 // 512)])
                    tiles = list(range(8 * hf, 8 * hf + 8)) + ([16] if hf == 1 else [])
                    for t in tiles:
                        rows = 128 if t < 16 else NS
                        l0 = t * 128 - base
                        s = tcount[0] % 2
                        tcount[0] += 1
                        P.dma("sp", W.xs[s][:rows, :], x2s[t * 128:t * 128 + rows, :], reads=[("x2s", t)],
                              writes=[("xs", s), ("yst", s, 0), ("yst", s, 1)], stream="x%d" % s)
                        for half in range(2):
                            bank = 4 + half
                            pb = psb[bank]

                            def dm(e, pb=pb, half=half, l0=l0, rows=rows):
                                ins = None
                                for fc in range(NFC):
                                    ins = e.matmul(pb[:rows, :], lhsT=AT[:, fc, l0:l0 + rows], rhs=WD[:, fc, half * 512:(half + 1) * 512],
                                                   start=(fc == 0), stop=(fc == NFC - 1))
                                return ins
                            P.op("pe", dm, reads=["WD", ("AT", l0 // 512)], writes=[("ps", bank)])
                            P.op("dve", lambda e, pb=pb, half=half, s=s, rows=rows: e.tensor_tensor(
                                out=yst[s][:rows, half * 512:(half + 1) * 512], in0=pb[:rows, :], in1=W.xs[s][:rows, half * 512:(half + 1) * 512], op=ALU.add),
                                reads=[("ps", bank), ("xs", s)], writes=[("yst", s, half)])
                        rstd_of(yst[s], ("yst", s, 1), rows, s)
                        P.op("dve", lambda e, s=s, rows=rows: e.scalar_tensor_tensor(out=yst[s][:rows, :], in0=yst[s][:rows, :], scalar=ss[:rows, 4 + s:5 + s],
                                                                                   in1=gfb[:rows, :], op0=ALU.mult, op1=ALU.mult),
                             reads=[("yst", s, 0), ("yst", s, 1), ("rstd", s), "gfb"], writes=[("yst", s, 0), ("yst", s, 1)])
                        dst = y_d[t * 128:(t + 1) * 128, :] if t < 16 else ys_d
                        P.dma("sp", dst, yst[s][:rows, :], reads=[("yst", s, 0), ("yst", s, 1)], stream="yo%d" % s)

        P.finish()
        esp[0].close()
    return nc


_CACHE = {}


def _get_program():
    if "nc" not in _CACHE:
        _CACHE["nc"] = build_program()
    return _CACHE["nc"]


def kernel(x_prompt, x_sample, cache_k_win, cache_v_win, norm1_g, w_in, sgu_ln_g, sgu_ln_b, sgu_w, sgu_b,
           w_out, norm2_g, w_gate, w_up, w_down, rel_bias, final_g):
    f = lambda a: np.ascontiguousarray(np.asarray(a, dtype=np.float32))
    xp = f(x_prompt); xsa = f(x_sample)
    ck = f(cache_k_win)[0].reshape(32, 2048, 512); cv = f(cache_v_win)[0].reshape(32, 2048, 512)
    common = {
        "w_in": f(w_in)[0], "w_out": f(w_out)[0], "w_gate": f(w_gate)[0], "w_up": f(w_up)[0], "w_down": f(w_down)[0],
        "g1t": f(f(norm1_g)[0].reshape(8, 128).T), "g2t": f(f(norm2_g)[0].reshape(8, 128).T),
        "gfb": f(np.broadcast_to(f(final_g)[None, :], (128, D))),
        "lng": f(np.broadcast_to(f(sgu_ln_g)[0][None, :], (128, 512))),
        "lnb": f(np.broadcast_to(f(sgu_ln_b)[0][None, :], (128, 512))),
        "sgwT": f(f(sgu_w)[0].transpose(0, 2, 1)),
        "tri": f(np.triu(np.ones((128, 128), np.float32))),
        "bsp": f(np.repeat(f(sgu_b)[0].reshape(4, 2, 128), 64, axis=1)),
        "rb33": f(np.concatenate([f(rel_bias), np.full((1, 8), NEG, np.float32)], axis=0)),
        "oh": _onehot_consts(), "ident": np.eye(128, dtype=np.float32),
        "bdm": f(np.kron(np.eye(4, dtype=np.float32), np.triu(np.ones((4, 4), np.float32)))),
        "rb34": f(np.concatenate([f(rel_bias), np.full((1, 8), NEG, np.float32), np.ones((1, 8), np.float32)], axis=0)),
        "ohs": _sample_consts()[0], "ohn": _sample_consts()[1],
    }
    in_maps = []
    for c in range(NCORES):
        b, half = c // 2, c % 2
        own = xp[b, half * NT:(half + 1) * NT]
        hist = xp[b, 0:NT]
        m = dict(common)
        m.update({
            "xo": f(own), "xh": f(hist), "xsm": f(xsa[4 * c:4 * c + 4].reshape(NS, D)),
            "hm": np.full((128, 1), 0.0 if half == 1 else NEG, np.float32),
            "ck": f(ck[4 * c:4 * c + 4]), "cv": f(cv[4 * c:4 * c + 4]),
        })
        in_maps.append(m)
    if _CACHE.get("debug_core") is not None:
        c = _CACHE["debug_core"]
        return run_bass_kernel_spmd(_get_program(), [in_maps[c]], core_ids=[0]).results[0]
    nc = _get_program()
    res = run_bass_kernel_spmd(nc, in_maps, core_ids=list(range(NCORES)))
    R = res.results
    y = np.zeros((4, 4096, D), np.float32)
    kp = np.zeros((1, 4, 2048, 8, 64), np.float32); vp = np.zeros_like(kp)
    for c in range(NCORES):
        b, half = c // 2, c % 2
        y[b, half * NT:(half + 1) * NT] = R[c]["y"]
        if half == 1:
            kp[0, b] = R[c]["ko"].reshape(2048, 8, 64)
            vp[0, b] = R[c]["vo"].reshape(2048, 8, 64)
    ysm = np.concatenate([R[c]["ys"].reshape(4, 4, D) for c in range(NCORES)], axis=0)
    ks = np.concatenate([R[c]["kso"] for c in range(NCORES)], axis=0).reshape(1, 32, 2048, 8, 64)
    vs = np.concatenate([R[c]["vso"] for c in range(NCORES)], axis=0).reshape(1, 32, 2048, 8, 64)
    sv = np.concatenate([R[c]["svo"].reshape(4, 4, 512) for c in range(NCORES)], axis=0).reshape(1, 32, 4, 512)
    return (y, ysm, kp, vp, ks, vs, sv)
```

```python
import numpy as np
from contextlib import ExitStack
import concourse.bass as bass
import concourse.mybir as mybir
from concourse.bass_utils import run_bass_kernel_spmd

F32 = mybir.dt.float32
BF = mybir.dt.bfloat16
AF = mybir.ActivationFunctionType
ALU = mybir.AluOpType

NCORES = 8
D = 1024
NT = 2048
NS = 16
NTS = NT + NS
DFF = 2816
NFC = 22
EPS = 1e-6
NEG = -30000.0
BRANCH_D = (1, 4, 16)
import os
STAGE = int(os.environ.get('KSTAGE', '99'))


class Prog:
    def __init__(self, nc, es):
        self.nc = nc
        self.es = es
        self.eng = {"pe": nc.tensor, "act": nc.scalar, "dve": nc.vector, "pool": nc.gpsimd, "sp": nc.sync}
        self.sem = {k: es.enter_context(nc.semaphore("s_" + k)) for k in self.eng}
        self.cnt = {k: 0 for k in self.eng}
        self.waited = {k: {} for k in self.eng}
        self.lastw = {}
        self.readers = {}
        self.dsem = {}
        self.dcnt = {}

    def _deps(self, e, reads, writes):
        toks = []
        for r in reads:
            t = self.lastw.get(r)
            if t is not None:
                toks.append(t)
            if isinstance(r, tuple) and r[0] in ("ps", "pt"):
                toks.extend(tk for tk in self.readers.get(r, ()) if tk[3] != e)
        for w in writes:
            t = self.lastw.get(w)
            if t is not None:
                toks.append(t)
            toks.extend(self.readers.get(w, ()))
        for (key, handle, val, prod) in toks:
            if prod == "pe" and e == "pe":
                continue
            if prod is None:
                val = max(val, 16 * self.dcnt[key[2:]])
            if self.waited[e].get(key, 0) >= val:
                continue
            self.eng[e].wait_ge(handle, val)
            self.waited[e][key] = val

    def _commit(self, tok, reads, writes):
        for w in writes:
            self.lastw[w] = tok
            self.readers[w] = []
        for r in reads:
            if r in writes:
                continue
            self.readers.setdefault(r, []).append(tok)

    def op(self, e, fn, reads=(), writes=()):
        self._deps(e, reads, writes)
        ins = fn(self.eng[e])
        self.cnt[e] += 1
        ins.then_inc(self.sem[e], 1)
        self._commit((e, self.sem[e], self.cnt[e], e), reads, writes)

    def dma(self, q, out, in_, reads=(), writes=(), stream=None, **kw):
        if stream not in self.dsem:
            self.dsem[stream] = self.es.enter_context(self.nc.semaphore("d_" + str(stream)))
            self.dcnt[stream] = 0
        self._deps(q, reads, writes)
        ins = self.eng[q].dma_start(out=out, in_=in_, **kw)
        self.dcnt[stream] += 1
        ins.then_inc(self.dsem[stream], 16)
        self._commit(("d_" + str(stream), self.dsem[stream], 16 * self.dcnt[stream], None), reads, writes)

    def dma_group(self, q, pairs, reads=(), writes=(), stream=None, **kw):
        if stream not in self.dsem:
            self.dsem[stream] = self.es.enter_context(self.nc.semaphore("d_" + str(stream)))
            self.dcnt[stream] = 0
        self._deps(q, reads, writes)
        for (out, in_) in pairs:
            ins = self.eng[q].dma_start(out=out, in_=in_, **kw)
            self.dcnt[stream] += 1
            ins.then_inc(self.dsem[stream], 16)
        self._commit(("d_" + str(stream), self.dsem[stream], 16 * self.dcnt[stream], None), reads, writes)

    def barrier(self):
        for e in self.eng:
            for o in self.eng:
                if o != e and self.cnt[o] > self.waited[e].get(o, 0):
                    self.eng[e].wait_ge(self.sem[o], self.cnt[o])
                    self.waited[e][o] = self.cnt[o]
            for s, h in self.dsem.items():
                if s in ("cpk", "cpv"):
                    continue
                v = 16 * self.dcnt[s]
                if v > self.waited[e].get("d_" + str(s), 0):
                    self.eng[e].wait_ge(h, v)
                    self.waited[e]["d_" + str(s)] = v
        self.lastw.clear()
        self.readers.clear()

    def finish(self):
        for s, h in self.dsem.items():
            v = 16 * self.dcnt[s]
            if v > self.waited["sp"].get("d_" + str(s), 0):
                self.nc.sync.wait_ge(h, v)


def _rel_bucket_np(dist):
    dist = np.asarray(dist)
    df = np.maximum(dist, 1).astype(np.float32)
    large = 16 + (np.log(df / np.float32(16)) / np.float32(np.log(2048 / 16)) * np.float32(16)).astype(np.int32)
    large = np.minimum(large, 31)
    return np.where(dist < 16, dist, large)


def _onehot_consts():
    oh = np.zeros((3, 33, 384), np.float32)
    for bi, d in enumerate(BRANCH_D):
        for i in range(384):
            j = 255 - i
            if 0 <= j <= 128:
                oh[bi, int(_rel_bucket_np(j * d)), i] = 1.0
            else:
                oh[bi, 32, i] = 1.0
    return oh


def _nmult(delta):
    return int(delta <= 128) + int(delta % 4 == 0 and delta <= 512) + int(delta % 16 == 0 and delta <= 2048)


def _sample_row(u, p):
    return 16 * (32 * u + p // 4) + p % 4 if u < 3 else 1536 + 128 * (u - 3) + p


def _sample_consts():
    ohs = np.zeros((34, 28, 128), np.float32)
    for u in range(7):
        for t in range(4):
            for p in range(128):
                delta = 2048 + t - _sample_row(u, p)
                n = _nmult(delta)
                if n == 0:
                    ohs[32, u * 4 + t, p] = 1.0
                else:
                    ohs[int(_rel_bucket_np(delta)), u * 4 + t, p] = 1.0
                    ohs[33, u * 4 + t, p] = np.float32(np.log(n))
    ohn = np.zeros((34, 4, 4), np.float32)
    for t in range(4):
        for tp in range(4):
            delta = t - tp
            if delta < 0:
                ohn[32, t, tp] = 1.0
            else:
                ohn[int(_rel_bucket_np(delta)), t, tp] = 1.0
                ohn[33, t, tp] = np.float32(np.log(_nmult(delta)))
    return ohs, ohn


def build_program():
    nc = bass.Bass("TRN2", target_bir_lowering=False)
    dt = lambda n, s, kind="ExternalInput": nc.dram_tensor(n, s, F32, kind=kind).ap()
    xo = dt("xo", [NT, D]); xh = dt("xh", [NT, D]); xsm = dt("xsm", [NS, D])
    hm_d = dt("hm", [128, 1])
    w_in = dt("w_in", [D, 2560]); w_out = dt("w_out", [D, D])
    w_gate = dt("w_gate", [D, DFF]); w_up = dt("w_up", [D, DFF]); w_down = dt("w_down", [DFF, D])
    g1t_d = dt("g1t", [128, 8]); g2t_d = dt("g2t", [128, 8]); gf_d = dt("gfb", [128, D])
    lng_d = dt("lng", [128, 512]); lnb_d = dt("lnb", [128, 512])
    sgw_d = dt("sgwT", [8, 128, 128]); tri_d = dt("tri", [128, 128]); bsp_d = dt("bsp", [4, 128, 128])
    bdm_d = dt("bdm", [NS, NS])
    rb34_d = dt("rb34", [34, 8]); ohs_d = dt("ohs", [34, 28, 128]); ohn_d = dt("ohn", [34, 4, 4])
    rb_d = dt("rb33", [33, 8]); oh_d = dt("oh", [3, 33, 384]); id_d = dt("ident", [128, 128])
    ck_d = dt("ck", [4, 2048, 512]); cv_d = dt("cv", [4, 2048, 512])
    y_d = dt("y", [NT, D], "ExternalOutput"); ys_d = dt("ys", [NS, D], "ExternalOutput")
    ko_d = dt("ko", [NT, 512], "ExternalOutput"); vo_d = dt("vo", [NT, 512], "ExternalOutput")
    kso_d = dt("kso", [4, 2048, 512], "ExternalOutput"); vso_d = dt("vso", [4, 2048, 512], "ExternalOutput")
    svo_d = dt("svo", [NS, 512], "ExternalOutput")
    x2s = dt("x2s", [NTS, D], "Internal")
    esc = dt("esc", [3, 8, 384], "Internal")

    with ExitStack() as es:
        P = Prog(nc, es)

        uid = [0]

        def mk_sb(st):
            def f(n, s, d=F32):
                uid[0] += 1
                return st.enter_context(nc.sbuf_tensor("sb%d_%s" % (uid[0], n), s, d))
            return f
        sb = mk_sb(es)
        psb, ptb, psA = [], [], [None]
        esp = [ExitStack()]
        pcount = [0]

        def set_psum(mode):
            esp[0].close()
            esp[0] = ExitStack()
            pcount[0] += 1
            psb[:] = []
            ptb[:] = []
            if mode == "std":
                psb.extend(esp[0].enter_context(nc.psum_tensor("ps%d_%d" % (pcount[0], i), [128, 512], F32)) for i in range(6))
                ptb.extend(esp[0].enter_context(nc.psum_tensor("pt%d_%d" % (pcount[0], i), [128, 1024], BF)) for i in range(2))
            else:
                psA[0] = esp[0].enter_context(nc.psum_tensor("psA%d" % pcount[0], [128, 8, 512], F32))
        set_psum("std")

        ident_f = sb("ident_f", [128, 128]); ident = sb("ident", [128, 128], BF)
        g1t = sb("g1t", [128, 8]); g2t = sb("g2t", [128, 8]); hm = sb("hm", [128, 1])
        epst = sb("epst", [128, 1]); ss = sb("ss", [128, 8]); ones = sb("ones", [128, 64], BF)
        P.dma("sp", ident_f[:], id_d, writes=["ident_f"], stream="c0")
        P.dma("sp", g1t[:], g1t_d, writes=["g1t"], stream="c0")
        P.dma("sp", g2t[:], g2t_d, writes=["g2t"], stream="c0")
        P.dma("sp", hm[:], hm_d, writes=["hm"], stream="c0")
        P.op("dve", lambda e: e.tensor_copy(out=ident[:], in_=ident_f[:]), reads=["ident_f"], writes=["ident"])
        P.op("dve", lambda e: e.memset(epst[:], EPS), writes=["eps"])
        P.op("dve", lambda e: e.memset(ones[:], 1.0), writes=["ones"])

        cp_pending = []
        for b in range(4):
            for q in range(4):
                cp_pending.append((kso_d[b, 511 * q:511 * (q + 1), :], ck_d[b, 4 + 511 * q:4 + 511 * (q + 1), :], "cpk"))
                cp_pending.append((vso_d[b, 511 * q:511 * (q + 1), :], cv_d[b, 4 + 511 * q:4 + 511 * (q + 1), :], "cpv"))

        def issue_copies(n, q="act"):
            for _ in range(min(n, len(cp_pending))):
                o, i_, st = cp_pending.pop(0)
                P.dma(q, o, i_, stream=st)

        BS = sb("BS", [128, 7, 4, 8]); BN = sb("BN", [4, 4, 8])
        rb34 = sb("rb34", [34, 8]); ohn = sb("ohn", [34, 4, 4])
        with ExitStack() as es0:
            sb0 = mk_sb(es0)
            rb33 = sb0("rb33", [33, 8]); ohs = sb0("ohs", [33, 3, 384]); e_sb = sb0("e_sb", [8, 3, 384])
            P.dma("sp", rb33[:], rb_d, writes=["rb33"], stream="c0")
            P.dma("sp", ohs[:], oh_d.rearrange("d k i -> k d i"), writes=["ohs"], stream="c0")
            for bi in range(3):
                P.op("pe", lambda e, bi=bi: e.matmul(psb[bi][:8, 0:384], lhsT=rb33[:, :], rhs=ohs[:, bi, :], start=True, stop=True),
                     reads=["rb33", "ohs"], writes=[("ps", bi)])
                P.op("dve", lambda e, bi=bi: e.tensor_copy(out=e_sb[:, bi, :], in_=psb[bi][:8, 0:384]), reads=[("ps", bi)], writes=["e_sb"])
            P.dma("sp", esc.rearrange("d h i -> h d i"), e_sb[:], reads=["e_sb"], writes=["esc"], stream="c0")
            P.barrier()

        hT = sb("hT", [128, 8, NTS], BF)
        QT = sb("QT", [128, 4, NTS], BF)
        KTs = sb("KTs", [128, 4, NS], BF)
        wpre = sb("wpre", [128, 8, 512], BF)

        EB0 = sb("EB0", [128, 3, 2, 2, 128], BF)

        def eb_setup(j, EB, ebname, hq, hqname, tb, tbname):
            for bi in range(3):
                hv = hq[:, :].rearrange("p (h t q) -> p h t q", h=2, t=2)
                tv = tb[:, :].rearrange("p (a h q) -> p a h q", a=2, h=2)
                pairs = []
                for hh in range(2):
                    src = bass.AP(esc.tensor, (bi * 8 + 2 * j + hh) * 384, [[1, 128], [128, 2], [1, 128]])
                    pairs.append((hv[:, hh, :, :], src))
                P.dma_group("sp", pairs, reads=["esc"], writes=[hqname], stream="hq")
                for dp in range(2):
                    for hh in range(2):
                        x = hv[:, hh, 1 - dp, 127:128]
                        rv = bass.AP(x.tensor, x.offset, [x.ap[0], [-1, 128]])
                        P.op("pool", lambda e, dp=dp, hh=hh, rv=rv, tv=tv: e.tensor_copy(out=tv[:, dp, hh, :], in_=rv),
                             reads=[hqname], writes=[tbname])
                P.op("act", lambda e, bi=bi, tb=tb: e.activation(out=EB[:, bi, :, :, :].rearrange("p a h q -> p (a h q)"), in_=tb[:, :], func=AF.Exp),
                     reads=[tbname], writes=[ebname])

        wcount = [0]
        tcount = [0]
        kvc = [0]
        TG_OWN = [(0, 512), (512, 512), (1024, 512), (1536, 512), (2048, NS)]
        TG_HIST = [(0, 512), (512, 512), (1024, 512), (1536, 512)]

        class WS:
            pass
        W = WS()

        def alloc_work(sbw):
            W.xs = [sbw("xs%d" % i, [128, D]) for i in range(2)]
            W.xn = [sbw("xn%d" % i, [128, D], BF) for i in range(2)]
            W.wsl = [sbw("wsl%d" % i, [128, 8, 512], BF) for i in range(2)]

        def load_wslab(parts):
            s = wcount[0] % 2
            wcount[0] += 1
            pairs = []
            for (wd, c0, ncols, o0) in parts:
                pairs += [(W.wsl[s][:, kc, o0:o0 + ncols], wd[kc * 128:(kc + 1) * 128, c0:c0 + ncols]) for kc in range(8)]
            P.dma_group("pool", pairs, writes=[("wsl", s)], stream="w%d" % s)
            return s

        def norm_T(src_d, rows, gt, gname, dstT, dstname, col0):
            s = tcount[0] % 2
            tcount[0] += 1
            X = W.xs[s]
            P.dma("sp", X[:rows, :], src_d, writes=[("xs", s)], stream="x%d" % s)
            norm_T_sb(X, ("xs", s), rows, gt, gname, dstT, dstname, col0, s)
            return s

        def rstd_of(X, xname, rows, s):
            P.op("act", lambda e: e.activation(out=W.xn[s][:rows, :], in_=X[:rows, :], func=AF.Square,
                                               accum_out=ss[:rows, s:s + 1]),
                 reads=[xname], writes=[("ss", s), ("xn", s)])
            P.op("act", lambda e: e.activation(out=ss[:rows, 2 + s:3 + s], in_=ss[:rows, s:s + 1], func=AF.Ln,
                                               scale=1.0 / D, bias=epst[:rows, :]),
                 reads=[("ss", s), "eps"], writes=[("sd", s)])
            P.op("act", lambda e: e.activation(out=ss[:rows, 4 + s:5 + s], in_=ss[:rows, 2 + s:3 + s], func=AF.Exp, scale=-0.5),
                 reads=[("sd", s)], writes=[("rstd", s)])

        def norm_B(X, xname, rows, s):
            rstd_of(X, xname, rows, s)
            P.op("dve", lambda e: e.tensor_scalar(out=W.xn[s][:rows, :], in0=X[:rows, :], scalar1=ss[:rows, 4 + s:5 + s],
                                                  scalar2=None, op0=ALU.mult),
                 reads=[xname, ("rstd", s)], writes=[("xn", s)])

        def norm_C(rows, gt, gname, dstT, dstname, col0, s, gran=512):
            pt = ptb[s]

            def tr(e):
                ins = None
                for kc in range(8):
                    ins = e.transpose(out=pt[:, kc * 128:kc * 128 + rows], in_=W.xn[s][:rows, kc * 128:(kc + 1) * 128],
                                      identity=ident[:rows, :rows])
                return ins
            P.op("pe", tr, reads=[("xn", s), "ident"], writes=[("pt", s)])
            src3 = pt[:].rearrange("p (k t) -> p k t", k=8)[:, :, 0:rows]
            gb = bass.AP(gt[:].tensor, gt[:].offset, [gt[:].ap[0], [1, 8], [0, rows]])
            P.op("dve", lambda e: e.tensor_tensor(out=dstT[:, :, col0:col0 + rows], in0=src3, in1=gb, op=ALU.mult),
                 reads=[("pt", s), gname], writes=[(dstname, col0 // gran)])

        def norm_T_sb(X, xname, rows, gt, gname, dstT, dstname, col0, s, gran=512):
            norm_B(X, xname, rows, s)
            norm_C(rows, gt, gname, dstT, dstname, col0, s, gran)

        def norm_pipeline(tiles, after_cb):
            n = len(tiles)
            base = tcount[0]
            tcount[0] += n
            for i in range(n + 2):
                if i < n:
                    src_d, rows, col0 = tiles[i]
                    s = (base + i) % 2
                    P.dma("sp", W.xs[s][:rows, :], src_d, writes=[("xs", s)], stream="x%d" % s)
                if 0 <= i - 1 < n:
                    src_d, rows, col0 = tiles[i - 1]
                    s = (base + i - 1) % 2
                    norm_B(W.xs[s], ("xs", s), rows, s)
                if 0 <= i - 2 < n:
                    src_d, rows, col0 = tiles[i - 2]
                    s = (base + i - 2) % 2
                    norm_C(rows, g1t, "g1t", hT, "hT", col0, s, gran=128)
                    after_cb(i - 2)

        def fm_proj(slab, c_lo, nchunks, srcT, srcname, tgs, evac):
            k = 0
            for c in range(nchunks):
                for (t0, n) in tgs:
                    bank = k % 4
                    k += 1
                    pb = psb[bank]

                    def mm(e, c=c, t0=t0, n=n, pb=pb):
                        ins = None
                        for kc in range(8):
                            ins = e.matmul(pb[:, 0:n], lhsT=W.wsl[slab][:, kc, (c_lo + c) * 128:(c_lo + c + 1) * 128],
                                           rhs=srcT[:, kc, t0:t0 + n], start=(kc == 0), stop=(kc == 7))
                        return ins
                    P.op("pe", mm, reads=[("wsl", slab), (srcname, t0 // 512)], writes=[("ps", bank)])
                    evac(c, t0, n, pb, ("ps", bank))

        def tm_proj(slab, srcname, col_ap_fn, rows, evac, src_reads=None):
            bank = 4 + (kvc[0] % 2)
            pb = psb[bank]

            def mm(e):
                ins = None
                for kc in range(8):
                    ins = e.matmul(pb[:rows, :], lhsT=col_ap_fn(kc), rhs=W.wsl[slab][:, kc, :], start=(kc == 0), stop=(kc == 7))
                return ins
            P.op("pe", mm, reads=[("wsl", slab)] + (src_reads if src_reads is not None else [(srcname, i) for i in range(5)]),
                 writes=[("ps", bank)])
            evac(pb, ("ps", bank))
            kvc[0] += 1

        vt_index = {}
        for bi, d in enumerate(BRANCH_D):
            for r in range(d):
                for b in range(-1, 16 // d):
                    vt_index[(bi, r, b)] = len(vt_index)
        NVT = len(vt_index)

        with ExitStack() as es1:
            sb1 = mk_sb(es1)
            KT = sb1("KT", [128, 4, 2 * NT], BF)
            V = sb1("V", [128, NVT, 512], BF)
            with ExitStack() as es2:
                sb2 = mk_sb(es2)
                alloc_work(sb2)
                kvst = [sb2("kvst%d" % i, [128, 512]) for i in range(2)]
                wv = sb2("wv", [128, 8, 512], BF)
                P.dma_group("pool", [(wv[:, kc, :], w_in[kc * 128:(kc + 1) * 128, 1024:1536]) for kc in range(8)], writes=["wv"], stream="wv")
                from collections import deque
                pend = deque()
                fmk = [0]

                def fm_unit(slab, c, t0, n, evac):
                    def run():
                        bank = fmk[0] % 4
                        fmk[0] += 1
                        pb = psb[bank]

                        def mm(e):
                            ins = None
                            for kc in range(8):
                                ins = e.matmul(pb[:, 0:n], lhsT=W.wsl[slab][:, kc, c * 128:(c + 1) * 128], rhs=hT[:, kc, t0:t0 + n],
                                               start=(kc == 0), stop=(kc == 7))
                            return ins
                        P.op("pe", mm, reads=[("wsl", slab)] + [("hT", x) for x in range(t0 // 128, (t0 + n + 127) // 128)], writes=[("ps", bank)])
                        evac(c, t0, n, pb, ("ps", bank))
                    return run

                def tm_unit(wsrc, wname, col_ap_fn, rows, tiles, evac):
                    def run():
                        bank = 4 + (kvc[0] % 2)
                        pb = psb[bank]

                        def mm(e):
                            ins = None
                            for kc in range(8):
                                ins = e.matmul(pb[:rows, :], lhsT=col_ap_fn(kc), rhs=wsrc[:, kc, :], start=(kc == 0), stop=(kc == 7))
                            return ins
                        P.op("pe", mm, reads=[wname] + [("hT", x) for x in tiles], writes=[("ps", bank)])
                        evac(pb, ("ps", bank))
                        kvc[0] += 1
                    return run

                def drain(k):
                    for _ in range(min(k, len(pend))):
                        pend.popleft()()

                def v_unit(bi, d, r, b, hist):
                    c0 = (NT - 128 * d + r) if hist else (128 * d * b + r)
                    vi = vt_index[(bi, r, b)]
                    if d == 1:
                        tiles = [15] if hist else [b]
                    elif d == 4:
                        tiles = list(range(12, 16)) if hist else list(range(4 * b, 4 * b + 4))
                    else:
                        tiles = list(range(16))

                    def ev(pb, bn):
                        if d == 1 and not hist:
                            st = kvc[0] % 2
                            P.op("dve", lambda e: e.tensor_copy(out=kvst[st][:, :], in_=pb[:, :]), reads=[bn], writes=[("kvst", st)])
                            P.op("pool", lambda e: e.tensor_copy(out=V[:, vi, :], in_=kvst[st][:, :]), reads=[("kvst", st)], writes=[("V", vi)])
                            P.dma("sp", vo_d[b * 128:(b + 1) * 128, :], kvst[st][:, :], reads=[("kvst", st)], stream="kv%d" % st)
                        else:
                            P.op("act", lambda e: e.copy(out=V[:, vi, :], in_=pb[:, :]), reads=[bn], writes=[("V", vi)])
                    return tm_unit(wv, "wv", lambda kc: hT[:, kc, c0:c0 + 127 * d + 1:d], 128, tiles, ev)

                sk = load_wslab([(w_in, 512, 512, 0)])
                P.dma("sp", rb34[:], rb34_d, writes=["rb34"], stream="c1")
                P.dma("sp", ohn[:], ohn_d, writes=["ohn"], stream="c1")
                for q in range(7):
                    st = q % 2
                    ohv = kvst[st][0:34, :].rearrange("p (a b) -> p a b", a=4)
                    P.dma("sp", ohv, ohs_d[:, 4 * q:4 * q + 4, :], writes=[("kvst", st)], stream="kv%d" % st)

                    def bsm(e, q=q, ohv=ohv):
                        ins = None
                        for k in range(4):
                            ut = 4 * q + k
                            ins = e.matmul(psb[5][:, ut * 8:ut * 8 + 8], lhsT=ohv[:, k, :], rhs=rb34[:, :], start=True, stop=True)
                        return ins
                    P.op("pe", bsm, reads=["rb34", ("kvst", st)], writes=[("ps", 5)])

                def bnm(e):
                    ins = None
                    for t in range(4):
                        ins = e.matmul(psb[5][0:4, 256 + t * 8:256 + t * 8 + 8], lhsT=ohn[:, t, :], rhs=rb34[:, :], start=True, stop=True)
                    return ins
                P.op("pe", bnm, reads=["rb34", "ohn"], writes=[("ps", 5)])
                P.op("dve", lambda e: e.tensor_copy(out=BS[:].rearrange("p u t h -> p (u t h)"), in_=psb[5][:, 0:224]), reads=[("ps", 5)], writes=["BS"])
                P.op("dve", lambda e: e.tensor_copy(out=BN[:].rearrange("p t h -> p (t h)"), in_=psb[5][0:4, 256:288]), reads=[("ps", 5)], writes=["BN"])

                def evac_kh(c, t0, n, pb, bn):
                    P.op("act", lambda e: e.copy(out=KT[:, c, t0:t0 + n], in_=pb[:, 0:n]), reads=[bn], writes=[("KT", c, t0 // 512)])

                def after_a(t):
                    if t % 4 == 3:
                        for c in range(4):
                            pend.append(fm_unit(sk, c, 512 * (t // 4), 512, evac_kh))
                    if t == 15:
                        pend.append(v_unit(0, 1, 0, -1, True))
                        for r in range(4):
                            pend.append(v_unit(1, 4, r, -1, True))
                        for r in range(16):
                            pend.append(v_unit(2, 16, r, -1, True))
                    drain(2)
                norm_pipeline([(xh[t * 128:(t + 1) * 128, :], 128, t * 128) for t in range(16)], after_a)
                sq = load_wslab([(w_in, 0, 512, 0)])
                drain(12)

                def evac_q(c, t0, n, pb, bn):
                    P.op("act", lambda e: e.mul(out=QT[:, c, t0:t0 + n], in_=pb[:, 0:n], mul=0.125), reads=[bn], writes=[("QT", c, t0 // 512)])

                def evac_k(c, t0, n, pb, bn):
                    if t0 < NT:
                        P.op("act", lambda e: e.copy(out=KT[:, c, NT + t0:NT + t0 + n], in_=pb[:, 0:n]), reads=[bn], writes=[("KT", c, 4 + t0 // 512)])
                    else:
                        P.op("act", lambda e: e.copy(out=KTs[:, c, :], in_=pb[:, 0:n]), reads=[bn], writes=[("KTs", c)])

                def ktm_unit(t):
                    rows = 128 if t < 16 else NS

                    def ev(pb, bn):
                        st = kvc[0] % 2
                        P.op("dve", lambda e: e.tensor_copy(out=kvst[st][:rows, :], in_=pb[:rows, :]), reads=[bn], writes=[("kvst", st)])
                        if t < 16:
                            P.dma("sp", ko_d[t * 128:(t + 1) * 128, :], kvst[st][:, :], reads=[("kvst", st)], stream="kv%d" % st)
                        else:
                            P.dma_group("sp", [(kso_d[b, 2044:2048, :], kvst[st][4 * b:4 * b + 4, :]) for b in range(4)],
                                        reads=[("kvst", st)], stream="kv%d" % st)
                    return tm_unit(W.wsl[sk], ("wsl", sk), lambda kc: hT[:, kc, t * 128:t * 128 + rows], rows, [t], ev)

                def vs_unit():
                    def ev(pb, bn):
                        st = kvc[0] % 2
                        P.op("dve", lambda e: e.tensor_copy(out=kvst[st][:NS, :], in_=pb[:NS, :]), reads=[bn], writes=[("kvst", st)])
                        P.dma_group("sp", [(vso_d[b, 2044:2048, :], kvst[st][4 * b:4 * b + 4, :]) for b in range(4)],
                                    reads=[("kvst", st)], stream="kv%d" % st)
                    return tm_unit(wv, "wv", lambda kc: hT[:, kc, NT:NTS], NS, [16], ev)

                def after_b(t):
                    pend.append(ktm_unit(t))
                    if t < 16:
                        pend.append(v_unit(0, 1, 0, t, False))
                    else:
                        pend.append(vs_unit())
                    if t % 4 == 3 or t == 16:
                        t0, n = TG_OWN[t // 4]
                        for c in range(4):
                            pend.append(fm_unit(sq, c, t0, n, evac_q))
                            pend.append(fm_unit(sk, c, t0, n, evac_k))
                        if t < 16:
                            for r in range(4):
                                pend.append(v_unit(1, 4, r, t // 4, False))
                    if t == 15:
                        for r in range(16):
                            pend.append(v_unit(2, 16, r, 0, False))
                    drain(4)
                drain(10 ** 6)
                norm_pipeline([(xo[t * 128:(t + 1) * 128, :], 128, t * 128) for t in range(16)] + [(xsm, NS, NT)], after_b)
                eb_setup(0, EB0, ("EB", 0), kvst[0], ("kvst", 0), kvst[1], ("kvst", 1))
                drain(10 ** 6)
                P.barrier()

            if STAGE < 6:
                P.finish()
                return nc
            with ExitStack() as es3:
                sb3 = mk_sb(es3)
                set_psum("attn")
                PA = psA[0]
                acc = sb3("acc", [128, 2, NT])
                NBUF = 3
                Eb = [sb3("Eb%d" % i, [128, 512], BF) for i in range(NBUF)]
                PT = [sb3("PT%d" % i, [128, 512], BF) for i in range(NBUF)]
                Hq = sb3("Hq", [128, 3, 2, 2, 128])
                TB = sb3("TB", [128, 3, 2, 2, 128])
                EBs = [EB0, sb3("EB1", [128, 3, 2, 2, 128], BF)]
                groups = []
                for j in range(4):
                    for bi, d in enumerate(BRANCH_D):
                        for r in range(d):
                            for b in range(16 // d):
                                groups.append((j, bi, d, r, b))

                def chunk_setup1(j):
                    pairs = []
                    for bi in range(3):
                        for hh in range(2):
                            src = bass.AP(esc.tensor, (bi * 8 + 2 * j + hh) * 384, [[1, 128], [128, 2], [1, 128]])
                            pairs.append((Hq[:, bi, hh, :, :], src))
                    P.dma_group("sp", pairs, reads=["esc"], writes=["Hq"], stream="hq")
                    for bi in range(3):
                        for dp in range(2):
                            for hh in range(2):
                                x = Hq[:, bi, hh, 1 - dp, 127:128]
                                rv = bass.AP(x.tensor, x.offset, [x.ap[0], [-1, 128]])
                                P.op("pool", lambda e, bi=bi, dp=dp, hh=hh, rv=rv: e.tensor_copy(out=TB[:, bi, dp, hh, :], in_=rv),
                                     reads=["Hq"], writes=["TB"])

                def chunk_setup2(j):
                    P.op("act", lambda e: e.activation(out=EBs[j % 2][:].rearrange("p a b c q -> p (a b c q)"),
                                                       in_=TB[:].rearrange("p a b c q -> p (a b c q)"), func=AF.Exp),
                         reads=["TB"], writes=[("EB", j % 2)])

                def geom(gi):
                    j, bi, d, r, b = groups[gi]
                    q0 = 128 * d * b + r
                    kD0 = NT + q0
                    kP0 = NT + 128 * d * (b - 1) + r
                    return j, bi, d, r, b, q0, kD0, kP0

                def stage1(gi):
                    j, bi, d, r, b, q0, kD0, kP0 = geom(gi)
                    if gi % 48 == 10 and j < 3:
                        chunk_setup1(j + 1)
                    if gi % 48 == 30 and j < 3:
                        chunk_setup2(j + 1)
                    if gi % 6 == 3:
                        issue_copies(1, q="sp")
                    if gi == 4:
                        P.dma_group("pool", [(wpre[:, kc, :], w_in[kc * 128:(kc + 1) * 128, 1536:2048]) for kc in range(8)],
                                    writes=["wpre"], stream="wpre")
                    sl = gi % NBUF
                    qgs = sorted(set([q0 // 512, (q0 + 127 * d) // 512]))
                    kgs = sorted(set([kD0 // 512, (kD0 + 127 * d) // 512, kP0 // 512, (kP0 + 127 * d) // 512]))

                    def smm(e):
                        ins = None
                        for hh in range(2):
                            rows = slice(64 * hh, 64 * hh + 64)
                            for dp, k0 in ((0, kD0), (1, kP0)):
                                ins = e.matmul(PA[:, 2 * sl + hh, dp * 128:dp * 128 + 128], lhsT=KT[rows, j, k0:k0 + 127 * d + 1:d],
                                               rhs=QT[rows, j, q0:q0 + 127 * d + 1:d], start=True, stop=True)
                        return ins
                    P.op("pe", smm, reads=[("QT", j, x) for x in qgs] + [("KT", j, x) for x in kgs], writes=[("ps", 2 * sl), ("ps", 2 * sl + 1)])
                    src = PA[:, 2 * sl:2 * sl + 2, 0:256]
                    e3 = Eb[sl][:, :].rearrange("p (h c) -> p h c", h=2)
                    if b == 0:
                        P.op("act", lambda e: e.activation(out=e3[:, :, 0:128], in_=src[:, :, 0:128], func=AF.Exp),
                             reads=[("ps", 2 * sl), ("ps", 2 * sl + 1)], writes=[("Eba", sl)])
                        P.op("act", lambda e: e.activation(out=e3[:, :, 128:256], in_=src[:, :, 128:256], func=AF.Exp, bias=hm[:, :]),
                             reads=[("ps", 2 * sl), ("ps", 2 * sl + 1), "hm"], writes=[("Ebb", sl)])
                    else:
                        P.op("act", lambda e: e.activation(out=e3, in_=src, func=AF.Exp),
                             reads=[("ps", 2 * sl), ("ps", 2 * sl + 1)], writes=[("Eba", sl), ("Ebb", sl)])
                    ebv = EBs[j % 2][:, bi, :, :, :].rearrange("p a h q -> p h a q")
                    e4 = Eb[sl][:, :].rearrange("p (h a q) -> p h a q", h=2, a=2)
                    p4 = PT[sl][:, :].rearrange("p (h a q) -> p h a q", h=2, a=2)
                    P.op("dve", lambda e: e.tensor_tensor(out=p4, in0=e4, in1=ebv, op=ALU.mult),
                         reads=[("Eba", sl), ("Ebb", sl), ("EB", j % 2)], writes=[("PTa", sl), ("PTb", sl)])

                def stage2(gi):
                    j, bi, d, r, b, q0, kD0, kP0 = geom(gi)
                    sl = gi % NBUF
                    ob = 6 + gi % 2
                    obank = PA[:, ob, :]
                    viD = vt_index[(bi, r, b)]
                    viP = vt_index[(bi, r, b - 1)]
                    qgs = sorted(set([q0 // 512, (q0 + 127 * d) // 512]))

                    def pvm(e):
                        ins = None
                        for hh in range(2):
                            h = 2 * j + hh
                            rows = slice(64 * hh, 64 * hh + 64)
                            pD = PT[sl][:, hh * 256:hh * 256 + 128]
                            pP = PT[sl][:, hh * 256 + 128:hh * 256 + 256]
                            e.matmul(PA[rows, ob, 0:128], lhsT=V[:, viD, 64 * h:64 * h + 64], rhs=pD, start=True, stop=False)
                            e.matmul(PA[rows, ob, 0:128], lhsT=V[:, viP, 64 * h:64 * h + 64], rhs=pP, start=False, stop=True)
                            e.matmul(PA[rows, ob, 128:256], lhsT=ones[:, :], rhs=pD, start=True, stop=False)
                            ins = e.matmul(PA[rows, ob, 128:256], lhsT=ones[:, :], rhs=pP, start=False, stop=True)
                        return ins
                    P.op("pe", pvm, reads=[("PTa", sl), ("PTb", sl), ("V", viD), ("V", viP), "ones"], writes=[("ps", ob)])
                    accv = acc[:, :, q0:q0 + 127 * d + 1:d]
                    ov = PA[:, ob, 0:256].rearrange("p (a q) -> p a q", a=2)
                    if bi == 0:
                        P.op("act", lambda e: e.copy(out=accv, in_=ov), reads=[("ps", ob)], writes=[("acc", x) for x in qgs])
                    else:
                        P.op("dve", lambda e: e.tensor_tensor(out=accv, in0=ov, in1=accv, op=ALU.add),
                             reads=[("ps", ob)] + [("acc", x) for x in qgs], writes=[("acc", x) for x in qgs])
                    if (bi, r, b) == (2, 15, 0):
                        allacc = [("acc", x) for x in range(4)]
                        P.op("act", lambda e: e.activation(out=acc[:, 1, :], in_=acc[:, 1, :], func=AF.Ln), reads=allacc, writes=allacc)
                        P.op("act", lambda e: e.activation(out=acc[:, 1, :], in_=acc[:, 1, :], func=AF.Exp, scale=-1.0), reads=allacc, writes=allacc)
                        P.op("dve", lambda e: e.tensor_tensor(out=QT[:, j, 0:NT], in0=acc[:, 0, :], in1=acc[:, 1, :], op=ALU.mult),
                             reads=allacc, writes=[("QT", j, x) for x in range(4)])

                LA = 2
                NG = len(groups)
                for gi in range(NG + LA):
                    if gi < NG:
                        stage1(gi)
                    if gi - LA >= 0:
                        stage2(gi - LA)
                P.barrier()
                set_psum("std")

        if STAGE < 7:
            P.finish()
            return nc
        with ExitStack() as es4:
            sb4 = mk_sb(es4)
            alloc_work(sb4)
            GU = sb4("GU", [128, 4, NTS], BF)
            WD = sb4("WD", [128, NFC, D], BF)
            with ExitStack() as es7:
                sb7 = mk_sb(es7)
                SbS = sb7("SbS", [128, 2, 128]); PS = sb7("PS", [128, 2, 128], BF)
                osb = sb7("osb", [128, 2, 64])
                Vn = sb7("Vn", [4, 4, 512], BF)
                Kst = sb7("Kst", [128, 3, 512]); Vst = sb7("Vst", [128, 3, 512])
                Kc = sb7("Kc", [128, 7, 512], BF)
                Vc = [sb7("Vc%d" % i, [128, 7, 512], BF) for i in range(2)]
                KsT = sb7("KsT", [128, 4, 7, 128], BF)

                def issue_loads(b, vfull=True):
                    pk, pv = [], []
                    for u in range(3):
                        for r4 in range(4):
                            r0 = 512 * u + r4
                            pk.append((Kst[r4:128:4, u, :], ck_d[b, r0:r0 + 16 * 31 + 1:16, :]))
                            pv.append((Vst[r4:128:4, u, :], cv_d[b, r0:r0 + 16 * 31 + 1:16, :]))
                    P.dma_group("sp", pk, writes=["Kst"], stream="kst")
                    P.dma_group("sp", pv, writes=["Vst"], stream="vst")
                    P.dma_group("pool", [(Kc[:, 3 + k, :], ck_d[b, 1536 + 128 * k:1536 + 128 * (k + 1), :]) for k in range(4)],
                                writes=["KcF"], stream="kcf")
                    if vfull:
                        issue_vfull(b)

                def issue_vfull(b):
                    P.dma_group("pool", [(Vc[b % 2][:, 3 + k, :], cv_d[b, 1536 + 128 * k:1536 + 128 * (k + 1), :]) for k in range(4)],
                                writes=[("VcF", b % 2)], stream="vcf%d" % (b % 2))
                issue_loads(0)
                P.dma_group("pool", [(Vn[0:4, b, :], vso_d[b, 2044:2048, :]) for b in range(4)],
                            writes=[("Vn", b) for b in range(4)], stream="vn")
                P.lastw["wpre"] = ("d_wpre", P.dsem["wpre"], 16 * P.dcnt["wpre"], None)
                k = 0
                for c in range(4):
                    for (t0, n) in TG_OWN:
                        bank = k % 4
                        k += 1
                        pb = psb[bank]

                        def mm(e, c=c, t0=t0, n=n, pb=pb):
                            ins = None
                            for kc in range(8):
                                ins = e.matmul(pb[:, 0:n], lhsT=wpre[:, kc, c * 128:(c + 1) * 128], rhs=hT[:, kc, t0:t0 + n],
                                               start=(kc == 0), stop=(kc == 7))
                            return ins
                        P.op("pe", mm, reads=["wpre"], writes=[("ps", bank)])
                        P.op("act", lambda e, c=c, t0=t0, n=n, pb=pb: e.copy(out=GU[:, c, t0:t0 + n], in_=pb[:, 0:n]),
                             reads=[("ps", bank)], writes=[("GU", c, t0 // 512)])

                P.op("dve", lambda e: e.memset(SbS[:], 0.0), writes=["SbS"])
                obank = psb[5]

                def st_T(b):
                    i = b % 2
                    P.op("act", lambda e: e.copy(out=Kc[:, 0:3, :], in_=Kst[:, :, :]), reads=["Kst"], writes=["KcP"])
                    P.op("dve", lambda e, i=i: e.tensor_copy(out=Vc[i][:, 0:3, :], in_=Vst[:, :, :]), reads=["Vst"], writes=[("VcP", i)])
                    blocks = [(u, jj) for u in range(7) for jj in range(4)]
                    for c0 in range(0, 28, 8):
                        chunk = blocks[c0:c0 + 8]
                        ti = (c0 // 8) % 2
                        pt = ptb[ti]

                        def trs(e, chunk=chunk, pt=pt, i=i):
                            ins = None
                            for k, (u, jj) in enumerate(chunk):
                                ins = e.transpose(out=pt[:, k * 128:(k + 1) * 128], in_=Kc[:, u, jj * 128:(jj + 1) * 128], identity=ident[:, :])
                            return ins
                        P.op("pe", trs, reads=["KcP", "KcF", "ident"], writes=[("pt", ti)])
                        for k, (u, jj) in enumerate(chunk):
                            eng = "act" if k % 2 == 0 else "dve"
                            P.op(eng, (lambda e, k=k, u=u, jj=jj, pt=pt, i=i: (e.copy if False else e.tensor_copy)(out=KsT[:, jj, u, :], in_=pt[:, k * 128:(k + 1) * 128]))
                                 if eng == "dve" else (lambda e, k=k, u=u, jj=jj, pt=pt, i=i: e.copy(out=KsT[:, jj, u, :], in_=pt[:, k * 128:(k + 1) * 128])),
                                 reads=[("pt", ti)], writes=[("KsT", u, jj)])

                def st_S(b):
                    i = b % 2
                    qc = NT + 4 * b

                    def ssm(e, i=i, b=b, qc=qc):
                        ins = None
                        for hh in range(2):
                            rows = slice(64 * hh, 64 * hh + 64)
                            for jj in range(4):
                                for u in range(7):
                                    ins = e.matmul(psb[1 + hh][:, (u * 4 + jj) * 4:(u * 4 + jj) * 4 + 4], lhsT=KsT[rows, jj, u, :],
                                                   rhs=QT[rows, jj, qc:qc + 4], start=True, stop=True)
                                ins = e.matmul(psb[1 + hh][0:4, 112 + jj * 4:112 + jj * 4 + 4], lhsT=KTs[rows, jj, 4 * b:4 * b + 4],
                                               rhs=QT[rows, jj, qc:qc + 4], start=True, stop=True)
                        return ins
                    P.op("pe", ssm, reads=[("KsT", u, jj) for u in range(7) for jj in range(4)] + [("QT", jj, 4) for jj in range(4)] + [("KTs", jj) for jj in range(4)],
                         writes=[("ps", 1), ("ps", 2)])
                    for hh in range(2):
                        sbk = psb[1 + hh]
                        in0 = sbk[:, 0:112].rearrange("p (u j t) -> p u j t", u=7, j=4)
                        x = BS[:, :, :, hh:hh + 1]
                        in1 = bass.AP(x.tensor, x.offset, [x.ap[0], [32, 7], [2, 4], [8, 4]])
                        out = SbS[:, hh, 0:112].rearrange("p (u j t) -> p u j t", u=7, j=4)
                        P.op("dve", lambda e, in0=in0, in1=in1, out=out: e.tensor_tensor(out=out, in0=in0, in1=in1, op=ALU.add),
                             reads=[("ps", 1 + hh), "BS"], writes=[("SbS", hh)])
                        in0n = sbk[0:4, 112:128].rearrange("p (j t) -> p j t", j=4)
                        xn_ = BN[:, :, hh:hh + 1]
                        in1n = bass.AP(xn_.tensor, xn_.offset, [xn_.ap[0], [2, 4], [8, 4]])
                        outn = SbS[0:4, hh, 112:128].rearrange("p (j t) -> p j t", j=4)
                        P.op("dve", lambda e, in0n=in0n, in1n=in1n, outn=outn: e.tensor_tensor(out=outn, in0=in0n, in1=in1n, op=ALU.add),
                             reads=[("ps", 1 + hh), "BN"], writes=[("SbSn", hh)])
                    P.op("act", lambda e: e.activation(out=PS[:].rearrange("p a c -> p (a c)"), in_=SbS[:].rearrange("p a c -> p (a c)"), func=AF.Exp),
                         reads=[("SbS", 0), ("SbS", 1), ("SbSn", 0), ("SbSn", 1), "SbS"], writes=["PS"])

                def st_V(b):
                    i = b % 2

                    def spv(e, i=i, b=b):
                        ins = None
                        for hh in range(2):
                            rows = slice(64 * hh, 64 * hh + 64)
                            for jj in range(4):
                                h = 2 * jj + hh
                                for part in range(2):
                                    oc = (part * 4 + jj) * 16 + 4 * b
                                    for u in range(7):
                                        lhsT = Vc[i][:, u, 64 * h:64 * h + 64] if part == 0 else ones[:, :]
                                        e.matmul(obank[rows, oc:oc + 4], lhsT=lhsT, rhs=PS[:, hh, (u * 4 + jj) * 4:(u * 4 + jj) * 4 + 4],
                                                 start=(u == 0), stop=False)
                                    lhsT = Vn[0:4, b, 64 * h:64 * h + 64] if part == 0 else ones[0:4, :]
                                    ins = e.matmul(obank[rows, oc:oc + 4], lhsT=lhsT, rhs=PS[0:4, hh, 112 + jj * 4:112 + jj * 4 + 4],
                                                   start=False, stop=True)
                        return ins
                    P.op("pe", spv, reads=["PS", ("VcP", i), ("VcF", i), ("Vn", b), "ones"], writes=[("ps", 5)])
                st_T(0)
                issue_loads(1)
                st_S(0)
                for b in range(1, 4):
                    st_T(b)
                    if b + 1 < 4:
                        issue_loads(b + 1, vfull=False)
                    st_V(b - 1)
                    if b + 1 < 4:
                        issue_vfull(b + 1)
                    if b == 3:
                        svg = load_wslab([(w_in, 2048, 512, 0)])
                        wo0 = load_wslab([(w_out, 0, 512, 0)])
                        P.dma_group("pool", [(wpre[:, kc, :], w_out[kc * 128:(kc + 1) * 128, 512:1024]) for kc in range(8)],
                                    writes=["wpre"], stream="wpre")
                    st_S(b)
                st_V(3)
                P.op("dve", lambda e: e.tensor_copy(out=osb[:].rearrange("p a c -> p (a c)"), in_=obank[:, 0:128]), reads=[("ps", 5)], writes=["osb"])
                P.op("dve", lambda e: e.reciprocal(out=osb[:, 1, :], in_=osb[:, 1, :]), reads=["osb"], writes=["osb"])
                P.op("dve", lambda e: e.tensor_tensor(out=QT[:, :, NT:NTS], in0=osb[:, 0, :].rearrange("p (j c) -> p j c", j=4),
                                                      in1=osb[:, 1, :].rearrange("p (j c) -> p j c", j=4), op=ALU.mult),
                     reads=["osb"], writes=[("QT", jj, 4) for jj in range(4)])
                P.barrier()

            with ExitStack() as es5:
                sb5 = mk_sb(es5)
                lng = sb5("lng", [128, 512]); lnb = sb5("lnb", [128, 512])
                lnw = [sb5("lnw%d" % i, [128, 512]) for i in range(3)]
                vnb = [sb5("vnb%d" % i, [128, 512], BF) for i in range(3)]
                wtmp = sb5("wtmp", [128, 8, 128]); tri = sb5("tri", [128, 128]); WmT = sb5("WmT", [128, 8, 128], BF)
                bsp = sb5("bsp", [128, 4, 128]); gtmp = [sb5("gtmp%d" % i, [128, 512]) for i in range(2)]
                bdf = sb5("bdf", [NS, 8, NS]); bdm = sb5("bdm", [NS, NS]); BD = sb5("BD", [NS, 8, NS], BF)
                x2t = [sb5("x2t%d" % i, [128, D]) for i in range(2)]
                st2 = sb5("st2", [128, 24])
                P.dma("sp", lng[:], lng_d, writes=["lng"], stream="c0")
                P.dma("sp", lnb[:], lnb_d, writes=["lnb"], stream="c0")
                P.dma("sp", tri[:], tri_d, writes=["tri"], stream="c0")
                P.dma("sp", bdm[:], bdm_d, writes=["bdm"], stream="c0")
                P.dma("sp", bsp[:], bsp_d.rearrange("j p t -> p j t"), writes=["bsp"], stream="c0")
                P.dma("sp", wtmp[:], sgw_d.rearrange("g s t -> s g t"), writes=["wtmp"], stream="c0")
                trb = bass.AP(tri[:].tensor, tri[:].offset, [tri[:].ap[0], [0, 8], [1, 128]])
                P.op("dve", lambda e: e.tensor_tensor(out=WmT[:], in0=wtmp[:], in1=trb, op=ALU.mult), reads=["wtmp", "tri"], writes=["WmT"])
                P.op("dve", lambda e: e.memset(bdf[:], 0.0), writes=["bdf"])
                with nc.allow_non_contiguous_dma(reason="tiny 4x4 sgu blocks"):
                    P.dma_group("sp", [(bdf[4 * b:4 * b + 4, :, 4 * b:4 * b + 4], sgw_d[:, 0:4, 0:4].rearrange("g s t -> s g t")) for b in range(4)],
                                reads=[], writes=["bdf"], stream="c0")
                bdb = bass.AP(bdm[:].tensor, bdm[:].offset, [bdm[:].ap[0], [0, 8], [1, NS]])
                P.op("dve", lambda e: e.tensor_tensor(out=BD[:], in0=bdf[:], in1=bdb, op=ALU.mult), reads=["bdf", "bdm"], writes=["BD"])

                lnc = [0]

                def ln_rows(pb, bn, rows, out_ap, outname):
                    i = lnc[0] % 2
                    lnc[0] += 1
                    o = 8 * i
                    L = lnw[i]
                    P.op("act", lambda e: e.activation(out=L[:rows, :], in_=pb[:rows, :], func=AF.Identity, accum_out=st2[:rows, o:o + 1]),
                         reads=[bn], writes=[("lnw", i), ("st_sum", i)])
                    P.op("act", lambda e: e.activation(out=W.xn[i][:rows, 0:512], in_=pb[:rows, :], func=AF.Square, accum_out=st2[:rows, o + 1:o + 2]),
                         reads=[bn], writes=[("xn", i), ("st_sq", i)])
                    P.op("dve", lambda e: e.tensor_scalar(out=st2[:rows, o + 2:o + 3], in0=st2[:rows, o:o + 1], scalar1=1.0 / 512, scalar2=None, op0=ALU.mult),
                         reads=[("st_sum", i)], writes=[("st_mean", i)])
                    P.op("dve", lambda e: e.tensor_tensor(out=st2[:rows, o + 3:o + 4], in0=st2[:rows, o + 2:o + 3], in1=st2[:rows, o + 2:o + 3], op=ALU.mult),
                         reads=[("st_mean", i)], writes=[("st_m2", i)])
                    P.op("dve", lambda e: e.scalar_tensor_tensor(out=st2[:rows, o + 4:o + 5], in0=st2[:rows, o + 1:o + 2], scalar=1.0 / 512,
                                                                 in1=st2[:rows, o + 3:o + 4], op0=ALU.mult, op1=ALU.subtract),
                         reads=[("st_sq", i), ("st_m2", i)], writes=[("st_var", i)])
                    P.op("act", lambda e: e.activation(out=st2[:rows, o + 5:o + 6], in_=st2[:rows, o + 4:o + 5], func=AF.Sqrt, scale=1.0, bias=epst[:rows, :]),
                         reads=[("st_var", i), "eps"], writes=[("st_sd", i)])
                    P.op("dve", lambda e: e.reciprocal(out=st2[:rows, o + 6:o + 7], in_=st2[:rows, o + 5:o + 6]), reads=[("st_sd", i)], writes=[("st_rstd", i)])
                    P.op("dve", lambda e: e.tensor_scalar(out=L[:rows, :], in0=L[:rows, :], scalar1=st2[:rows, o + 2:o + 3], scalar2=st2[:rows, o + 6:o + 7],
                                                          op0=ALU.subtract, op1=ALU.mult),
                         reads=[("lnw", i), ("st_mean", i), ("st_rstd", i)], writes=[("lnw", i)])
                    P.op("dve", lambda e: e.tensor_tensor(out=L[:rows, :], in0=L[:rows, :], in1=lng[:rows, :], op=ALU.mult),
                         reads=[("lnw", i), "lng"], writes=[("lnw", i)])
                    P.op("dve", lambda e: e.tensor_tensor(out=out_ap, in0=L[:rows, :], in1=lnb[:rows, :], op=ALU.add),
                         reads=[("lnw", i), "lnb"], writes=[outname])
                    return i

                wx = wpre
                vslot = {}

                ljunk = [sb5("ljunk%d" % i, [128, 512], BF) for i in range(3)]
                vslot = {}
                pbank = {}

                def cd_vg(t):
                    rows = 128 if t < 16 else NS
                    bank = (4, 5, 1)[t % 3]
                    pb = psb[bank]
                    pbank[t] = (pb, ("ps", bank))

                    def mm(e):
                        ins = None
                        for kc in range(8):
                            ins = e.matmul(pb[:rows, :], lhsT=hT[:, kc, t * 128:t * 128 + rows], rhs=W.wsl[svg][:, kc, :], start=(kc == 0), stop=(kc == 7))
                        return ins
                    P.op("pe", mm, reads=[("wsl", svg), ("hTt", t)], writes=[("ps", bank)])

                def cd_ln(t):
                    rows = 128 if t < 16 else NS
                    pb, bn = pbank[t]
                    i = t % 3
                    vslot[t] = i
                    o = 8 * i
                    L = lnw[i]
                    P.op("act", lambda e: e.activation(out=L[:rows, :], in_=pb[:rows, :], func=AF.Identity, accum_out=st2[:rows, o:o + 1]),
                         reads=[bn], writes=[("lnw", i), ("st_sum", i)])
                    P.op("act", lambda e: e.activation(out=ljunk[i][:rows, :], in_=pb[:rows, :], func=AF.Square, accum_out=st2[:rows, o + 1:o + 2]),
                         reads=[bn], writes=[("ljunk", i), ("st_sq", i)])
                    P.op("dve", lambda e: e.tensor_scalar(out=st2[:rows, o + 2:o + 3], in0=st2[:rows, o:o + 1], scalar1=1.0 / 512, scalar2=None, op0=ALU.mult),
                         reads=[("st_sum", i)], writes=[("st_mean", i)])
                    P.op("dve", lambda e: e.tensor_tensor(out=st2[:rows, o + 3:o + 4], in0=st2[:rows, o + 2:o + 3], in1=st2[:rows, o + 2:o + 3], op=ALU.mult),
                         reads=[("st_mean", i)], writes=[("st_m2", i)])
                    P.op("dve", lambda e: e.scalar_tensor_tensor(out=st2[:rows, o + 4:o + 5], in0=st2[:rows, o + 1:o + 2], scalar=1.0 / 512,
                                                                 in1=st2[:rows, o + 3:o + 4], op0=ALU.mult, op1=ALU.subtract),
                         reads=[("st_sq", i), ("st_m2", i)], writes=[("st_var", i)])
                    P.op("act", lambda e: e.activation(out=st2[:rows, o + 5:o + 6], in_=st2[:rows, o + 4:o + 5], func=AF.Ln, scale=1.0, bias=epst[:rows, :]),
                         reads=[("st_var", i), "eps"], writes=[("st_sd", i)])
                    P.op("act", lambda e: e.activation(out=st2[:rows, o + 6:o + 7], in_=st2[:rows, o + 5:o + 6], func=AF.Exp, scale=-0.5),
                         reads=[("st_sd", i)], writes=[("st_rstd", i)])
                    P.op("dve", lambda e: e.tensor_scalar(out=L[:rows, :], in0=L[:rows, :], scalar1=st2[:rows, o + 2:o + 3], scalar2=st2[:rows, o + 6:o + 7],
                                                          op0=ALU.subtract, op1=ALU.mult),
                         reads=[("lnw", i), ("st_mean", i), ("st_rstd", i)], writes=[("lnw", i)])
                    P.op("pool", lambda e: e.tensor_tensor(out=L[:rows, :], in0=L[:rows, :], in1=lng[:rows, :], op=ALU.mult),
                         reads=[("lnw", i), "lng"], writes=[("lnw", i)])
                    if t < 16:
                        P.op("pool", lambda e: e.tensor_tensor(out=vnb[i][:rows, :], in0=L[:rows, :], in1=lnb[:rows, :], op=ALU.add),
                             reads=[("lnw", i), "lnb"], writes=[("vnb", i)])
                    else:
                        P.op("pool", lambda e: e.tensor_tensor(out=L[:rows, :], in0=L[:rows, :], in1=lnb[:rows, :], op=ALU.add),
                             reads=[("lnw", i), "lnb"], writes=[("lnw", i)])
                        P.dma("sp", svo_d, L[:rows, :], reads=[("lnw", i)], stream="svo")
                        P.op("pool", lambda e: e.tensor_copy(out=vnb[i][:rows, :], in_=L[:rows, :]), reads=[("lnw", i)], writes=[("vnb", i)])

                def cd_sgu(t):
                    rows = 128 if t < 16 else NS
                    ncol = rows
                    vi = vslot[t]
                    gi = t % 2
                    gb_ = psb[0]

                    def gm(e):
                        ins = None
                        for g8 in range(8):
                            jj, hh = g8 // 2, g8 % 2
                            rhs = WmT[:, g8, :] if t < 16 else BD[:, g8, :]
                            ins = e.matmul(gb_[64 * hh:64 * hh + 64, jj * 128:jj * 128 + ncol], lhsT=vnb[vi][:rows, 64 * g8:64 * g8 + 64],
                                           rhs=rhs, start=True, stop=True)
                        return ins
                    P.op("pe", gm, reads=[("vnb", vi), "WmT", "BD"], writes=[("ps", 0)])
                    gv = gb_[:, :].rearrange("p (j q) -> p j q", j=4)[:, :, 0:ncol]
                    gt_ = gtmp[gi][:, :].rearrange("p (j q) -> p j q", j=4)[:, :, 0:ncol]
                    if t < 16:
                        bv = bsp[:, :, :]
                    else:
                        x = bsp[:, :, 0:4]
                        bv = bass.AP(x.tensor, x.offset, [x.ap[0], x.ap[1], [0, 4], [1, 4]])
                        gv = gv.rearrange("p j (b q) -> p j b q", b=4)
                        gt_ = gt_.rearrange("p j (b q) -> p j b q", b=4)
                    P.op("dve", lambda e: e.tensor_tensor(out=gt_, in0=gv, in1=bv, op=ALU.add), reads=[("ps", 0), "bsp"], writes=[("gtmp", gi)])
                    guv = GU[:, :, t * 128:t * 128 + ncol]
                    g2 = gtmp[gi][:, :].rearrange("p (j q) -> p j q", j=4)[:, :, 0:ncol]
                    P.op("dve", lambda e: e.tensor_tensor(out=guv, in0=g2, in1=guv, op=ALU.mult),
                         reads=[("gtmp", gi)] + [("GU", c, t // 4) for c in range(4)], writes=[("GUt", t)])

                dslot = {}

                def cd_oproj(t):
                    rows = 128 if t < 16 else NS
                    s = tcount[0] % 2
                    tcount[0] += 1
                    dslot[t] = s
                    P.dma("sp", W.xs[s][:rows, :], (xo[t * 128:(t + 1) * 128, :] if t < 16 else xsm), writes=[("xs", s)], stream="x%d" % s)
                    for half in range(2):
                        bank = 2 + half
                        pb = psb[bank]
                        wsrc = W.wsl[wo0] if half == 0 else wx

                        def om(e, pb=pb, wsrc=wsrc):
                            ins = None
                            for kc in range(8):
                                src = QT if kc < 4 else GU
                                ins = e.matmul(pb[:rows, :], lhsT=src[:, kc % 4, t * 128:t * 128 + rows], rhs=wsrc[:, kc, :],
                                               start=(kc == 0), stop=(kc == 7))
                            return ins
                        P.op("pe", om, reads=[("wsl", wo0), "wx", ("GUt", t)], writes=[("ps", bank)])

                def cd_resid(t):
                    rows = 128 if t < 16 else NS
                    s = dslot[t]
                    for half in range(2):
                        bank = 2 + half
                        pb = psb[bank]
                        P.op("dve", lambda e, pb=pb, half=half: e.tensor_tensor(
                            out=x2t[s][:rows, half * 512:(half + 1) * 512], in0=pb[:rows, :], in1=W.xs[s][:rows, half * 512:(half + 1) * 512], op=ALU.add),
                            reads=[("ps", bank), ("xs", s)], writes=[("x2t", s, half)])
                    P.dma("sp", x2s[t * 128:t * 128 + rows, :], x2t[s][:rows, :], reads=[("x2t", s, 0), ("x2t", s, 1)], writes=[("x2s", t)], stream="x2o%d" % s)
                    norm_B(x2t[s], ("x2t", s, 1), rows, s)

                def cd_tr(t):
                    rows = 128 if t < 16 else NS
                    norm_C(rows, g2t, "g2t", hT, "hTt", t * 128, dslot[t], gran=128)

                cd_vg(0)
                cd_ln(0)
                cd_vg(1)
                cd_ln(1)
                cd_sgu(0)
                for i in range(18):
                    if i < 11:
                        P.dma_group("pool", [(WD[:, fc, :], w_down[fc * 128:(fc + 1) * 128, :]) for fc in (2 * i, 2 * i + 1)],
                                    writes=[("WD", i)], stream="wd")
                    if i + 2 < 17:
                        cd_vg(i + 2)
                    if i + 1 < 17:
                        cd_sgu(i + 1)
                    if i < 17:
                        cd_oproj(i)
                    if i + 2 < 17:
                        cd_ln(i + 2)
                    if i < 17:
                        cd_resid(i)
                    if 0 <= i - 1 < 17:
                        cd_tr(i - 1)
                s_ffn0 = load_wslab([(w_gate, 0, 256, 0), (w_up, 0, 256, 256)])
                P.barrier()

            if STAGE < 9:
                P.finish()
                return nc
            with ExitStack() as es6:
                sb6 = mk_sb(es6)
                HT = 1024 + NS
                AT = sb6("AT", [128, NFC, HT], BF)
                tmpf = [sb6("tmpf%d" % i, [128, 512]) for i in range(2)]
                yst = W.xs
                gfb = sb6("gfb", [128, D])
                P.dma("sp", gfb[:], gf_d, writes=["gfb"], stream="c0")
                issue_copies(99)
                ec = [0]
                for hf in range(2):
                    base = 1024 * hf
                    tgs = [(0, 512), (512, 512)] + ([(1024, NS)] if hf == 1 else [])
                    for fs in range(11):
                        if hf == 0 and fs == 0:
                            s = s_ffn0
                            P.lastw[("wsl", s)] = None
                        else:
                            s = load_wslab([(w_gate, fs * 256, 256, 0), (w_up, fs * 256, 256, 256)])
                        for fcl in range(2):
                            fc = 2 * fs + fcl
                            for (l0, n) in tgs:
                                i = ec[0] % 2
                                ec[0] += 1
                                ba, bb = psb[i], psb[2 + i]
                                t0 = base + l0

                                def gm(e, ba=ba, t0=t0, n=n, fcl=fcl, s=s):
                                    ins = None
                                    for kc in range(8):
                                        ins = e.matmul(ba[:, 0:n], lhsT=W.wsl[s][:, kc, fcl * 128:(fcl + 1) * 128], rhs=hT[:, kc, t0:t0 + n],
                                                       start=(kc == 0), stop=(kc == 7))
                                    return ins

                                def um(e, bb=bb, t0=t0, n=n, fcl=fcl, s=s):
                                    ins = None
                                    for kc in range(8):
                                        ins = e.matmul(bb[:, 0:n], lhsT=W.wsl[s][:, kc, 256 + fcl * 128:256 + (fcl + 1) * 128], rhs=hT[:, kc, t0:t0 + n],
                                                       start=(kc == 0), stop=(kc == 7))
                                    return ins
                                P.op("pe", gm, reads=[("wsl", s), ("hT", t0 // 512)], writes=[("ps", i)])
                                P.op("pe", um, reads=[("wsl", s), ("hT", t0 // 512)], writes=[("ps", 2 + i)])
                                P.op("act", lambda e, i=i, ba=ba, n=n: e.activation(out=tmpf[i][:, 0:n], in_=ba[:, 0:n], func=AF.Silu),
                                     reads=[("ps", i)], writes=[("tmpf", i)])
                                P.op("dve", lambda e, i=i, bb=bb, n=n, fc=fc, l0=l0: e.tensor_tensor(out=AT[:, fc, l0:l0 + n], in0=bb[:, 0:n], in1=tmpf[i][:, 0:n], op=ALU.mult),
                                     reads=[("ps", 2 + i), ("tmpf", i)], writes=[("AT", l0 // 512)])
                    tiles = list(range(8 * hf, 8 * hf + 8)) + ([16] if hf == 1 else [])
                    for t in tiles:
                        rows = 128 if t < 16 else NS
                        l0 = t * 128 - base
                        s = tcount[0] % 2
                        tcount[0] += 1
                        P.dma("sp", W.xs[s][:rows, :], x2s[t * 128:t * 128 + rows, :], reads=[("x2s", t)],
                              writes=[("xs", s), ("yst", s, 0), ("yst", s, 1)], stream="x%d" % s)
                        for half in range(2):
                            bank = 4 + half
                            pb = psb[bank]

                            def dm(e, pb=pb, half=half, l0=l0, rows=rows):
                                ins = None
                                for fc in range(NFC):
                                    ins = e.matmul(pb[:rows, :], lhsT=AT[:, fc, l0:l0 + rows], rhs=WD[:, fc, half * 512:(half + 1) * 512],
                                                   start=(fc == 0), stop=(fc == NFC - 1))
                                return ins
                            P.op("pe", dm, reads=["WD", ("AT", l0 // 512)], writes=[("ps", bank)])
                            P.op("dve", lambda e, pb=pb, half=half, s=s, rows=rows: e.tensor_tensor(
                                out=yst[s][:rows, half * 512:(half + 1) * 512], in0=pb[:rows, :], in1=W.xs[s][:rows, half * 512:(half + 1) * 512], op=ALU.add),
                                reads=[("ps", bank), ("xs", s)], writes=[("yst", s, half)])
                        rstd_of(yst[s], ("yst", s, 1), rows, s)
                        P.op("dve", lambda e, s=s, rows=rows: e.scalar_tensor_tensor(out=yst[s][:rows, :], in0=yst[s][:rows, :], scalar=ss[:rows, 4 + s:5 + s],
                                                                                   in1=gfb[:rows, :], op0=ALU.mult, op1=ALU.mult),
                             reads=[("yst", s, 0), ("yst", s, 1), ("rstd", s), "gfb"], writes=[("yst", s, 0), ("yst", s, 1)])
                        dst = y_d[t * 128:(t + 1) * 128, :] if t < 16 else ys_d
                        P.dma("sp", dst, yst[s][:rows, :], reads=[("yst", s, 0), ("yst", s, 1)], stream="yo%d" % s)

        P.finish()
        esp[0].close()
    return nc


_CACHE = {}


def _get_program():
    if "nc" not in _CACHE:
        _CACHE["nc"] = build_program()
    return _CACHE["nc"]


def kernel(x_prompt, x_sample, cache_k_win, cache_v_win, norm1_g, w_in, sgu_ln_g, sgu_ln_b, sgu_w, sgu_b,
           w_out, norm2_g, w_gate, w_up, w_down, rel_bias, final_g):
    f = lambda a: np.ascontiguousarray(np.asarray(a, dtype=np.float32))
    xp = f(x_prompt); xsa = f(x_sample)
    ck = f(cache_k_win)[0].reshape(32, 2048, 512); cv = f(cache_v_win)[0].reshape(32, 2048, 512)
    common = {
        "w_in": f(w_in)[0], "w_out": f(w_out)[0], "w_gate": f(w_gate)[0], "w_up": f(w_up)[0], "w_down": f(w_down)[0],
        "g1t": f(f(norm1_g)[0].reshape(8, 128).T), "g2t": f(f(norm2_g)[0].reshape(8, 128).T),
        "gfb": f(np.broadcast_to(f(final_g)[None, :], (128, D))),
        "lng": f(np.broadcast_to(f(sgu_ln_g)[0][None, :], (128, 512))),
        "lnb": f(np.broadcast_to(f(sgu_ln_b)[0][None, :], (128, 512))),
        "sgwT": f(f(sgu_w)[0].transpose(0, 2, 1)),
        "tri": f(np.triu(np.ones((128, 128), np.float32))),
        "bsp": f(np.repeat(f(sgu_b)[0].reshape(4, 2, 128), 64, axis=1)),
        "rb33": f(np.concatenate([f(rel_bias), np.full((1, 8), NEG, np.float32)], axis=0)),
        "oh": _onehot_consts(), "ident": np.eye(128, dtype=np.float32),
        "bdm": f(np.kron(np.eye(4, dtype=np.float32), np.triu(np.ones((4, 4), np.float32)))),
        "rb34": f(np.concatenate([f(rel_bias), np.full((1, 8), NEG, np.float32), np.ones((1, 8), np.float32)], axis=0)),
        "ohs": _sample_consts()[0], "ohn": _sample_consts()[1],
    }
    in_maps = []
    for c in range(NCORES):
        b, half = c // 2, c % 2
        own = xp[b, half * NT:(half + 1) * NT]
        hist = xp[b, 0:NT]
        m = dict(common)
        m.update({
            "xo": f(own), "xh": f(hist), "xsm": f(xsa[4 * c:4 * c + 4].reshape(NS, D)),
            "hm": np.full((128, 1), 0.0 if half == 1 else NEG, np.float32),
            "ck": f(ck[4 * c:4 * c + 4]), "cv": f(cv[4 * c:4 * c + 4]),
        })
        in_maps.append(m)
    if _CACHE.get("debug_core") is not None:
        c = _CACHE["debug_core"]
        return run_bass_kernel_spmd(_get_program(), [in_maps[c]], core_ids=[0]).results[0]
    nc = _get_program()
    res = run_bass_kernel_spmd(nc, in_maps, core_ids=list(range(NCORES)))
    R = res.results
    y = np.zeros((4, 4096, D), np.float32)
    kp = np.zeros((1, 4, 2048, 8, 64), np.float32); vp = np.zeros_like(kp)
    for c in range(NCORES):
        b, half = c // 2, c % 2
        y[b, half * NT:(half + 1) * NT] = R[c]["y"]
        if half == 1:
            kp[0, b] = R[c]["ko"].reshape(2048, 8, 64)
            vp[0, b] = R[c]["vo"].reshape(2048, 8, 64)
    ysm = np.concatenate([R[c]["ys"].reshape(4, 4, D) for c in range(NCORES)], axis=0)
    ks = np.concatenate([R[c]["kso"] for c in range(NCORES)], axis=0).reshape(1, 32, 2048, 8, 64)
    vs = np.concatenate([R[c]["vso"] for c in range(NCORES)], axis=0).reshape(1, 32, 2048, 8, 64)
    sv = np.concatenate([R[c]["svo"].reshape(4, 4, 512) for c in range(NCORES)], axis=0).reshape(1, 32, 4, 512)
    return (y, ysm, kp, vp, ks, vs, sv)
```
